# Optimizing a Trainium2 kernel written in Bass

```python
import math
import jax, jax.numpy as jnp
from jax import lax
import numpy as np

D_MODEL = 1024
BATCH = 32
SEQ = 256
DEPTH = 2
DEC_BATCH = 4
DEC_SEQ = 2048
PAST_LEN = 512

GRID_W = 64
D_MIX = 1024
DIFF_HEADS = 4
DIFF_HD = 32
DIFF_VD = 64
LRU_W = 512
LRU_BLOCKS = 8
LRU_BW = 64
CONV_W = 4
CONV_LEFT = 2
LRU_C = 8.0
MLA_HEADS = 4
Q_RANK = 192
KV_RANK = 128
NOPE_DIM = 64
ROPE_DIM = 32
MLA_VD = 64
QK_DIM = 96
D_FF = 2816
N_MOD = 9
ROPE_BASE = 10000.0
Q_BLOCK = 128
EPS = 1e-6
SPLITS = [256, 256, 256, 512, 512, 192, 128, 32]
D_IN = 2144

kernel_name = "hybrid_diff_lru_mla_prefix_dit_step"


def rmsnorm(x, g):
    xf = x.astype(jnp.float32)
    y = xf * lax.rsqrt(jnp.mean(xf * xf, axis=-1, keepdims=True) + EPS)
    return (y * g.astype(jnp.float32)).astype(x.dtype)


def swiglu(h, w_in, w_out):
    g, u = jnp.split(h @ w_in, 2, axis=-1)
    return (jax.nn.silu(g) * u) @ w_out


def axial_rope(T, dim):
    rows = T // GRID_W
    row = jnp.repeat(jnp.arange(rows), GRID_W).astype(jnp.float32)
    col = jnp.tile(jnp.arange(GRID_W), rows).astype(jnp.float32)
    n = dim // 4
    inv = ROPE_BASE ** (-jnp.arange(n, dtype=jnp.float32) / n)
    ang = jnp.concatenate([row[:, None] * inv, col[:, None] * inv], axis=-1)
    return jnp.cos(ang), jnp.sin(ang)


def apply_rope(x, cos, sin):
    half = x.shape[-1] // 2
    shape = (cos.shape[0],) + (1,) * (x.ndim - 3) + (half,)
    c = cos.reshape(shape)
    s = sin.reshape(shape)
    x1 = x[..., :half].astype(jnp.float32)
    x2 = x[..., half:].astype(jnp.float32)
    return jnp.concatenate([x1 * c - x2 * s, x2 * c + x1 * s], axis=-1).astype(x.dtype)


def over_query_blocks(fn, q):
    B, T = q.shape[:2]
    nb = T // Q_BLOCK
    qb = jnp.moveaxis(q.reshape((B, nb, Q_BLOCK) + q.shape[2:]), 1, 0)
    ob = jnp.moveaxis(lax.map(fn, qb), 0, 1)
    return ob.reshape((B, T) + ob.shape[3:])


def diff_attention(q, k, v, lam):
    scale = DIFF_HD ** -0.5
    kf = k.astype(jnp.float32)
    vf = v.astype(jnp.float32)

    def block(qb):
        s = jnp.einsum("bqhcd,bkhcd->bchqk", qb.astype(jnp.float32), kf) * scale
        p = jax.nn.softmax(s, axis=-1)
        a = p[:, 0] - lam * p[:, 1]
        return jnp.einsum("bhqk,bkhv->bqhv", a, vf)

    return over_query_blocks(block, q)


def softmax_attention(q, k, v):
    scale = q.shape[-1] ** -0.5
    kf = k.astype(jnp.float32)
    vf = v.astype(jnp.float32)

    def block(qb):
        s = jnp.einsum("bqhd,bkhd->bhqk", qb.astype(jnp.float32), kf) * scale
        p = jax.nn.softmax(s, axis=-1)
        return jnp.einsum("bhqk,bkhv->bqhv", p, vf)

    return over_query_blocks(block, q)


def dwconv(x, w, b):
    T = x.shape[1]
    xp = jnp.pad(x, ((0, 0), (CONV_LEFT, CONV_W - 1 - CONV_LEFT), (0, 0)))
    out = xp[:, 0:T] * w[0]
    for j in range(1, CONV_W):
        out = out + xp[:, j:j + T] * w[j]
    return out + b


def rglru(x, h0, w_gate, b_gate, lam, reverse):
    B, T, W = x.shape
    xb = x.reshape(B, T, LRU_BLOCKS, LRU_BW)
    g = jnp.einsum("btnc,gncd->gbtnd", xb, w_gate.astype(jnp.float32)).reshape(2, B, T, W)
    g = g + b_gate.astype(jnp.float32)[:, None, None, :]
    r = jax.nn.sigmoid(g[0])
    i = jax.nn.sigmoid(g[1])
    log_a = -LRU_C * r * jax.nn.softplus(-lam.astype(jnp.float32))
    a = jnp.exp(log_a)
    b = jnp.sqrt(-jnp.expm1(2.0 * log_a)) * (i * x)

    def combine(e1, e2):
        a1, b1 = e1
        a2, b2 = e2
        return a1 * a2, a2 * b1 + b2

    A, Bs = lax.associative_scan(combine, (a, b), axis=1, reverse=reverse)
    h = A * h0.astype(jnp.float32)[:, None, :] + Bs
    final = h[:, 0] if reverse else h[:, -1]
    return h, final


def mla_keys_values(c_lat, k_rope, w_ukv, g_k, rope):
    B, Tk = c_lat.shape[:2]
    kv = (c_lat @ w_ukv).reshape(B, Tk, MLA_HEADS, NOPE_DIM + MLA_VD)
    k_nope, v = kv[..., :NOPE_DIM], kv[..., NOPE_DIM:]
    k_r = jnp.broadcast_to(k_rope[:, :, None, :], (B, Tk, MLA_HEADS, ROPE_DIM)).astype(k_nope.dtype)
    k = rmsnorm(jnp.concatenate([k_nope, k_r], axis=-1), g_k)
    if rope is not None:
        k = jnp.concatenate([k[..., :NOPE_DIM], apply_rope(k[..., NOPE_DIM:], rope[0], rope[1])], axis=-1)
    return k, v


def mixer(h, p, lidx, ctx):
    B, T, _ = h.shape
    dt = h.dtype
    latent = ctx is not None
    offs = [int(o) for o in np.cumsum(SPLITS)[:-1]]
    qa, ka, va, xb, gb, cq, ckv, kr = jnp.split(h @ p["w_in"], offs, axis=-1)

    qa = rmsnorm(qa.reshape(B, T, DIFF_HEADS, 2, DIFF_HD), p["diff_qk_norm"][0])
    ka = rmsnorm(ka.reshape(B, T, DIFF_HEADS, 2, DIFF_HD), p["diff_qk_norm"][1])
    va = va.reshape(B, T, DIFF_HEADS, DIFF_VD)
    lam_init = 0.8 - 0.6 * math.exp(-0.3 * lidx)
    lp = p["diff_lambda"].astype(jnp.float32)
    lam = jnp.exp(jnp.sum(lp[0] * lp[1])) - jnp.exp(jnp.sum(lp[2] * lp[3])) + lam_init
    if latent:
        cos_d, sin_d = axial_rope(T, DIFF_HD)
        q_rot = apply_rope(qa, cos_d, sin_d)
        k_all = jnp.concatenate([ctx["diff_k"].astype(dt), apply_rope(ka, cos_d, sin_d)], axis=1)
        v_all = jnp.concatenate([ctx["diff_v"].astype(dt), va], axis=1)
        o_a = diff_attention(q_rot, k_all, v_all, lam)
    else:
        o_a = diff_attention(qa, ka, va, lam)
    o_a = (rmsnorm(o_a, p["diff_subln"]) * (1.0 - lam_init)).astype(dt).reshape(B, T, DIFF_HEADS * DIFF_VD)

    xc = dwconv(xb, p["lru_conv_w"], p["lru_conv_b"]).astype(jnp.float32)
    h0 = ctx["lru"] if latent else jnp.zeros((B, 2, LRU_W), jnp.float32)
    hf, sf = rglru(xc, h0[:, 0], p["lru_w_gate"][0], p["lru_b_gate"][0], p["lru_lambda"][0], False)
    hb, sb = rglru(xc, h0[:, 1], p["lru_w_gate"][1], p["lru_b_gate"][1], p["lru_lambda"][1], True)
    o_b = ((hf + hb) * jax.nn.gelu(gb.astype(jnp.float32))).astype(dt)

    cq = rmsnorm(cq, p["mla_cq_norm"])
    qc = rmsnorm((cq @ p["mla_w_uq"]).reshape(B, T, MLA_HEADS, QK_DIM), p["mla_qk_norm"][0])
    ckv_n = rmsnorm(ckv, p["mla_ckv_norm"])
    if latent:
        rope_m = axial_rope(T, ROPE_DIM)
        qc = jnp.concatenate([qc[..., :NOPE_DIM], apply_rope(qc[..., NOPE_DIM:], rope_m[0], rope_m[1])], axis=-1)
        k_ctx, v_ctx = mla_keys_values(ctx["ckv"].astype(dt), ctx["krope"].astype(dt), p["mla_w_ukv"], p["mla_qk_norm"][1], None)
        k_lat, v_lat = mla_keys_values(ckv_n, kr, p["mla_w_ukv"], p["mla_qk_norm"][1], rope_m)
        o_c = softmax_attention(qc, jnp.concatenate([k_ctx, k_lat], axis=1), jnp.concatenate([v_ctx, v_lat], axis=1))
    else:
        k_c, v_c = mla_keys_values(ckv_n, kr, p["mla_w_ukv"], p["mla_qk_norm"][1], None)
        o_c = softmax_attention(qc, k_c, v_c)
    o_c = o_c.astype(dt).reshape(B, T, MLA_HEADS * MLA_VD)

    out = jnp.concatenate([o_a, o_b, o_c], axis=-1) @ p["w_out"]
    if latent:
        return out, None
    return out, (ka, va, ckv_n, kr, jnp.stack([sf, sb], axis=1))


def trunk_layer(x, cond, p, lidx, ctx):
    mod = jax.nn.silu(cond.astype(jnp.float32)) @ p["w_ada"] + p["b_ada"]
    mod = mod.reshape(cond.shape[0], N_MOD, 1, D_MODEL).astype(x.dtype)
    g = p["norm_g"]
    h = rmsnorm(x, g[0]) * (1.0 + mod[:, 1]) + mod[:, 0]
    x = x + 0.5 * mod[:, 2] * swiglu(h, p["w_ffn_in"][0], p["w_ffn_out"][0])
    h = rmsnorm(x, g[1]) * (1.0 + mod[:, 4]) + mod[:, 3]
    m, new_ctx = mixer(h, p, lidx, ctx)
    x = x + mod[:, 5] * m
    h = rmsnorm(x, g[2]) * (1.0 + mod[:, 7]) + mod[:, 6]
    x = x + 0.5 * mod[:, 8] * swiglu(h, p["w_ffn_in"][1], p["w_ffn_out"][1])
    return x, new_ctx


def setup_inputs(seed: int = 0) -> dict:
    key = jax.random.key(seed)
    ks = jax.random.split(key, 32)

    def nrm(k, shape, scale):
        return jax.random.normal(k, shape, jnp.float32) * scale

    def gain(k, shape):
        return 1.0 + 0.05 * jax.random.normal(k, shape, jnp.float32)

    return {
        "x_prompt": nrm(ks[0], (BATCH, SEQ, D_MODEL), 1.0),
        "x_sample": nrm(ks[1], (DEC_BATCH, DEC_SEQ, D_MODEL), 1.0),
        "cache_diff_k": nrm(ks[2], (DEC_BATCH, DEPTH, PAST_LEN, DIFF_HEADS, 2, DIFF_HD), 1.0),
        "cache_diff_v": nrm(ks[3], (DEC_BATCH, DEPTH, PAST_LEN, DIFF_HEADS, DIFF_VD), 1.0),
        "cache_mla_ckv": nrm(ks[4], (DEC_BATCH, DEPTH, PAST_LEN, KV_RANK), 1.0),
        "cache_mla_krope": nrm(ks[5], (DEC_BATCH, DEPTH, PAST_LEN, ROPE_DIM), 1.0),
        "state_lru": nrm(ks[6], (DEC_BATCH, DEPTH, 2, LRU_W), 0.5),
        "c": nrm(ks[7], (DEC_BATCH, D_MODEL), 1.0),
        "c_ctx": nrm(ks[8], (D_MODEL,), 1.0),
        "norm_g": gain(ks[9], (DEPTH, 3, D_MODEL)),
        "w_ada": nrm(ks[10], (DEPTH, D_MODEL, N_MOD * D_MODEL), 0.5 * D_MODEL ** -0.5),
        "b_ada": nrm(ks[11], (DEPTH, N_MOD * D_MODEL), 0.02),
        "w_ffn_in": nrm(ks[12], (DEPTH, 2, D_MODEL, 2 * D_FF), D_MODEL ** -0.5),
        "w_ffn_out": nrm(ks[13], (DEPTH, 2, D_FF, D_MODEL), D_FF ** -0.5),
        "w_in": nrm(ks[14], (DEPTH, D_MODEL, D_IN), D_MODEL ** -0.5),
        "w_out": nrm(ks[15], (DEPTH, D_MIX, D_MODEL), D_MIX ** -0.5),
        "diff_qk_norm": gain(ks[16], (DEPTH, 2, DIFF_HD)),
        "diff_lambda": nrm(ks[17], (DEPTH, 4, DIFF_HD), 0.1),
        "diff_subln": gain(ks[18], (DEPTH, DIFF_VD)),
        "lru_conv_w": nrm(ks[19], (DEPTH, CONV_W, LRU_W), CONV_W ** -0.5),
        "lru_conv_b": nrm(ks[20], (DEPTH, LRU_W), 0.02),
        "lru_w_gate": nrm(ks[21], (DEPTH, 2, 2, LRU_BLOCKS, LRU_BW, LRU_BW), LRU_BW ** -0.5),
        "lru_b_gate": nrm(ks[22], (DEPTH, 2, 2, LRU_W), 0.02),
        "lru_lambda": jax.random.uniform(ks[23], (DEPTH, 2, LRU_W), jnp.float32, 2.0, 6.0),
        "mla_cq_norm": gain(ks[24], (DEPTH, Q_RANK)),
        "mla_ckv_norm": gain(ks[25], (DEPTH, KV_RANK)),
        "mla_w_uq": nrm(ks[26], (DEPTH, Q_RANK, MLA_HEADS * QK_DIM), Q_RANK ** -0.5),
        "mla_w_ukv": nrm(ks[27], (DEPTH, KV_RANK, MLA_HEADS * (NOPE_DIM + MLA_VD)), KV_RANK ** -0.5),
        "mla_qk_norm": gain(ks[28], (DEPTH, 2, QK_DIM)),
    }


def reference(x_prompt, x_sample, cache_diff_k, cache_diff_v, cache_mla_ckv, cache_mla_krope, state_lru, c, c_ctx,
              norm_g, w_ada, b_ada, w_ffn_in, w_ffn_out, w_in, w_out, diff_qk_norm, diff_lambda, diff_subln,
              lru_conv_w, lru_conv_b, lru_w_gate, lru_b_gate, lru_lambda, mla_cq_norm, mla_ckv_norm,
              mla_w_uq, mla_w_ukv, mla_qk_norm):
    xp = x_prompt
    xs = x_sample
    ks_, vs_, ckvs_, krs_, lrus_ = [], [], [], [], []
    for l in range(DEPTH):
        p = {
            "norm_g": norm_g[l], "w_ada": w_ada[l], "b_ada": b_ada[l],
            "w_ffn_in": w_ffn_in[l], "w_ffn_out": w_ffn_out[l], "w_in": w_in[l], "w_out": w_out[l],
            "diff_qk_norm": diff_qk_norm[l], "diff_lambda": diff_lambda[l], "diff_subln": diff_subln[l],
            "lru_conv_w": lru_conv_w[l], "lru_conv_b": lru_conv_b[l], "lru_w_gate": lru_w_gate[l],
            "lru_b_gate": lru_b_gate[l], "lru_lambda": lru_lambda[l],
            "mla_cq_norm": mla_cq_norm[l], "mla_ckv_norm": mla_ckv_norm[l], "mla_w_uq": mla_w_uq[l],
            "mla_w_ukv": mla_w_ukv[l], "mla_qk_norm": mla_qk_norm[l],
        }
        xp, (k_l, v_l, ckv_l, kr_l, lru_l) = trunk_layer(xp, c_ctx[None, :], p, l, None)
        ks_.append(k_l)
        vs_.append(v_l)
        ckvs_.append(ckv_l)
        krs_.append(kr_l)
        lrus_.append(lru_l)
        ctx = {"diff_k": cache_diff_k[:, l], "diff_v": cache_diff_v[:, l], "ckv": cache_mla_ckv[:, l],
               "krope": cache_mla_krope[:, l], "lru": state_lru[:, l]}
        xs, _ = trunk_layer(xs, c, p, l, ctx)
    new_diff_k = jnp.stack(ks_, axis=1)
    new_diff_v = jnp.stack(vs_, axis=1)
    new_mla_ckv = jnp.stack(ckvs_, axis=1)
    new_mla_krope = jnp.stack(krs_, axis=1)
    new_state_lru = jnp.stack(lrus_, axis=1)
    return (xp, xs, new_diff_k, new_diff_v, new_mla_ckv, new_mla_krope, new_state_lru)
```

```python
import math
import numpy as np
import ml_dtypes
import concourse.bass as bass
import concourse.mybir as mybir
from concourse.bass_utils import run_bass_kernel_spmd

F32 = mybir.dt.float32
F32R = mybir.dt.float32r
BF16 = mybir.dt.bfloat16
AF = mybir.ActivationFunctionType
ALU = mybir.AluOpType
AX = mybir.AxisListType

D = 1024
NT = 2048
NG = 4
NC_ = 512
NK = NT + NC_
NKB = NK // 128
DFF = 2816
NFF = DFF // 128
DEPTH = 2
EPS = 1e-6
BIG = 2048.0
NV = 276


class StopMixer(Exception):
    pass


class Buf:
    __slots__ = ("name", "w", "r", "x")

    def __init__(self, name, x=False):
        self.name = name
        self.w = None
        self.r = []
        self.x = x


class Sync:
    def __init__(self, nc, es):
        self.nc = nc
        self.eng = {"pe": nc.tensor, "act": nc.scalar, "dve": nc.vector, "pool": nc.gpsimd, "sp": nc.sync}
        self.sem = {k: es.enter_context(nc.semaphore("s_" + k)) for k in ("pe", "act", "dve", "pool")}
        self.cnt = {k: 0 for k in self.sem}
        self.pend = {k: False for k in self.sem}
        self.waited = {k: {} for k in self.eng}
        self.nslot = 12
        self.dq = {}
        for q in ("sp", "act", "pool"):
            self.dq[q] = {"sems": [es.enter_context(nc.semaphore("d_%s%d" % (q, i))) for i in range(self.nslot)],
                          "val": [0] * self.nslot, "i": 0}
        self.allsems = {}

    def _wait(self, e, ev):
        if ev is None:
            return
        sem, val = ev
        key = id(sem)
        if self.waited[e].get(key, 0) >= val:
            return
        self.waited[e][key] = val
        self.allsems[key] = sem
        self.eng[e].wait_ge(sem, val)

    def _deps(self, e, reads, writes, acc):
        for b in reads:
            self._wait(e, b.w)
            if b.x:
                for ev in b.r:
                    self._wait(e, ev)
        for b in writes:
            if not (acc and e == "pe"):
                self._wait(e, b.w)
            for ev in b.r:
                self._wait(e, ev)

    def _mark(self, ev, reads, writes):
        for b in writes:
            b.w = ev
            b.r = []
        for b in reads:
            b.r.append(ev)
            if len(b.r) > 24:
                b.r = b.r[-24:]

    mute = False

    def op(self, e, fn, reads=(), writes=(), acc=False, inc=True):
        if self.mute:
            return None
        self._deps(e, reads, writes, acc)
        ins = fn(self.eng[e])
        if inc:
            self.cnt[e] += 1
            ins.then_inc(self.sem[e], 1)
            self.pend[e] = False
            ev = (self.sem[e], self.cnt[e])
        else:
            self.pend[e] = True
            ev = (self.sem[e], self.cnt[e] + 1)
        self._mark(ev, reads, writes)
        return ev

    def dma(self, q, out, in_, reads=(), writes=()):
        if self.mute:
            return None
        if q in ("act", "pool"):
            q = "sp"
        dq = self.dq[q]
        i = dq["i"]
        dq["i"] = (i + 1) % self.nslot
        sem = dq["sems"][i]
        if dq["val"][i]:
            self._wait(q, (sem, dq["val"][i]))
        self._deps(q, reads, writes, False)
        dq["val"][i] += 16
        self.eng[q].dma_start(out=out, in_=in_).then_inc(sem, 16)
        ev = (sem, dq["val"][i])
        self._mark(ev, reads, writes)
        return ev

    def barrier(self):
        if self.mute:
            return
        evs = []
        for k in self.sem:
            assert not self.pend[k]
            if self.cnt[k]:
                evs.append((self.sem[k], self.cnt[k]))
        for q in self.dq.values():
            for s, v in zip(q["sems"], q["val"]):
                if v:
                    evs.append((s, v))
        for e in self.eng:
            for ev in evs:
                self._wait(e, ev)


def build_program():
    nc = bass.Bass("TRN2", target_bir_lowering=False)
    nc.dge_precook = False
    from contextlib import ExitStack

    def din(name, shape, dt=F32):
        return nc.dram_tensor(name, list(shape), dt, kind="ExternalInput").ap()

    def dout(name, shape, dt=F32):
        return nc.dram_tensor(name, list(shape), dt, kind="ExternalOutput").ap()

    x_d = din("x", [NT, D])
    cond_d = din("cond", [128, 8])
    ck_d = din("ck", [DEPTH, NC_, 256])
    cv_d = din("cv", [DEPTH, NC_, 256])
    cckv_d = din("cckv", [DEPTH, NC_, 128])
    ckr_d = din("ckr", [DEPTH, NC_, 32])
    lru0_d = din("lru0", [DEPTH, 128, 8])
    keep_d = din("keepv", [128, 2])
    cos_d = din("cosT", [128, NT])
    sin_d = din("sinT", [128, NT])
    mq_d = din("maskq", [8, NT], BF16)
    mk_d = din("maskk", [8, NK], BF16)
    cR_d = din("constsR", [128, 5 * 128], F32R)
    id_d = din("ident", [128, 128])
    vecs_d = din("vecs", [DEPTH, 128, NV])
    w_ada_d = din("w_ada", [DEPTH, D, 9 * D], F32R)
    w_fi_d = din("w_ffn_in", [DEPTH, 2, D, 2 * DFF], F32R)
    w_fo_d = din("w_ffn_out", [DEPTH, 2, DFF, D], F32R)
    w_in_d = din("w_in", [DEPTH, D, 2144], F32R)
    w_out_d = din("w_out", [DEPTH, D, D], F32R)
    w_gate_d = din("lru_w_gate", [DEPTH, 2, 2, 8, 64, 64], F32R)
    w_uq_d = din("mla_w_uq", [DEPTH, 192, 384], F32R)
    w_ukv_d = din("mla_w_ukv", [DEPTH, 128, 512], F32R)

    y_d = dout("y", [NT, D])
    ok_d = dout("o_k", [DEPTH, NT, 256])
    ov_d = dout("o_v", [DEPTH, NT, 256])
    ockv_d = dout("o_ckv", [DEPTH, NT, 128])
    okr_d = dout("o_kr", [DEPTH, NT, 32])
    ost_d = dout("o_st", [128, 128])

    with ExitStack() as es:
        S = Sync(nc, es)

        uid = [0]

        def sb(stack, name, shape, dt=F32):
            uid[0] += 1
            return stack.enter_context(nc.sbuf_tensor("sb%d_%s" % (uid[0], name), list(shape), dt))

        xT = sb(es, "xT", [128, 8, NT])
        xB = [[Buf("x%d_%d" % (c, g)) for g in range(NG)] for c in range(8)]
        cR = sb(es, "cR", [128, 5 * 128], F32R)
        ident = sb(es, "ident", [128, 128])
        vecs = sb(es, "vecs", [128, DEPTH, NV])
        modv = sb(es, "modv", [128, DEPTH, 72])
        modA = sb(es, "modA", [128, DEPTH, 24])
        modG = sb(es, "modG", [128, DEPTH, 24])
        keepv = sb(es, "keepv", [128, 2])
        lru0 = sb(es, "lru0", [128, DEPTH, 8])
        stT = sb(es, "stT", [128, 128])
        mhalf = sb(es, "mhalf", [128, 512])
        epsc = sb(es, "epsc", [128, 1])
        small = sb(es, "small", [128, 64])
        cB = Buf("consts")
        stB = Buf("stT")
        smB = Buf("small")
        PS = [es.enter_context(nc.psum_tensor("ps%d" % i, [128, 512], F32)) for i in range(8)]
        PSB = [Buf("ps%d" % i, x=True) for i in range(8)]
        psi = [0]

        def ps():
            i = psi[0]
            psi[0] = (i + 1) % 6
            return PS[i], PSB[i]

        ones = cR[:, 0:128]
        bd32 = cR[:, 128:256]
        sel65 = cR[:, 256:384]
        R64 = cR[:, 384:512]
        R96 = cR[:, 512:640]

        S.dma("sp", cR[:], cR_d, writes=[cB])
        S.dma("sp", ident[:], id_d, writes=[cB])
        S.dma("sp", vecs[:], vecs_d.rearrange("l p n -> p l n"), writes=[cB])
        S.dma("sp", keepv[:], keep_d, writes=[cB])
        S.dma("sp", lru0[:], lru0_d.rearrange("l p n -> p l n"), writes=[cB])
        S.op("pool", lambda e: e.memset(mhalf[:], 0.0), writes=[cB])
        S.op("pool", lambda e: e.memset(epsc[:], EPS), writes=[cB])
        S.op("pool", lambda e: e.memset(stT[:], 0.0), writes=[stB])

        def rstd_from_ps(pst, psb, rows, n, dst, dstB, tmp, tmpB, cols=512):
            S.op("act", lambda e: e.activation(out=tmp[0:rows, 0:cols], in_=pst[0:rows, 0:cols], func=AF.Ln, scale=1.0 / n, bias=epsc[0:rows, 0:1]),
                 reads=[psb, cB], writes=[tmpB])
            S.op("act", lambda e: e.activation(out=dst[0:rows, 0:cols], in_=tmp[0:rows, 0:cols], func=AF.Exp, scale=-0.5),
                 reads=[tmpB], writes=[dstB])

        def mm(out, lhsT, rhs, start, stop, reads, writes, lazy=False):
            return S.op("pe", lambda e: e.matmul(out, lhsT, rhs, start=start, stop=stop), reads=reads, writes=writes,
                        acc=not start, inc=(stop if lazy else True))


        def attn_pipe(items, s_mm, pv_mm, PT, PTB, ptc, sc, LA=2):
            q = []
            n = len(items)
            for i in range(n + LA):
                if i < n:
                    pS, pSb = ps()
                    s_mm(items[i], pS, pSb)
                    q.append((pS, pSb))
                j = i - LA
                if j >= 0:
                    pS, pSb = q[j]
                    k_ = ptc[0] % len(PT)
                    ptc[0] += 1
                    P_, PB_ = PT[k_], PTB[k_]
                    S.op("act", lambda e, pS=pS, P_=P_: e.activation(out=P_[:], in_=pS[:], func=AF.Exp, scale=sc), reads=[pSb], writes=[PB_])
                    pv_mm(items[j], P_, PB_)

        with ExitStack() as ph:
            stg = sb(ph, "xstg", [128, 4, D])
            stgB = Buf("xstg")
            for g in range(NG):
                S.dma("sp", stg[:], x_d[g * 512:(g + 1) * 512, :].rearrange("(t p) n -> p t n", p=128), writes=[stgB])
                for c in range(8):
                    pt, pb = ps()
                    for t in range(4):
                        S.op("pe", lambda e, t=t, c=c, pt=pt: e.transpose(pt[:, t * 128:(t + 1) * 128], stg[:, t, c * 128:(c + 1) * 128], ident[:]),
                             reads=[stgB, cB], writes=[pb], acc=(t > 0), inc=(t == 3))
                    S.op("dve" if c % 2 else "act",
                         (lambda e, c=c, pt=pt, g=g: e.tensor_copy(out=xT[:, c, g * 512:(g + 1) * 512], in_=pt[:]))
                         if c % 2 else
                         (lambda e, c=c, pt=pt, g=g: e.copy(out=xT[:, c, g * 512:(g + 1) * 512], in_=pt[:])),
                         reads=[pb], writes=[xB[c][g]])

            cnd = sb(ph, "cnd", [128, 8])
            s2 = sb(ph, "s2", [128, 8, 2], F32R)
            cndB = Buf("cnd")
            S.dma("sp", cnd[:], cond_d, writes=[cndB])
            for j in range(2):
                S.op("act", lambda e, j=j: e.activation(out=s2[:, :, j], in_=cnd[:], func=AF.Silu), reads=[cndB], writes=[smB])
            wad = [sb(ph, "wad%d" % i, [128, 8, 512], F32R) for i in range(2)]
            wadB = [Buf("wad%d" % i) for i in range(2)]
            for l in range(DEPTH):
                pm, pmb = ps()
                wv = w_ada_d[l].rearrange("(k p) n -> p k n", p=128)
                for cg in range(18):
                    wt, wb = wad[cg % 2], wadB[cg % 2]
                    S.dma("sp" if cg % 2 else "act", wt[:], wv[:, :, cg * 512:(cg + 1) * 512], writes=[wb])
                    for jj in range(4):
                        j = cg * 4 + jj
                        for k in range(8):
                            mm(pm[:, 2 * j:2 * j + 2], wt[:, k, jj * 128:(jj + 1) * 128], s2[:, k, :], k == 0, k == 7,
                               [wb, smB], [pmb], lazy=True)
                pmv = pm[:, 0:144].rearrange("p (j t) -> p j t", t=2)
                S.op("dve", lambda e, l=l, pmv=pmv: e.tensor_tensor(out=modv[:, l, :], in0=pmv[:, :, 0], in1=vecs[:, l, 24:96], op=ALU.add),
                     reads=[pmb, cB], writes=[cB])
                for s in range(3):
                    S.op("dve", lambda e, l=l, s=s: e.scalar_tensor_tensor(out=modA[:, l, s * 8:(s + 1) * 8], in0=modv[:, l, (3 * s + 1) * 8:(3 * s + 2) * 8],
                                                                          scalar=1.0, in1=vecs[:, l, s * 8:(s + 1) * 8], op0=ALU.add, op1=ALU.mult),
                         reads=[cB], writes=[cB])
                    S.op("dve", lambda e, l=l, s=s: e.tensor_scalar(out=modG[:, l, s * 8:(s + 1) * 8], in0=modv[:, l, (3 * s + 2) * 8:(3 * s + 3) * 8],
                                                                   scalar1=(1.0 if s == 1 else 0.5), scalar2=None, op0=ALU.mult),
                         reads=[cB], writes=[cB])
            S.barrier()

        def make_h(l, s, g, h_ap, hB, sq, sqB, rs, rsB, tmp, tmpB, tmp2=None):
            pt, pb = ps()
            for c in range(8):
                S.op("act" if c % 2 else "dve",
                     (lambda e, c=c: e.activation(out=sq[:, c % 2, :], in_=xT[:, c, g * 512:(g + 1) * 512], func=AF.Square)) if c % 2 else
                     (lambda e, c=c: e.tensor_tensor(out=sq[:, c % 2, :], in0=xT[:, c, g * 512:(g + 1) * 512], in1=xT[:, c, g * 512:(g + 1) * 512], op=ALU.mult)),
                     reads=[xB[c][g]], writes=[sqB[c % 2]])
                mm(pt[:], ones, sq[:, c % 2, :], c == 0, c == 7, [sqB[c % 2], cB], [pb])
            rstd_from_ps(pt, pb, 128, D, rs, rsB, tmp, tmpB)
            tps = [(tmp, tmpB)] + ([tmp2] if tmp2 is not None else [])
            for c in range(8):
                tq, tqB = tps[c % len(tps)]
                S.op("dve", lambda e, c=c, tq=tq: e.tensor_tensor(out=tq[:, :], in0=xT[:, c, g * 512:(g + 1) * 512], in1=rs[:, :], op=ALU.mult),
                     reads=[xB[c][g], rsB], writes=[tqB])
                S.op("act", lambda e, c=c, tq=tq: e.activation(out=h_ap(c), in_=tq[:, :], func=AF.Identity,
                                                               scale=modA[:, l, s * 8 + c:s * 8 + c + 1], bias=modv[:, l, 3 * s * 8 + c:3 * s * 8 + c + 1]),
                     reads=[tqB, cB], writes=[hB])

        def x_update(l, s, o, g, pt, pb):
            S.op("dve", lambda e: e.scalar_tensor_tensor(out=xT[:, o, g * 512:(g + 1) * 512], in0=pt[:], scalar=modG[:, l, s * 8 + o:s * 8 + o + 1],
                                                         in1=xT[:, o, g * 512:(g + 1) * 512], op0=ALU.mult, op1=ALU.add),
                 reads=[pb, cB], writes=[xB[o][g]])

        def ffn(l, s, fi):
            with ExitStack() as ph:
                h = sb(ph, "f_h", [128, 8, 1024], F32R)
                hB = [Buf("f_h0"), Buf("f_h1")]
                sq = sb(ph, "f_sq", [128, 2, 512], F32R)
                sqB = [Buf("f_sq0"), Buf("f_sq1")]
                rs = sb(ph, "f_rs", [128, 512])
                rsB = Buf("f_rs")
                tmp = sb(ph, "f_tmp", [128, 512])
                tmpB = Buf("f_tmp")
                tmpx = sb(ph, "f_tmpx", [128, 512])
                tmpxB = Buf("f_tmpx")
                NW = 3
                wg = [sb(ph, "f_wg%d" % i, [128, 8, 256], F32R) for i in range(NW)]
                wgB = [Buf("f_wg%d" % i) for i in range(NW)]
                FB = 3
                act = [sb(ph, "f_act%d" % i, [128, FB, 1024], F32R) for i in range(2)]
                actB = [[Buf("f_act%d_%d" % (i, f)) for f in range(FB)] for i in range(2)]
                wo = [sb(ph, "f_wo%d" % i, [128, FB, D], F32R) for i in range(2)]
                woB = [Buf("f_wo%d" % i) for i in range(2)]
                sg = [sb(ph, "f_sg%d" % i, [128, 512]) for i in range(2)]
                sgB = [Buf("f_sg%d" % i) for i in range(2)]
                wiv = w_fi_d[l, fi].rearrange("(k p) n -> p k n", p=128)
                wov = w_fo_d[l, fi].rearrange("(f p) n -> p f n", p=128)
                wcnt = 0
                sgc = 0
                wbc = 0
                for tg in range(2):
                    for hh in range(2):
                        g = tg * 2 + hh
                        make_h(l, s, g, lambda c, hh=hh: h[:, c, hh * 512:(hh + 1) * 512], hB[hh], sq, sqB, rs, rsB, tmp, tmpB, tmp2=(tmpx, tmpxB))
                    blocks = [(b0, min(FB, NFF - b0)) for b0 in range(0, NFF, FB)]
                    for bi, (b0, nb) in enumerate(blocks):
                        ab = bi % 2
                        wb_ = wbc % 2
                        wbc += 1
                        S.dma("pool", wo[wb_][:, 0:nb, :], wov[:, b0:b0 + nb, :], writes=[woB[wb_]])
                        for f in range(nb):
                            wt, wb = wg[wcnt % NW], wgB[wcnt % NW]
                            q = "sp" if wcnt % 2 else "act"
                            wcnt += 1
                            S.dma(q, wt[:, :, 0:128], wiv[:, :, (b0 + f) * 128:(b0 + f + 1) * 128], writes=[wb])
                            S.dma(q, wt[:, :, 128:256], wiv[:, :, DFF + (b0 + f) * 128:DFF + (b0 + f + 1) * 128], writes=[wb])
                            for hh in range(2):
                                pg, pgb = ps()
                                pu, pub = ps()
                                for k in range(8):
                                    mm(pg[:], wt[:, k, 0:128], h[:, k, hh * 512:(hh + 1) * 512], k == 0, k == 7, [wb, hB[hh]], [pgb], lazy=True)
                                for k in range(8):
                                    mm(pu[:], wt[:, k, 128:256], h[:, k, hh * 512:(hh + 1) * 512], k == 0, k == 7, [wb, hB[hh]], [pub], lazy=True)
                                st, stb = sg[sgc % 2], sgB[sgc % 2]
                                sgc += 1
                                S.op("act", lambda e, st=st, pg=pg: e.activation(out=st[:], in_=pg[:], func=AF.Silu), reads=[pgb], writes=[stb])
                                S.op("dve", lambda e, st=st, pu=pu, ab=ab, f=f, hh=hh: e.tensor_tensor(out=act[ab][:, f, hh * 512:(hh + 1) * 512], in0=st[:], in1=pu[:], op=ALU.mult),
                                     reads=[stb, pub], writes=[actB[ab][f]])
                        for o in range(8):
                            for hh in range(2):
                                po, pob = ps()
                                for f in range(nb):
                                    mm(po[:], wo[wb_][:, f, o * 128:(o + 1) * 128], act[ab][:, f, hh * 512:(hh + 1) * 512], f == 0, f == nb - 1,
                                       [woB[wb_], actB[ab][f]], [pob], lazy=True)
                                x_update(l, s, o, tg * 2 + hh, po, pob)
                S.barrier()

        def mixer(l):
            import os
            mixlim = int(os.environ.get("MK_MIX", "99"))
            alim = int(os.environ.get("MK_A", "99"))
            plim = int(os.environ.get("MK_P", "1000000000"))
            pcount = [0]

            def chk():
                pcount[0] += 1
                if pcount[0] == plim:
                    S.barrier()
                    S.mute = True
            with ExitStack() as ph:
                tabB = Buf("tab")
                hall = sb(ph, "m_hall", [128, 8, NT], BF16)
                hallB = [Buf("m_hall%d" % g) for g in range(NG)]
                hb_ = [None]
                hbB = [Buf("m_h0")]
                sq = sb(ph, "m_sq", [128, 2, 512], F32R)
                sqB = [Buf("m_sq0"), Buf("m_sq1")]
                tmp = sb(ph, "m_tmp", [128, 512])
                tmpB = Buf("m_tmp")
                tm2 = sb(ph, "m_tm2", [128, 512])
                tm2B = Buf("m_tm2")
                wp = [sb(ph, "m_wp%d" % i, [128, 8, 128], F32R) for i in range(2)]
                wpB = [Buf("m_wp%d" % i) for i in range(2)]
                wo = sb(ph, "m_wo", [128, D], F32R)
                woB = Buf("m_wo")
                lv = sb(ph, "m_lv", [128, 16])
                lvB = Buf("m_lv")
                wiv = w_in_d[l].rearrange("(k p) n -> p k n", p=128)

                for g in range(NG):
                    pt, pb = ps()
                    for c in range(8):
                        S.op("act" if c % 2 else "dve",
                             (lambda e, c=c, g=g: e.activation(out=sq[:, c % 2, :], in_=xT[:, c, g * 512:(g + 1) * 512], func=AF.Square)) if c % 2 else
                             (lambda e, c=c, g=g: e.tensor_tensor(out=sq[:, c % 2, :], in0=xT[:, c, g * 512:(g + 1) * 512], in1=xT[:, c, g * 512:(g + 1) * 512], op=ALU.mult)),
                             reads=[xB[c][g]], writes=[sqB[c % 2]])
                        mm(pt[:], ones, sq[:, c % 2, :], c == 0, c == 7, [sqB[c % 2], cB], [pb])
                    rstd_from_ps(pt, pb, 128, D, tm2, tm2B, tmp, tmpB)
                    for c in range(8):
                        S.op("dve", lambda e, c=c, g=g: e.tensor_tensor(out=tmp[:, :], in0=xT[:, c, g * 512:(g + 1) * 512], in1=tm2[:, :], op=ALU.mult),
                             reads=[xB[c][g], tm2B], writes=[tmpB])
                        S.op("act", lambda e, c=c, g=g: e.activation(out=hall[:, c, g * 512:(g + 1) * 512], in_=tmp[:, :], func=AF.Identity,
                                                                     scale=modA[:, l, 8 + c:8 + c + 1], bias=modv[:, l, 24 + c:24 + c + 1]),
                             reads=[tmpB, cB], writes=[hallB[g]])

                hcB = [Buf("m_hc%d" % c) for c in range(8)]

                def get_h(g):
                    ht = hb_[0]
                    for c in range(8):
                        if c % 2:
                            S.op("act", lambda e, c=c: e.copy(out=ht[:, c, :], in_=hall[:, c, g * 512:(g + 1) * 512]), reads=[hallB[g]], writes=[hcB[c]])
                        else:
                            S.op("dve", lambda e, c=c: e.tensor_copy(out=ht[:, c, :], in_=hall[:, c, g * 512:(g + 1) * 512]), reads=[hallB[g]], writes=[hcB[c]])
                    return ht, hcB

                def load_wp(i, col0, ncol, q="sp"):
                    S.dma(q, wp[i][:, :, 0:ncol], wiv[:, :, col0:col0 + ncol], writes=[wpB[i]])

                def proj(i, ncol, ht, hB):
                    pt, pb = ps()
                    for k in range(8):
                        mm(pt[0:ncol, :], wp[i][:, k, 0:ncol], ht[:, k, :], k == 0, k == 7, [wpB[i], hB[k]], [pb], lazy=True)
                    return pt, pb

                def out_proj(rows, src_ap, srcB, g):
                    for o in range(8):
                        po, pob = ps()
                        mm(po[:], wo[0:rows, o * 128:(o + 1) * 128], src_ap, True, True, [woB, srcB], [pob])
                        x_update(l, 1, o, g, po, pob)

                if mixlim < 1:
                    S.barrier()
                    return
                lp = vecs[:, l, 148:276]
                S.op("dve", lambda e: e.tensor_tensor(out=tmp[:, 0:32], in0=lp[:, 0:32], in1=lp[:, 32:64], op=ALU.mult), reads=[cB], writes=[tmpB])
                S.op("dve", lambda e: e.reduce_sum(out=lv[:, 0:1], in_=tmp[:, 0:32], axis=AX.X), reads=[tmpB], writes=[lvB])
                S.op("dve", lambda e: e.tensor_tensor(out=tmp[:, 0:32], in0=lp[:, 64:96], in1=lp[:, 96:128], op=ALU.mult), reads=[cB, lvB], writes=[tmpB])
                S.op("dve", lambda e: e.reduce_sum(out=lv[:, 1:2], in_=tmp[:, 0:32], axis=AX.X), reads=[tmpB], writes=[lvB])
                S.op("act", lambda e: e.activation(out=lv[:, 2:4], in_=lv[:, 0:2], func=AF.Exp), reads=[lvB], writes=[lvB])
                lam_init = 0.8 - 0.6 * math.exp(-0.3 * l)
                S.op("dve", lambda e: e.scalar_tensor_tensor(out=lv[:, 4:5], in0=lv[:, 3:4], scalar=-lam_init, in1=lv[:, 2:3], op0=ALU.add, op1=ALU.subtract),
                     reads=[lvB], writes=[lvB])
                S.op("dve", lambda e: e.tensor_scalar(out=lv[:, 5:6], in0=vecs[:, l, 98:99], scalar1=1.0 - lam_init, scalar2=None, op0=ALU.mult),
                     reads=[cB, lvB], writes=[lvB])
                S.op("act", lambda e: e.activation(out=lv[:, 8:16], in_=vecs[:, l, 135:143], func=AF.Exp, scale=-1.0), reads=[cB, lvB], writes=[lvB])
                S.op("act", lambda e: e.activation(out=lv[:, 8:16], in_=lv[:, 8:16], func=AF.Ln, bias=1.0), reads=[lvB], writes=[lvB])
                S.op("dve", lambda e: e.tensor_scalar(out=lv[:, 8:16], in0=lv[:, 8:16], scalar1=-8.0, scalar2=None, op0=ALU.mult), reads=[lvB], writes=[lvB])

                with ExitStack() as pc:
                    cosT = sb(pc, "cosT", [128, NT])
                    sinT = sb(pc, "sinT", [128, NT])
                    S.dma("sp", cosT[:], cos_d, writes=[tabB])
                    S.dma("sp", sinT[:], sin_d, writes=[tabB])
                    cq0 = sb(pc, "c_cq0", [128, NT], F32R)
                    cq1 = sb(pc, "c_cq1", [128, NK], F32R)
                    ckv = sb(pc, "c_ckv", [128, NK], F32R)
                    cq0B, cq1B, krB, ckvB = Buf("c_cq0"), Buf("c_cq1"), Buf("c_kr"), Buf("c_ckv")
                    wuq = sb(pc, "c_wuq", [128, 2, 384], F32R)
                    wukv = sb(pc, "c_wukv", [128, 512], F32R)
                    wuB = Buf("c_wu")
                    pc0 = ExitStack()
                    hb_[0] = sb(pc0, "c_h", [128, 8, 512], F32R)
                    cst = sb(pc0, "c_cst", [128, 4, 160])
                    cstB = Buf("c_cst")
                    ost = sb(pc0, "c_ost", [128, 4, 160])
                    ostB = Buf("c_ost")
                    S.dma("sp", wuq[:, 0, :], w_uq_d[l, 0:128, :], writes=[wuB])
                    S.dma("sp", wuq[0:64, 1, :], w_uq_d[l, 128:192, :], writes=[wuB])
                    S.dma("sp", wukv[:], w_ukv_d[l], writes=[wuB])
                    S.dma("sp", cst[:, :, 0:128], cckv_d[l].rearrange("(t p) n -> p t n", p=128), writes=[cstB])
                    S.dma("sp", cst[:, :, 128:160], ckr_d[l].rearrange("(t p) n -> p t n", p=128), writes=[cstB])
                    pt, pb = ps()
                    for t in range(4):
                        S.op("pe", lambda e, t=t, pt=pt: e.transpose(pt[:, t * 128:(t + 1) * 128], cst[:, t, 0:128], ident[:]), reads=[cstB, cB], writes=[pb], acc=(t > 0), inc=(t == 3))
                    S.op("act", lambda e, pt=pt: e.copy(out=ckv[:, 0:NC_], in_=pt[:]), reads=[pb], writes=[ckvB])
                    pt, pb = ps()
                    for t in range(4):
                        S.op("pe", lambda e, t=t, pt=pt: e.transpose(pt[0:32, t * 128:(t + 1) * 128], cst[:, t, 128:160], ident[:]), reads=[cstB, cB], writes=[pb], acc=(t > 0), inc=(t == 3))
                    S.op("act", lambda e, pt=pt: e.copy(out=cq1[64:96, 0:NC_], in_=pt[0:32, :]), reads=[pb], writes=[krB])
                    for g in range(NG):
                        sl = slice(g * 512, (g + 1) * 512)
                        ksl = slice(NC_ + g * 512, NC_ + (g + 1) * 512)
                        ht, hB = get_h(g)
                        load_wp(0, 1792, 128, "sp")
                        load_wp(1, 1920, 64, "act")
                        p0, p0b = proj(0, 128, ht, hB)
                        p1, p1b = proj(1, 64, ht, hB)
                        S.op("act", lambda e, p0=p0: e.activation(out=sq[:, 0, :], in_=p0[:], func=AF.Square), reads=[p0b], writes=[sqB[0]])
                        S.op("act", lambda e, p1=p1: e.activation(out=sq[0:64, 1, :], in_=p1[0:64, :], func=AF.Square), reads=[p1b], writes=[sqB[1]])
                        p2, p2b = ps()
                        mm(p2[:], ones[:, :], sq[:, 0, :], True, False, [sqB[0], cB], [p2b])
                        mm(p2[:], ones[0:64, :], sq[0:64, 1, :], False, True, [sqB[1], cB], [p2b])
                        rstd_from_ps(p2, p2b, 128, 192, tm2, tm2B, tmp, tmpB)
                        S.op("dve", lambda e, p0=p0: e.tensor_tensor(out=tmp[:, :], in0=p0[:], in1=tm2[:, :], op=ALU.mult), reads=[p0b, tm2B], writes=[tmpB])
                        S.op("act", lambda e, sl=sl: e.activation(out=cq0[:, sl], in_=tmp[:, :], func=AF.Identity, scale=vecs[:, l, 143:144]), reads=[tmpB, cB], writes=[cq0B])
                        S.op("dve", lambda e, p1=p1: e.tensor_tensor(out=tmp[0:64, :], in0=p1[0:64, :], in1=tm2[0:64, :], op=ALU.mult), reads=[p1b, tm2B], writes=[tmpB])
                        S.op("act", lambda e, sl=sl: e.activation(out=cq1[0:64, sl], in_=tmp[0:64, :], func=AF.Identity, scale=vecs[0:64, l, 144:145]), reads=[tmpB, cB], writes=[cq1B])
                        load_wp(0, 1984, 128, "sp")
                        load_wp(1, 2112, 32, "act")
                        p0, p0b = proj(0, 128, ht, hB)
                        p1, p1b = proj(1, 32, ht, hB)
                        S.op("act", lambda e, p0=p0: e.activation(out=sq[:, 0, :], in_=p0[:], func=AF.Square), reads=[p0b], writes=[sqB[0]])
                        p2, p2b = ps()
                        mm(p2[:], ones[:, :], sq[:, 0, :], True, True, [sqB[0], cB], [p2b])
                        rstd_from_ps(p2, p2b, 128, 128, tm2, tm2B, tmp, tmpB)
                        S.op("dve", lambda e, p0=p0: e.tensor_tensor(out=tmp[:, :], in0=p0[:], in1=tm2[:, :], op=ALU.mult), reads=[p0b, tm2B], writes=[tmpB])
                        S.op("act", lambda e, ksl=ksl: e.activation(out=ckv[:, ksl], in_=tmp[:, :], func=AF.Identity, scale=vecs[:, l, 145:146]), reads=[tmpB, cB], writes=[ckvB])
                        S.op("act", lambda e, p1=p1, ksl=ksl: e.copy(out=cq1[64:96, ksl], in_=p1[0:32, :]), reads=[p1b], writes=[krB])
                        p3, p3b = ps()
                        for t in range(4):
                            S.op("pe", lambda e, t=t, p3=p3, ksl=ksl: e.transpose(p3[:, t * 128:(t + 1) * 128], ckv[:, ksl.start + t * 128:ksl.start + (t + 1) * 128].bitcast(F32), ident[:]),
                                 reads=[ckvB, cB], writes=[p3b], acc=(t > 0), inc=(t == 3))
                        S.op("dve", lambda e, p3=p3: e.tensor_copy(out=ost[:, :, 0:128], in_=p3[:].rearrange("p (t n) -> p t n", n=128)), reads=[p3b], writes=[ostB])
                        p4, p4b = ps()
                        for t in range(4):
                            S.op("pe", lambda e, t=t, p4=p4, ksl=ksl: e.transpose(p4[:, t * 32:(t + 1) * 32], cq1[64:96, ksl.start + t * 128:ksl.start + (t + 1) * 128].bitcast(F32), ident[64:96, 64:96]),
                                 reads=[krB, cB], writes=[p4b], acc=(t > 0), inc=(t == 3))
                        S.op("dve", lambda e, p4=p4: e.tensor_copy(out=ost[:, :, 128:160], in_=p4[:, 0:128].rearrange("p (t n) -> p t n", n=32)), reads=[p4b, ostB], writes=[ostB])
                        S.dma("pool", ockv_d[l, sl, :].rearrange("(t p) n -> p t n", p=128), ost[:, :, 0:128], reads=[ostB])
                        S.dma("pool", okr_d[l, sl, :].rearrange("(t p) n -> p t n", p=128), ost[:, :, 128:160], reads=[ostB])
                    S.barrier()
                    pc0.close()
                    if mixlim < 2:
                        return
                    Qm = sb(pc, "c_Q", [104, NT], BF16)
                    Km = sb(pc, "c_K", [104, NK], BF16)
                    Vm = sb(pc, "c_V", [128, NKB, 65], BF16)
                    QB, KB, VB = Buf("c_Q"), Buf("c_K"), Buf("c_V")
                    PT = [sb(pc, "c_PT%d" % i, [128, 512], BF16) for i in range(4)]
                    PTB = [Buf("c_PT%d" % i) for i in range(4)]
                    osb = sb(pc, "c_osb", [65, 512], F32R)
                    osbB = Buf("c_osb")
                    oc = sb(pc, "c_oc", [64, 512], F32R)
                    ocB = Buf("c_oc")
                    qn = sb(pc, "c_qn", [96, 512], F32R)
                    qnB = Buf("c_qn")
                    rs = sb(pc, "c_rs", [96, 512])
                    rsB = Buf("c_rs")
                    S.dma("sp", Qm[96:104, :], mq_d, writes=[QB])
                    S.dma("sp", Km[96:104, :], mk_d, writes=[KB])
                    S.op("pool", lambda e: e.memset(Vm[:, :, 64:65], 1.0), writes=[VB])
                    sc = 96 ** -0.5
                    ptc = [0]
                    for hd in range(4):
                        S.dma("pool", wo[0:64, :], w_out_d[l, 768 + hd * 64:768 + (hd + 1) * 64, :], writes=[woB])
                        for kg in range(NK // 512):
                            ksl = slice(kg * 512, (kg + 1) * 512)
                            pk, pkb = ps()
                            mm(pk[0:64, :], wukv[:, hd * 128:hd * 128 + 64], ckv[:, ksl], True, True, [wuB, ckvB], [pkb])
                            S.op("act", lambda e, pk=pk: e.activation(out=sq[0:64, 0, :], in_=pk[0:64, :], func=AF.Square), reads=[pkb], writes=[sqB[0]])
                            S.op("dve", lambda e, ksl=ksl: e.tensor_tensor(out=sq[0:32, 1, :], in0=cq1[64:96, ksl].bitcast(F32), in1=cq1[64:96, ksl].bitcast(F32), op=ALU.mult),
                                 reads=[krB], writes=[sqB[1]])
                            p2, p2b = ps()
                            mm(p2[0:96, :], ones[0:64, 0:96], sq[0:64, 0, :], True, False, [sqB[0], cB], [p2b])
                            mm(p2[0:96, :], ones[0:32, 0:96], sq[0:32, 1, :], False, True, [sqB[1], cB], [p2b])
                            rstd_from_ps(p2, p2b, 96, 96, rs, rsB, tm2, tm2B)
                            S.op("dve", lambda e, pk=pk: e.tensor_tensor(out=tmp[0:64, :], in0=pk[0:64, :], in1=rs[0:64, :], op=ALU.mult), reads=[pkb, rsB], writes=[tmpB])
                            S.op("dve", lambda e, ksl=ksl: e.tensor_tensor(out=tmp[64:96, :], in0=cq1[64:96, ksl].bitcast(F32), in1=rs[64:96, :], op=ALU.mult), reads=[krB, rsB], writes=[tmpB])
                            if kg == 0:
                                S.op("act", lambda e, ksl=ksl: e.activation(out=Km[0:96, ksl], in_=tmp[0:96, :], func=AF.Identity, scale=vecs[0:96, l, 147:148]), reads=[tmpB, cB], writes=[KB])
                            else:
                                g = kg - 1
                                S.op("act", lambda e: e.activation(out=qn[:, :], in_=tmp[0:96, :], func=AF.Identity, scale=vecs[0:96, l, 147:148]), reads=[tmpB, cB], writes=[qnB])
                                p4, p4b = ps()
                                mm(p4[0:96, :], R96[0:96, 0:96], qn[:, :], True, True, [qnB, cB], [p4b])
                                S.op("act", lambda e, ksl=ksl: e.copy(out=Km[0:64, ksl], in_=qn[0:64, :].bitcast(F32)), reads=[qnB], writes=[KB])
                                S.op("dve", lambda e, g=g: e.tensor_tensor(out=tmp[64:96, :], in0=qn[64:96, :].bitcast(F32), in1=cosT[64:96, g * 512:(g + 1) * 512], op=ALU.mult),
                                     reads=[qnB, tabB], writes=[tmpB])
                                S.op("dve", lambda e, p4=p4, g=g: e.tensor_tensor(out=tm2[64:96, :], in0=p4[64:96, :], in1=sinT[64:96, g * 512:(g + 1) * 512], op=ALU.mult),
                                     reads=[p4b, tabB], writes=[tm2B])
                                S.op("dve", lambda e, ksl=ksl: e.tensor_tensor(out=Km[64:96, ksl], in0=tmp[64:96, :], in1=tm2[64:96, :], op=ALU.add), reads=[tmpB, tm2B], writes=[KB])
                            pv, pvb = ps()
                            for t in range(4):
                                mm(pv[:, t * 64:(t + 1) * 64], ckv[:, kg * 512 + t * 128:kg * 512 + (t + 1) * 128], wukv[:, hd * 128 + 64:hd * 128 + 128], True, True, [wuB, ckvB], [pvb])
                            S.op("act", lambda e, pv=pv, kg=kg: e.copy(out=Vm[:, kg * 4:kg * 4 + 4, 0:64], in_=pv[:, 0:256].rearrange("p (t n) -> p t n", n=64)), reads=[pvb], writes=[VB])
                        for g in range(NG):
                            sl = slice(g * 512, (g + 1) * 512)
                            pq, pqb = ps()
                            mm(pq[0:96, :], wuq[:, 0, hd * 96:(hd + 1) * 96], cq0[:, sl], True, False, [wuB, cq0B], [pqb])
                            mm(pq[0:96, :], wuq[0:64, 1, hd * 96:(hd + 1) * 96], cq1[0:64, sl], False, True, [wuB, cq1B], [pqb])
                            S.op("act", lambda e, pq=pq: e.activation(out=sq[0:96, 0, :], in_=pq[0:96, :], func=AF.Square), reads=[pqb], writes=[sqB[0]])
                            p2, p2b = ps()
                            mm(p2[0:96, :], ones[0:96, 0:96], sq[0:96, 0, :], True, True, [sqB[0], cB], [p2b])
                            rstd_from_ps(p2, p2b, 96, 96, rs, rsB, tm2, tm2B)
                            S.op("dve", lambda e, pq=pq: e.scalar_tensor_tensor(out=qn[:, :], in0=pq[0:96, :], scalar=vecs[0:96, l, 146:147], in1=rs[0:96, :], op0=ALU.mult, op1=ALU.mult),
                                 reads=[pqb, rsB, cB], writes=[qnB])
                            p4, p4b = ps()
                            mm(p4[0:96, :], R96[0:96, 0:96], qn[:, :], True, True, [qnB, cB], [p4b])
                            S.op("act", lambda e, sl=sl: e.copy(out=Qm[0:64, sl], in_=qn[0:64, :].bitcast(F32)), reads=[qnB], writes=[QB])
                            S.op("dve", lambda e, sl=sl: e.tensor_tensor(out=tmp[64:96, :], in0=qn[64:96, :].bitcast(F32), in1=cosT[64:96, sl], op=ALU.mult), reads=[qnB, tabB], writes=[tmpB])
                            S.op("dve", lambda e, p4=p4, sl=sl: e.tensor_tensor(out=tm2[64:96, :], in0=p4[64:96, :], in1=sinT[64:96, sl], op=ALU.mult), reads=[p4b, tabB], writes=[tm2B])
                            S.op("dve", lambda e, sl=sl: e.tensor_tensor(out=Qm[64:96, sl], in0=tmp[64:96, :], in1=tm2[64:96, :], op=ALU.add), reads=[tmpB, tm2B], writes=[QB])
                        for g in range(NG):
                            O, OB = PS[6], PSB[6]

                            def s_mm(kb, pS, pSb, g=g):
                                S.op("pe", lambda e: e.matmul(pS[:], Km[0:104, kb * 128:(kb + 1) * 128], Qm[0:104, g * 512:(g + 1) * 512], start=True, stop=True),
                                     reads=[KB, QB], writes=[pSb])

                            def pv_mm(kb, P_, PB_):
                                mm(PS[6][0:65, :], Vm[:, kb, :], P_[:], kb == 0, kb == NKB - 1, [VB, PB_], [PSB[6]])

                            attn_pipe(list(range(NKB)), s_mm, pv_mm, PT, PTB, ptc, sc, LA=3)
                            S.op("act", lambda e, O=O: e.copy(out=osb[:, :], in_=O[0:65, :]), reads=[OB], writes=[osbB])
                            pd, pdb = ps()
                            mm(pd[0:64, :], sel65[0:65, 0:64], osb[:, :], True, True, [osbB, cB], [pdb])
                            S.op("act", lambda e, pd=pd: e.activation(out=tm2[0:64, :], in_=pd[0:64, :], func=AF.Ln), reads=[pdb], writes=[tm2B])
                            S.op("act", lambda e: e.activation(out=tm2[0:64, :], in_=tm2[0:64, :], func=AF.Exp, scale=-1.0), reads=[tm2B], writes=[tm2B])
                            S.op("dve", lambda e: e.tensor_tensor(out=oc[:, :], in0=osb[0:64, :].bitcast(F32), in1=tm2[0:64, :], op=ALU.mult), reads=[osbB, tm2B], writes=[ocB])
                            out_proj(64, oc[:, :], ocB, g)
                    S.barrier()

                if mixlim < 3:
                    return
                with ExitStack() as pa:
                    cosT = sb(pa, "cosT", [128, NT])
                    sinT = sb(pa, "sinT", [128, NT])
                    S.dma("sp", cosT[:], cos_d, writes=[tabB])
                    S.dma("sp", sinT[:], sin_d, writes=[tabB])
                    hb_[0] = sb(pa, "a_h", [128, 8, 512], F32R)
                    Qd = sb(pa, "a_Q", [72, 2, NT], BF16)
                    Kd = sb(pa, "a_K", [72, 2, NK], BF16)
                    Vd = sb(pa, "a_V", [128, NKB, 65], BF16)
                    QB, KB, VB = Buf("a_Q"), Buf("a_K"), Buf("a_V")
                    PT = [sb(pa, "a_PT%d" % i, [128, 512], BF16) for i in range(4)]
                    PTB = [Buf("a_PT%d" % i) for i in range(4)]
                    osb = sb(pa, "a_osb", [65, 2, 512], F32R)
                    osbB = Buf("a_osb")
                    on = sb(pa, "a_on", [64, 2, 512])
                    onB = Buf("a_on")
                    oa = sb(pa, "a_oa", [64, 512], F32R)
                    oaB = Buf("a_oa")
                    kn = sb(pa, "a_kn", [64, 512], F32R)
                    knB = Buf("a_kn")
                    rs = sb(pa, "a_rs", [64, 512])
                    rsB = Buf("a_rs")
                    kn1 = sb(pa, "a_kn1", [64, 512], F32R)
                    rs1 = sb(pa, "a_rs1", [64, 512])
                    tmpb = sb(pa, "a_tmpb", [64, 512])
                    tm2b = sb(pa, "a_tm2b", [64, 512])
                    knw, knwB = [kn, kn1], [knB, Buf("a_kn1")]
                    rsw, rswB = [rs, rs1], [rsB, Buf("a_rs1")]
                    tw1, tw1B = [tmp, tmpb], [tmpB, Buf("a_tmpb")]
                    tw2, tw2B = [tm2, tm2b], [tm2B, Buf("a_tm2b")]
                    cst = sb(pa, "a_cst", [128, 4, 64])
                    cstB = Buf("a_cst")
                    ost = sb(pa, "a_ost", [128, 4, 64])
                    ostB = Buf("a_ost")
                    S.op("pool", lambda e: e.memset(Qd[:], 0.0), writes=[QB])
                    S.op("pool", lambda e: e.memset(Kd[:], 0.0), writes=[KB])
                    for c in range(2):
                        S.dma("sp", Qd[32:40, c, :], mq_d, writes=[QB])
                        S.dma("sp", Kd[32:40, c, :], mk_d, writes=[KB])
                    S.op("pool", lambda e: e.memset(Vd[:, :, 64:65], 1.0), writes=[VB])
                    ptc = [0]
                    for hd in range(4):
                        load_wp(0, 0 + hd * 64, 64, "sp")
                        load_wp(1, 256 + hd * 64, 64, "act")
                        S.dma("pool", wo[0:64, :], w_out_d[l, hd * 64:(hd + 1) * 64, :], writes=[woB])
                        S.dma("sp", cst[:], ck_d[l, :, hd * 64:(hd + 1) * 64].rearrange("(t p) n -> p t n", p=128), writes=[cstB])
                        pt, pb = ps()
                        for t in range(4):
                            S.op("pe", lambda e, t=t, pt=pt: e.transpose(pt[0:64, t * 128:(t + 1) * 128], cst[:, t, :], ident[:]),
                                 reads=[cstB, cB], writes=[pb], acc=(t > 0), inc=(t == 3))
                        for c in range(2):
                            S.op("dve", lambda e, c=c, pt=pt: e.tensor_copy(out=Kd[0:32, c, 0:NC_], in_=pt[c * 32:(c + 1) * 32, :]), reads=[pb], writes=[KB])
                        S.dma("sp", cst[:], cv_d[l, :, hd * 64:(hd + 1) * 64].rearrange("(t p) n -> p t n", p=128), reads=[], writes=[cstB])
                        S.op("dve", lambda e: e.tensor_copy(out=Vd[:, 0:4, 0:64], in_=cst[:]), reads=[cstB], writes=[VB])
                        if alim < 1:
                            S.barrier()
                            return
                        for g in range(NG):
                            ht, hB = get_h(g)
                            W2 = (0, 1)
                            pts = [proj(w, 64, ht, hB) for w in W2]
                            for w in W2:
                                S.op("act", lambda e, w=w: e.activation(out=sq[0:64, w, :], in_=pts[w][0][0:64, :], func=AF.Square), reads=[pts[w][1]], writes=[sqB[w]])
                            p2s = []
                            for w in W2:
                                p2, p2b = ps()
                                mm(p2[0:64, :], bd32[0:64, 0:64], sq[0:64, w, :], True, True, [sqB[w], cB], [p2b])
                                p2s.append((p2, p2b))
                            for w in W2:
                                S.op("act", lambda e, w=w: e.activation(out=tw2[w][0:64, :], in_=p2s[w][0][0:64, :], func=AF.Ln, scale=1.0 / 32, bias=epsc[0:64, 0:1]),
                                     reads=[p2s[w][1], cB], writes=[tw2B[w]])
                            for w in W2:
                                S.op("act", lambda e, w=w: e.activation(out=rsw[w][:, :], in_=tw2[w][0:64, :], func=AF.Exp, scale=-0.5), reads=[tw2B[w]], writes=[rswB[w]])
                            for w in W2:
                                S.op("dve", lambda e, w=w: e.scalar_tensor_tensor(out=knw[w][:, :], in0=pts[w][0][0:64, :], scalar=vecs[0:64, l, 96 + w:97 + w], in1=rsw[w][:, :],
                                                                                  op0=ALU.mult, op1=ALU.mult),
                                     reads=[pts[w][1], rswB[w], cB], writes=[knwB[w]])
                            p4s = []
                            for w in W2:
                                p4, p4b = ps()
                                mm(p4[0:64, :], R64[0:64, 0:64], knw[w][:, :], True, True, [knwB[w], cB], [p4b])
                                p4s.append((p4, p4b))
                            p3, p3b = ps()
                            for t in range(4):
                                S.op("pe", lambda e, t=t, p3=p3: e.transpose(p3[:, t * 64:(t + 1) * 64], knw[1][:, t * 128:(t + 1) * 128].bitcast(F32), ident[0:64, 0:64]),
                                     reads=[knwB[1], cB], writes=[p3b], acc=(t > 0), inc=(t == 3))
                            S.op("dve", lambda e, p3=p3: e.tensor_copy(out=ost[:].rearrange("p t n -> p (t n)"), in_=p3[:, 0:256]), reads=[p3b], writes=[ostB])
                            S.dma("pool", ok_d[l, g * 512:(g + 1) * 512, hd * 64:(hd + 1) * 64].rearrange("(t p) n -> p t n", p=128), ost[:], reads=[ostB])
                            for w in W2:
                                S.op("dve", lambda e, w=w: e.tensor_tensor(out=tw1[w][0:64, :], in0=knw[w][:, :].bitcast(F32), in1=cosT[0:64, g * 512:(g + 1) * 512], op=ALU.mult),
                                     reads=[knwB[w], tabB], writes=[tw1B[w]])
                            for w in W2:
                                S.op("dve", lambda e, w=w: e.tensor_tensor(out=tw2[w][0:64, :], in0=p4s[w][0][0:64, :], in1=sinT[0:64, g * 512:(g + 1) * 512], op=ALU.mult),
                                     reads=[p4s[w][1], tabB], writes=[tw2B[w]])
                            for w in W2:
                                for c in range(2):
                                    if w == 0:
                                        dst = Qd[0:32, c, g * 512:(g + 1) * 512]
                                    else:
                                        dst = Kd[0:32, c, NC_ + g * 512:NC_ + (g + 1) * 512]
                                    S.op("dve", lambda e, c=c, dst=dst, w=w: e.tensor_tensor(out=dst, in0=tw1[w][c * 32:(c + 1) * 32, :], in1=tw2[w][c * 32:(c + 1) * 32, :], op=ALU.add),
                                         reads=[tw1B[w], tw2B[w]], writes=[QB if w == 0 else KB])
                            chk()
                            S.dma("sp", wp[0][:, :, 64:128], wiv[:, :, 512 + hd * 64:512 + (hd + 1) * 64], writes=[wpB[0]]) if g == 0 else None
                            pv, pvb = ps()
                            for t in range(4):
                                for k in range(8):
                                    mm(pv[:, t * 64:(t + 1) * 64], ht[:, k, t * 128:(t + 1) * 128], wp[0][:, k, 64:128], k == 0, k == 7, [wpB[0], hB[k]], [pvb])
                            S.op("act", lambda e, pv=pv, g=g: e.copy(out=Vd[:, 4 + g * 4:8 + g * 4, 0:64], in_=pv[:, 0:256].rearrange("p (t n) -> p t n", n=64)),
                                 reads=[pvb], writes=[VB])
                            S.op("dve", lambda e, pv=pv: e.tensor_copy(out=ost[:].rearrange("p t n -> p (t n)"), in_=pv[:, 0:256]), reads=[pvb], writes=[ostB])
                            S.dma("pool", ov_d[l, g * 512:(g + 1) * 512, hd * 64:(hd + 1) * 64].rearrange("(t p) n -> p t n", p=128), ost[:], reads=[ostB])
                            chk()
                        if alim < 2:
                            S.barrier()
                            return
                        sc = 32 ** -0.5
                        for g in range(NG):
                            items = [(kb, c) for kb in range(NKB) for c in range(2)]

                            def s_mm(it, pS, pSb, g=g):
                                kb, c = it
                                S.op("pe", lambda e: e.matmul(pS[:], Kd[0:72, c, kb * 128:(kb + 1) * 128], Qd[0:72, c, g * 512:(g + 1) * 512], start=True, stop=True),
                                     reads=[KB, QB], writes=[pSb])

                            def pv_mm(it, P_, PB_):
                                kb, c = it
                                mm(PS[6 + c][0:65, :], Vd[:, kb, :], P_[:], kb == 0, kb == NKB - 1, [VB, PB_], [PSB[6 + c]])

                            attn_pipe(items, s_mm, pv_mm, PT, PTB, ptc, sc, LA=3)
                            for c in range(2):
                                O, OB = PS[6 + c], PSB[6 + c]
                                S.op("act", lambda e, c=c, O=O: e.copy(out=osb[:, c, :], in_=O[0:65, :]), reads=[OB], writes=[osbB])
                                pd, pdb = ps()
                                mm(pd[0:64, :], sel65[0:65, 0:64], osb[:, c, :], True, True, [osbB, cB], [pdb])
                                S.op("act", lambda e, pd=pd: e.activation(out=tm2[0:64, :], in_=pd[0:64, :], func=AF.Ln), reads=[pdb], writes=[tm2B])
                                S.op("act", lambda e: e.activation(out=tm2[0:64, :], in_=tm2[0:64, :], func=AF.Exp, scale=-1.0), reads=[tm2B], writes=[tm2B])
                                S.op("dve", lambda e, c=c: e.tensor_tensor(out=on[:, c, :], in0=osb[0:64, c, :].bitcast(F32), in1=tm2[0:64, :], op=ALU.mult),
                                     reads=[osbB, tm2B], writes=[onB])
                            S.op("dve", lambda e: e.scalar_tensor_tensor(out=on[:, 0, :], in0=on[:, 1, :], scalar=lv[0:64, 4:5], in1=on[:, 0, :], op0=ALU.mult, op1=ALU.add),
                                 reads=[onB, lvB], writes=[onB])
                            S.op("act", lambda e: e.activation(out=sq[0:64, 0, :], in_=on[:, 0, :], func=AF.Square), reads=[onB], writes=[sqB[0]])
                            p2, p2b = ps()
                            mm(p2[0:64, :], ones[0:64, 0:64], sq[0:64, 0, :], True, True, [sqB[0], cB], [p2b])
                            rstd_from_ps(p2, p2b, 64, 64, rs, rsB, tm2, tm2B)
                            S.op("dve", lambda e: e.tensor_tensor(out=tmp[0:64, :], in0=on[:, 0, :], in1=rs[:, :], op=ALU.mult), reads=[onB, rsB], writes=[tmpB])
                            S.op("act", lambda e: e.activation(out=oa[:, :], in_=tmp[0:64, :], func=AF.Identity, scale=lv[0:64, 5:6]), reads=[tmpB, lvB], writes=[oaB])
                            out_proj(64, oa[:, :], oaB, g)
                    S.barrier()

                if mixlim < 4:
                    return
                with ExitStack() as pb_:
                    hb_[0] = sb(pb_, "b_h", [128, 8, 512], F32R)
                    xb = sb(pb_, "b_xb", [128, NT])
                    xc = sb(pb_, "b_xc", [128, NT], F32R)
                    gg = sb(pb_, "b_gg", [128, NT])
                    Pb = sb(pb_, "b_P", [128, NT])
                    Qb = sb(pb_, "b_Q", [128, NT])
                    xbB, xcB, ggB, PbB, QbB = Buf("b_xb"), Buf("b_xc"), Buf("b_gg"), Buf("b_P"), Buf("b_Q")
                    wgt = sb(pb_, "b_wg", [128, 4, 128], F32R)
                    wgtB = Buf("b_wg")
                    A2 = sb(pb_, "b_A2", [128, NT])
                    A2B = Buf("b_A2")
                    rrd = [sb(pb_, "b_r%d" % d, [128, 512]) for d in range(2)]
                    iid = [sb(pb_, "b_i%d" % d, [128, 512]) for d in range(2)]
                    tad = [sb(pb_, "b_t%d" % d, [128, 512]) for d in range(2)]
                    rrdB = [Buf("b_r%d" % d) for d in range(2)]
                    iidB = [Buf("b_i%d" % d) for d in range(2)]
                    tadB = [Buf("b_t%d" % d) for d in range(2)]
                    nk = sb(pb_, "b_nk", [128, 4])
                    nkB = Buf("b_nk")
                    xcf = xc[:].bitcast(F32)
                    for cc in range(4):
                        load_wp(0, 768 + cc * 128, 128, "sp")
                        load_wp(1, 1280 + cc * 128, 128, "act")
                        S.dma("pool", wo[:, :], w_out_d[l, 256 + cc * 128:256 + (cc + 1) * 128, :], writes=[woB])
                        S.op("dve", lambda e: e.tensor_scalar(out=wgt[:], in0=mhalf[:, :].rearrange("p (a b) -> p a b", b=128), scalar1=0.0, scalar2=None, op0=ALU.mult), reads=[cB], writes=[wgtB])
                        for d in range(2):
                            for gt in range(2):
                                for bl in range(2):
                                    S.dma("sp", wgt[bl * 64:(bl + 1) * 64, d * 2 + gt, bl * 64:(bl + 1) * 64], w_gate_d[l, d, gt, cc * 2 + bl], writes=[wgtB])
                        for g in range(NG):
                            ht, hB = get_h(g)
                            pt, pb = proj(0, 128, ht, hB)
                            S.op("act", lambda e, pt=pt, g=g: e.copy(out=xb[:, g * 512:(g + 1) * 512], in_=pt[:]), reads=[pb], writes=[xbB])
                            pt, pb = proj(1, 128, ht, hB)
                            S.op("act", lambda e, pt=pt: e.activation(out=tmp[:, :], in_=pt[:], func=AF.Square), reads=[pb], writes=[tmpB])
                            S.op("dve", lambda e: e.tensor_scalar(out=tmp[:, :], in0=tmp[:, :], scalar1=0.044715, scalar2=1.0, op0=ALU.mult, op1=ALU.add), reads=[tmpB], writes=[tmpB])
                            S.op("dve", lambda e, pt=pt: e.tensor_tensor(out=tmp[:, :], in0=tmp[:, :], in1=pt[:], op=ALU.mult), reads=[tmpB, pb], writes=[tmpB])
                            S.op("act", lambda e: e.activation(out=tm2[:, :], in_=tmp[:, :], func=AF.Sigmoid, scale=1.5957691216057308), reads=[tmpB], writes=[tm2B])
                            S.op("dve", lambda e, pt=pt, g=g: e.tensor_tensor(out=gg[:, g * 512:(g + 1) * 512], in0=tm2[:, :], in1=pt[:], op=ALU.mult), reads=[tm2B, pb], writes=[ggB])
                        cw = lambda j: vecs[:, l, 99 + j * 4 + cc:100 + j * 4 + cc]
                        S.op("dve", lambda e: e.tensor_scalar(out=xc[:, :], in0=xb[:, :], scalar1=cw(2), scalar2=vecs[:, l, 115 + cc:116 + cc], op0=ALU.mult, op1=ALU.add),
                             reads=[xbB, cB], writes=[xcB])
                        for j, off in ((0, -2), (1, -1), (3, 1)):
                            lo, hi = max(0, -off), NT - max(0, off)
                            S.op("dve", lambda e, j=j, off=off, lo=lo, hi=hi: e.scalar_tensor_tensor(out=xc[:, lo:hi], in0=xb[:, lo + off:hi + off], scalar=cw(j), in1=xcf[:, lo:hi],
                                                                                                  op0=ALU.mult, op1=ALU.add), reads=[xbB, cB], writes=[xcB])
                        S.op("dve", lambda e: e.tensor_scalar(out=nk[:, 0:4], in0=vecs[:, l, 99 + cc:99 + cc + 13:4], scalar1=keepv[:, 1:2], scalar2=None, op0=ALU.mult),
                             reads=[cB], writes=[nkB])
                        xc3 = xc[:].rearrange("p (s t) -> p s t", t=256)
                        xcf3 = xcf.rearrange("p (s t) -> p s t", t=256)
                        xb3 = xb[:].rearrange("p (s t) -> p s t", t=256)
                        S.op("dve", lambda e: e.scalar_tensor_tensor(out=xc3[:, 1:8, 0:2], in0=xb3[:, 0:7, 254:256], scalar=nk[:, 0:1], in1=xcf3[:, 1:8, 0:2], op0=ALU.mult, op1=ALU.add),
                             reads=[xbB, nkB], writes=[xcB])
                        S.op("dve", lambda e: e.scalar_tensor_tensor(out=xc3[:, 1:8, 0:1], in0=xb3[:, 0:7, 255:256], scalar=nk[:, 1:2], in1=xcf3[:, 1:8, 0:1], op0=ALU.mult, op1=ALU.add),
                             reads=[xbB, nkB], writes=[xcB])
                        S.op("dve", lambda e: e.scalar_tensor_tensor(out=xc3[:, 0:7, 255:256], in0=xb3[:, 1:8, 0:1], scalar=nk[:, 3:4], in1=xcf3[:, 0:7, 255:256], op0=ALU.mult, op1=ALU.add),
                             reads=[xbB, nkB], writes=[xcB])
                        AA = [xb, A2]
                        AAB = [xbB, A2B]
                        HB_ = [Pb, Qb]
                        HBB = [PbB, QbB]
                        D2 = (0, 1)
                        for g in range(NG):
                            sl = slice(g * 512, (g + 1) * 512)
                            prs, pis = [], []
                            for d in D2:
                                pr, prb = ps()
                                mm(pr[:], wgt[:, d * 2 + 0, :], xc[:, sl], True, True, [wgtB, xcB], [prb])
                                prs.append((pr, prb))
                            for d in D2:
                                pi, pib = ps()
                                mm(pi[:], wgt[:, d * 2 + 1, :], xc[:, sl], True, True, [wgtB, xcB], [pib])
                                pis.append((pi, pib))
                            for d in D2:
                                S.op("act", lambda e, d=d: e.activation(out=rrd[d][:, :], in_=prs[d][0][:], func=AF.Sigmoid, bias=vecs[:, l, 119 + (d * 2 + 0) * 4 + cc:120 + (d * 2 + 0) * 4 + cc]),
                                     reads=[prs[d][1], cB], writes=[rrdB[d]])
                            for d in D2:
                                S.op("act", lambda e, d=d: e.activation(out=iid[d][:, :], in_=pis[d][0][:], func=AF.Sigmoid, bias=vecs[:, l, 119 + (d * 2 + 1) * 4 + cc:120 + (d * 2 + 1) * 4 + cc]),
                                     reads=[pis[d][1], cB], writes=[iidB[d]])
                            for d in D2:
                                S.op("act", lambda e, d=d: e.activation(out=AA[d][:, sl], in_=rrd[d][:, :], func=AF.Exp, scale=lv[:, 8 + d * 4 + cc:9 + d * 4 + cc]),
                                     reads=[rrdB[d], lvB, xcB], writes=[AAB[d]])
                            for d in D2:
                                S.op("dve", lambda e, d=d: e.tensor_tensor(out=tad[d][:, :], in0=AA[d][:, sl], in1=AA[d][:, sl], op=ALU.mult), reads=[AAB[d]], writes=[tadB[d]])
                            for d in D2:
                                S.op("dve", lambda e, d=d: e.tensor_scalar(out=tad[d][:, :], in0=tad[d][:, :], scalar1=-1.0, scalar2=1.0, op0=ALU.mult, op1=ALU.add), reads=[tadB[d]], writes=[tadB[d]])
                            for d in D2:
                                S.op("dve", lambda e, d=d: e.tensor_scalar(out=tad[d][:, :], in0=tad[d][:, :], scalar1=1e-20, scalar2=None, op0=ALU.max), reads=[tadB[d]], writes=[tadB[d]])
                            for d in D2:
                                S.op("act", lambda e, d=d: e.activation(out=tad[d][:, :], in_=tad[d][:, :], func=AF.Ln), reads=[tadB[d]], writes=[tadB[d]])
                            for d in D2:
                                S.op("act", lambda e, d=d: e.activation(out=rrd[d][:, :], in_=tad[d][:, :], func=AF.Exp, scale=0.5), reads=[tadB[d], rrdB[d]], writes=[rrdB[d]])
                            for d in D2:
                                S.op("dve", lambda e, d=d: e.tensor_tensor(out=iid[d][:, :], in0=iid[d][:, :], in1=xcf[:, sl], op=ALU.mult), reads=[iidB[d], xcB], writes=[iidB[d]])
                            for d in D2:
                                S.op("dve", lambda e, d=d: e.tensor_tensor(out=HB_[d][:, sl], in0=iid[d][:, :], in1=rrd[d][:, :], op=ALU.mult), reads=[iidB[d], rrdB[d]], writes=[HBB[d]])
                        for d in D2:
                            A3 = AA[d][:].rearrange("p (s t) -> p s t", t=256)
                            col = 0 if d == 0 else 255
                            S.op("dve", lambda e, col=col, A3=A3: e.tensor_scalar(out=A3[:, :, col:col + 1], in0=A3[:, :, col:col + 1], scalar1=keepv[:, 0:1], scalar2=None, op0=ALU.mult),
                                 reads=[AAB[d], cB], writes=[AAB[d]])
                            h0 = lru0[:, l, d * 4 + cc:d * 4 + cc + 1]
                            if d == 0:
                                S.op("dve", lambda e, h0=h0: e.tensor_tensor_scan(out=Pb[:, :], data0=AA[0][:, :], data1=Pb[:, :], initial=h0, op0=ALU.mult, op1=ALU.add),
                                     reads=[AAB[0], PbB, cB], writes=[PbB])
                                fin = Pb[:].rearrange("p (s t) -> p s t", t=256)[:, :, 255]
                                fB = PbB
                            else:
                                Qf = Qb[:]
                                S.op("dve", lambda e, h0=h0, Qf=Qf: e.tensor_tensor_scan(out=Qf[:, ::-1], data0=AA[1][:, ::-1], data1=Qf[:, ::-1], initial=h0, op0=ALU.mult, op1=ALU.add),
                                     reads=[AAB[1], QbB, cB], writes=[QbB])
                                fin = Qf.rearrange("p (s t) -> p s t", t=256)[:, :, 0]
                                fB = QbB
                            c0 = ((l * 2 + d) * 4 + cc) * 8
                            S.op("dve", lambda e, fin=fin, c0=c0: e.tensor_copy(out=stT[:, c0:c0 + 8], in_=fin), reads=[fB], writes=[stB])
                        S.op("dve", lambda e: e.tensor_tensor(out=Pb[:, :], in0=Pb[:, :], in1=Qb[:, :], op=ALU.add), reads=[PbB, QbB], writes=[PbB])
                        S.op("dve", lambda e: e.tensor_tensor(out=xc[:, :], in0=Pb[:, :], in1=gg[:, :], op=ALU.mult), reads=[PbB, ggB], writes=[xcB])
                        for g in range(NG):
                            out_proj(128, xc[:, g * 512:(g + 1) * 512], xcB, g)
                    S.barrier()

        import os
        stop = int(os.environ.get("MK_STOP", "99"))
        stage = 0
        for l in range(DEPTH):
            for fn in (lambda: ffn(l, 0, 0), lambda: mixer(l), lambda: ffn(l, 2, 1)):
                if stage < stop:
                    try:
                        fn()
                    except StopMixer:
                        pass
                    S.mute = False
                stage += 1

        with ExitStack() as ph:
            stg = [sb(ph, "ystg%d" % i, [128, D]) for i in range(2)]
            stgB = [Buf("ystg%d" % i) for i in range(2)]
            for tt in range(NT // 128):
                st_, stb_ = stg[tt % 2], stgB[tt % 2]
                g = tt // 4
                for half in range(2):
                    pt, pb = ps()
                    for cc in range(4):
                        c = half * 4 + cc
                        S.op("pe", lambda e, c=c, cc=cc, pt=pt: e.transpose(pt[:, cc * 128:(cc + 1) * 128], xT[:, c, tt * 128:(tt + 1) * 128], ident[:]),
                             reads=[xB[c][g], cB], writes=[pb], acc=(cc > 0), inc=(cc == 3))
                    S.op("dve" if half else "act",
                         (lambda e, pt=pt, st_=st_: e.tensor_copy(out=st_[:, 512:1024], in_=pt[:])) if half else
                         (lambda e, pt=pt, st_=st_: e.copy(out=st_[:, 0:512], in_=pt[:])),
                         reads=[pb], writes=[stb_])
                S.dma("sp" if tt % 2 else "act", y_d[tt * 128:(tt + 1) * 128, :], st_[:], reads=[stb_])
            pt, pb = ps()
            S.op("pe", lambda e: e.transpose(pt[:, 0:128], stT[:], ident[:]), reads=[stB, cB], writes=[pb])
            S.op("dve", lambda e: e.tensor_copy(out=stg[0][:, 0:128], in_=pt[:, 0:128]), reads=[pb, stgB[0]], writes=[stgB[0]])
            S.dma("sp", ost_d, stg[0][:, 0:128], reads=[stgB[0]])
            S.barrier()
    return nc


_CACHE = {}


def _consts():
    cR = np.zeros((128, 5 * 128), np.float32)
    cR[:, 0:128] = 1.0
    for b in range(4):
        cR[b * 32:(b + 1) * 32, 128 + b * 32:128 + (b + 1) * 32] = 1.0
    cR[64, 256:256 + 64] = 1.0
    R32 = np.zeros((32, 32), np.float32)
    for m in range(16):
        R32[m + 16, m] = -1.0
        R32[m, m + 16] = 1.0
    for b in range(4):
        cR[b * 32:(b + 1) * 32, 384 + b * 32:384 + (b + 1) * 32] = R32
    cR[64:96, 512 + 64:512 + 96] = R32
    return cR


def _rope_tables():
    T = NT
    rows = T // 64
    row = np.repeat(np.arange(rows), 64).astype(np.float32)
    col = np.tile(np.arange(64), rows).astype(np.float32)
    n = 8
    inv = (10000.0 ** (-np.arange(n, dtype=np.float32) / n)).astype(np.float32)
    ang = np.concatenate([row[:, None] * inv, col[:, None] * inv], axis=-1)
    cos, sin = np.cos(ang).astype(np.float32), np.sin(ang).astype(np.float32)
    idx = np.arange(128) % 16
    return np.ascontiguousarray(cos[:, idx].T), np.ascontiguousarray(sin[:, idx].T)


def kernel(x_prompt, x_sample, cache_diff_k, cache_diff_v, cache_mla_ckv, cache_mla_krope, state_lru, c, c_ctx,
           norm_g, w_ada, b_ada, w_ffn_in, w_ffn_out, w_in, w_out, diff_qk_norm, diff_lambda, diff_subln,
           lru_conv_w, lru_conv_b, lru_w_gate, lru_b_gate, lru_lambda, mla_cq_norm, mla_ckv_norm,
           mla_w_uq, mla_w_ukv, mla_qk_norm):
    f = lambda a: np.ascontiguousarray(np.asarray(a, dtype=np.float32))
    x_prompt, x_sample = f(x_prompt), f(x_sample)
    if "nc" not in _CACHE:
        _CACHE["nc"] = build_program()
    nc = _CACHE["nc"]

    vecs = np.zeros((DEPTH, 128, NV), np.float32)
    p = np.arange(128)
    for l in range(DEPTH):
        vecs[l, :, 0:24] = f(norm_g)[l].reshape(3, 8, 128).transpose(2, 0, 1).reshape(128, 24)
        vecs[l, :, 24:96] = f(b_ada)[l].reshape(72, 128).T
        vecs[l, :, 96] = f(diff_qk_norm)[l, 0][p % 32]
        vecs[l, :, 97] = f(diff_qk_norm)[l, 1][p % 32]
        vecs[l, :, 98] = f(diff_subln)[l][p % 64]
        vecs[l, :, 99:115] = f(lru_conv_w)[l].reshape(4, 4, 128).transpose(2, 0, 1).reshape(128, 16)
        vecs[l, :, 115:119] = f(lru_conv_b)[l].reshape(4, 128).T
        vecs[l, :, 119:135] = f(lru_b_gate)[l].reshape(4, 4, 128).transpose(2, 0, 1).reshape(128, 16)
        vecs[l, :, 135:143] = f(lru_lambda)[l].reshape(2, 4, 128).transpose(2, 0, 1).reshape(128, 8)
        vecs[l, :, 143] = f(mla_cq_norm)[l, 0:128]
        vecs[l, 0:64, 144] = f(mla_cq_norm)[l, 128:192]
        vecs[l, :, 145] = f(mla_ckv_norm)[l]
        vecs[l, 0:96, 146] = f(mla_qk_norm)[l, 0]
        vecs[l, 0:96, 147] = f(mla_qk_norm)[l, 1]
        vecs[l, :, 148:276] = f(diff_lambda)[l].reshape(1, 128)
    cR = _consts()
    ident = np.eye(128, dtype=np.float32)
    cosT, sinT = _rope_tables()
    seg = np.arange(NT) // 256
    mq_p = (seg[None, :] == np.arange(8)[:, None]).astype(np.float32)
    mk_p = np.full((8, NK), -BIG, np.float32)
    mk_p[:, NC_:] = np.where(seg[None, :] == np.arange(8)[:, None], 0.0, -BIG)
    bf = ml_dtypes.bfloat16
    shared = {
        "constsR": cR, "ident": ident, "vecs": vecs,
        "w_ada": f(w_ada), "w_ffn_in": f(w_ffn_in), "w_ffn_out": f(w_ffn_out), "w_in": f(w_in), "w_out": f(w_out),
        "lru_w_gate": f(lru_w_gate), "mla_w_uq": f(mla_w_uq), "mla_w_ukv": f(mla_w_ukv),
    }
    in_maps = []
    for core in range(8):
        m = dict(shared)
        if core < 4:
            m["x"] = x_prompt[core * 8:(core + 1) * 8].reshape(NT, D)
            m["cond"] = np.ascontiguousarray(f(c_ctx).reshape(8, 128).T)
            m["ck"] = np.zeros((DEPTH, NC_, 256), np.float32)
            m["cv"] = np.zeros((DEPTH, NC_, 256), np.float32)
            m["cckv"] = np.zeros((DEPTH, NC_, 128), np.float32)
            m["ckr"] = np.zeros((DEPTH, NC_, 32), np.float32)
            m["lru0"] = np.zeros((DEPTH, 128, 8), np.float32)
            m["keepv"] = np.ascontiguousarray(np.tile(np.array([[0.0, -1.0]], np.float32), (128, 1)))
            m["cosT"] = np.ones((128, NT), np.float32)
            m["sinT"] = np.zeros((128, NT), np.float32)
            m["maskq"] = mq_p.astype(bf)
            m["maskk"] = mk_p.astype(bf)
        else:
            b = core - 4
            m["x"] = x_sample[b]
            m["cond"] = np.ascontiguousarray(f(c)[b].reshape(8, 128).T)
            m["ck"] = f(cache_diff_k)[b].reshape(DEPTH, NC_, 256)
            m["cv"] = f(cache_diff_v)[b].reshape(DEPTH, NC_, 256)
            m["cckv"] = f(cache_mla_ckv)[b]
            m["ckr"] = f(cache_mla_krope)[b]
            m["lru0"] = np.ascontiguousarray(f(state_lru)[b].reshape(DEPTH, 2, 4, 128).transpose(0, 3, 1, 2).reshape(DEPTH, 128, 8))
            m["keepv"] = np.ascontiguousarray(np.tile(np.array([[1.0, 0.0]], np.float32), (128, 1)))
            m["cosT"] = cosT
            m["sinT"] = sinT
            m["maskq"] = np.zeros((8, NT), bf)
            m["maskk"] = np.zeros((8, NK), bf)
        in_maps.append(m)

    if _CACHE.get("prep_only"):
        return in_maps
    res = run_bass_kernel_spmd(nc, in_maps, core_ids=list(range(8)))
    r = res.results
    y_prompt = np.stack([r[i]["y"] for i in range(4)]).reshape(32, 256, D)
    y_sample = np.stack([r[i]["y"] for i in range(4, 8)]).reshape(4, NT, D)

    def gather(name, tail):
        a = np.stack([r[i][name] for i in range(4)])
        a = a.reshape(4, DEPTH, 8, 256, -1).transpose(0, 2, 1, 3, 4).reshape((32, DEPTH, 256) + tail)
        return np.ascontiguousarray(a)

    new_k = gather("o_k", (4, 2, 32))
    new_v = gather("o_v", (4, 64))
    new_ckv = gather("o_ckv", (128,))
    new_kr = gather("o_kr", (32,))
    st = np.stack([r[i]["o_st"] for i in range(4)])
    st = st.reshape(4, DEPTH, 2, 4, 8, 128).transpose(0, 4, 1, 2, 3, 5).reshape(32, DEPTH, 2, 512)
    return (y_prompt.astype(np.float32), y_sample.astype(np.float32), new_k.astype(np.float32), new_v.astype(np.float32),
            new_ckv.astype(np.float32), new_kr.astype(np.float32), np.ascontiguousarray(st).astype(np.float32))
```

```python
import math
import numpy as np
import ml_dtypes
import concourse.bass as bass
import concourse.mybir as mybir
from concourse.bass_utils import run_bass_kernel_spmd

F32 = mybir.dt.float32
F32R = mybir.dt.float32r
BF16 = mybir.dt.bfloat16
AF = mybir.ActivationFunctionType
ALU = mybir.AluOpType
AX = mybir.AxisListType

D = 1024
NT = 2048
NG = 4
NC_ = 512
NK = NT + NC_
NKB = NK // 128
DFF = 2816
NFF = DFF // 128
DEPTH = 2
EPS = 1e-6
BIG = 2048.0
NV = 276


class StopMixer(Exception):
    pass


class Buf:
    __slots__ = ("name", "w", "r", "x")

    def __init__(self, name, x=False):
        self.name = name
        self.w = None
        self.r = []
        self.x = x


class Sync:
    def __init__(self, nc, es):
        self.nc = nc
        self.eng = {"pe": nc.tensor, "act": nc.scalar, "dve": nc.vector, "pool": nc.gpsimd, "sp": nc.sync}
        self.sem = {k: es.enter_context(nc.semaphore("s_" + k)) for k in ("pe", "act", "dve", "pool")}
        self.cnt = {k: 0 for k in self.sem}
        self.pend = {k: False for k in self.sem}
        self.waited = {k: {} for k in self.eng}
        self.nslot = 12
        self.dq = {}
        for q in ("sp", "act", "pool"):
            self.dq[q] = {"sems": [es.enter_context(nc.semaphore("d_%s%d" % (q, i))) for i in range(self.nslot)],
                          "val": [0] * self.nslot, "i": 0}
        self.allsems = {}

    def _wait(self, e, ev):
        if ev is None:
            return
        sem, val = ev
        key = id(sem)
        if self.waited[e].get(key, 0) >= val:
            return
        self.waited[e][key] = val
        self.allsems[key] = sem
        self.eng[e].wait_ge(sem, val)

    def _deps(self, e, reads, writes, acc):
        for b in reads:
            self._wait(e, b.w)
            if b.x:
                for ev in b.r:
                    self._wait(e, ev)
        for b in writes:
            if not (acc and e == "pe"):
                self._wait(e, b.w)
            for ev in b.r:
                self._wait(e, ev)

    def _mark(self, ev, reads, writes):
        for b in writes:
            b.w = ev
            b.r = []
        for b in reads:
            b.r.append(ev)
            if len(b.r) > 24:
                b.r = b.r[-24:]

    mute = False

    def op(self, e, fn, reads=(), writes=(), acc=False, inc=True):
        if self.mute:
            return None
        self._deps(e, reads, writes, acc)
        ins = fn(self.eng[e])
        if inc:
            self.cnt[e] += 1
            ins.then_inc(self.sem[e], 1)
            self.pend[e] = False
            ev = (self.sem[e], self.cnt[e])
        else:
            self.pend[e] = True
            ev = (self.sem[e], self.cnt[e] + 1)
        self._mark(ev, reads, writes)
        return ev

    def dma(self, q, out, in_, reads=(), writes=()):
        if self.mute:
            return None
        if q in ("act", "pool"):
            q = "sp"
        dq = self.dq[q]
        i = dq["i"]
        dq["i"] = (i + 1) % self.nslot
        sem = dq["sems"][i]
        if dq["val"][i]:
            self._wait(q, (sem, dq["val"][i]))
        self._deps(q, reads, writes, False)
        dq["val"][i] += 16
        self.eng[q].dma_start(out=out, in_=in_).then_inc(sem, 16)
        ev = (sem, dq["val"][i])
        self._mark(ev, reads, writes)
        return ev

    def barrier(self):
        if self.mute:
            return
        evs = []
        for k in self.sem:
            assert not self.pend[k]
            if self.cnt[k]:
                evs.append((self.sem[k], self.cnt[k]))
        for q in self.dq.values():
            for s, v in zip(q["sems"], q["val"]):
                if v:
                    evs.append((s, v))
        for e in self.eng:
            for ev in evs:
                self._wait(e, ev)


def build_program():
    nc = bass.Bass("TRN2", target_bir_lowering=False)
    nc.dge_precook = False
    from contextlib import ExitStack

    def din(name, shape, dt=F32):
        return nc.dram_tensor(name, list(shape), dt, kind="ExternalInput").ap()

    def dout(name, shape, dt=F32):
        return nc.dram_tensor(name, list(shape), dt, kind="ExternalOutput").ap()

    x_d = din("x", [NT, D])
    cond_d = din("cond", [128, 8])
    ck_d = din("ck", [DEPTH, NC_, 256])
    cv_d = din("cv", [DEPTH, NC_, 256])
    cckv_d = din("cckv", [DEPTH, NC_, 128])
    ckr_d = din("ckr", [DEPTH, NC_, 32])
    lru0_d = din("lru0", [DEPTH, 128, 8])
    keep_d = din("keepv", [128, 2])
    cos_d = din("cosT", [128, NT])
    sin_d = din("sinT", [128, NT])
    mq_d = din("maskq", [8, NT], BF16)
    mk_d = din("maskk", [8, NK], BF16)
    cR_d = din("constsR", [128, 5 * 128], F32R)
    id_d = din("ident", [128, 128])
    vecs_d = din("vecs", [DEPTH, 128, NV])
    w_ada_d = din("w_ada", [DEPTH, D, 9 * D], F32R)
    w_fi_d = din("w_ffn_in", [DEPTH, 2, D, 2 * DFF], F32R)
    w_fo_d = din("w_ffn_out", [DEPTH, 2, DFF, D], F32R)
    w_in_d = din("w_in", [DEPTH, D, 2144], F32R)
    w_out_d = din("w_out", [DEPTH, D, D], F32R)
    w_gate_d = din("lru_w_gate", [DEPTH, 2, 2, 8, 64, 64], F32R)
    w_uq_d = din("mla_w_uq", [DEPTH, 192, 384], F32R)
    w_ukv_d = din("mla_w_ukv", [DEPTH, 128, 512], F32R)

    y_d = dout("y", [NT, D])
    ok_d = dout("o_k", [DEPTH, NT, 256])
    ov_d = dout("o_v", [DEPTH, NT, 256])
    ockv_d = dout("o_ckv", [DEPTH, NT, 128])
    okr_d = dout("o_kr", [DEPTH, NT, 32])
    ost_d = dout("o_st", [128, 128])

    with ExitStack() as es:
        S = Sync(nc, es)

        uid = [0]

        def sb(stack, name, shape, dt=F32):
            uid[0] += 1
            return stack.enter_context(nc.sbuf_tensor("sb%d_%s" % (uid[0], name), list(shape), dt))

        xT = sb(es, "xT", [128, 8, NT])
        xB = [[Buf("x%d_%d" % (c, g)) for g in range(NG)] for c in range(8)]
        cR = sb(es, "cR", [128, 5 * 128], F32R)
        ident = sb(es, "ident", [128, 128])
        vecs = sb(es, "vecs", [128, DEPTH, NV])
        modv = sb(es, "modv", [128, DEPTH, 72])
        modA = sb(es, "modA", [128, DEPTH, 24])
        modG = sb(es, "modG", [128, DEPTH, 24])
        keepv = sb(es, "keepv", [128, 2])
        lru0 = sb(es, "lru0", [128, DEPTH, 8])
        stT = sb(es, "stT", [128, 128])
        mhalf = sb(es, "mhalf", [128, 512])
        epsc = sb(es, "epsc", [128, 1])
        small = sb(es, "small", [128, 64])
        cB = Buf("consts")
        stB = Buf("stT")
        smB = Buf("small")
        PS = [es.enter_context(nc.psum_tensor("ps%d" % i, [128, 512], F32)) for i in range(8)]
        PSB = [Buf("ps%d" % i, x=True) for i in range(8)]
        psi = [0]

        def ps():
            i = psi[0]
            psi[0] = (i + 1) % 6
            return PS[i], PSB[i]

        ones = cR[:, 0:128]
        bd32 = cR[:, 128:256]
        sel65 = cR[:, 256:384]
        R64 = cR[:, 384:512]
        R96 = cR[:, 512:640]

        S.dma("sp", cR[:], cR_d, writes=[cB])
        S.dma("sp", ident[:], id_d, writes=[cB])
        S.dma("sp", vecs[:], vecs_d.rearrange("l p n -> p l n"), writes=[cB])
        S.dma("sp", keepv[:], keep_d, writes=[cB])
        S.dma("sp", lru0[:], lru0_d.rearrange("l p n -> p l n"), writes=[cB])
        S.op("pool", lambda e: e.memset(mhalf[:], 0.0), writes=[cB])
        S.op("pool", lambda e: e.memset(epsc[:], EPS), writes=[cB])
        S.op("pool", lambda e: e.memset(stT[:], 0.0), writes=[stB])

        def rstd_from_ps(pst, psb, rows, n, dst, dstB, tmp, tmpB, cols=512):
            S.op("act", lambda e: e.activation(out=tmp[0:rows, 0:cols], in_=pst[0:rows, 0:cols], func=AF.Ln, scale=1.0 / n, bias=epsc[0:rows, 0:1]),
                 reads=[psb, cB], writes=[tmpB])
            S.op("act", lambda e: e.activation(out=dst[0:rows, 0:cols], in_=tmp[0:rows, 0:cols], func=AF.Exp, scale=-0.5),
                 reads=[tmpB], writes=[dstB])

        def mm(out, lhsT, rhs, start, stop, reads, writes, lazy=False):
            return S.op("pe", lambda e: e.matmul(out, lhsT, rhs, start=start, stop=stop), reads=reads, writes=writes,
                        acc=not start, inc=(stop if lazy else True))


        def attn_pipe(items, s_mm, pv_mm, PT, PTB, ptc, sc, LA=2):
            q = []
            n = len(items)
            for i in range(n + LA):
                if i < n:
                    pS, pSb = ps()
                    s_mm(items[i], pS, pSb)
                    q.append((pS, pSb))
                j = i - LA
                if j >= 0:
                    pS, pSb = q[j]
                    k_ = ptc[0] % len(PT)
                    ptc[0] += 1
                    P_, PB_ = PT[k_], PTB[k_]
                    S.op("act", lambda e, pS=pS, P_=P_: e.activation(out=P_[:], in_=pS[:], func=AF.Exp, scale=sc), reads=[pSb], writes=[PB_])
                    pv_mm(items[j], P_, PB_)

        with ExitStack() as ph:
            stg = sb(ph, "xstg", [128, 4, D])
            stgB = Buf("xstg")
            for g in range(NG):
                S.dma("sp", stg[:], x_d[g * 512:(g + 1) * 512, :].rearrange("(t p) n -> p t n", p=128), writes=[stgB])
                for c in range(8):
                    pt, pb = ps()
                    for t in range(4):
                        S.op("pe", lambda e, t=t, c=c, pt=pt: e.transpose(pt[:, t * 128:(t + 1) * 128], stg[:, t, c * 128:(c + 1) * 128], ident[:]),
                             reads=[stgB, cB], writes=[pb], acc=(t > 0), inc=(t == 3))
                    S.op("dve" if c % 2 else "act",
                         (lambda e, c=c, pt=pt, g=g: e.tensor_copy(out=xT[:, c, g * 512:(g + 1) * 512], in_=pt[:]))
                         if c % 2 else
                         (lambda e, c=c, pt=pt, g=g: e.copy(out=xT[:, c, g * 512:(g + 1) * 512], in_=pt[:])),
                         reads=[pb], writes=[xB[c][g]])

            cnd = sb(ph, "cnd", [128, 8])
            s2 = sb(ph, "s2", [128, 8, 2], F32R)
            cndB = Buf("cnd")
            S.dma("sp", cnd[:], cond_d, writes=[cndB])
            for j in range(2):
                S.op("act", lambda e, j=j: e.activation(out=s2[:, :, j], in_=cnd[:], func=AF.Silu), reads=[cndB], writes=[smB])
            wad = [sb(ph, "wad%d" % i, [128, 8, 512], F32R) for i in range(2)]
            wadB = [Buf("wad%d" % i) for i in range(2)]
            for l in range(DEPTH):
                pm, pmb = ps()
                wv = w_ada_d[l].rearrange("(k p) n -> p k n", p=128)
                for cg in range(18):
                    wt, wb = wad[cg % 2], wadB[cg % 2]
                    S.dma("sp" if cg % 2 else "act", wt[:], wv[:, :, cg * 512:(cg + 1) * 512], writes=[wb])
                    for jj in range(4):
                        j = cg * 4 + jj
                        for k in range(8):
                            mm(pm[:, 2 * j:2 * j + 2], wt[:, k, jj * 128:(jj + 1) * 128], s2[:, k, :], k == 0, k == 7,
                               [wb, smB], [pmb], lazy=True)
                pmv = pm[:, 0:144].rearrange("p (j t) -> p j t", t=2)
                S.op("dve", lambda e, l=l, pmv=pmv: e.tensor_tensor(out=modv[:, l, :], in0=pmv[:, :, 0], in1=vecs[:, l, 24:96], op=ALU.add),
                     reads=[pmb, cB], writes=[cB])
                for s in range(3):
                    S.op("dve", lambda e, l=l, s=s: e.scalar_tensor_tensor(out=modA[:, l, s * 8:(s + 1) * 8], in0=modv[:, l, (3 * s + 1) * 8:(3 * s + 2) * 8],
                                                                          scalar=1.0, in1=vecs[:, l, s * 8:(s + 1) * 8], op0=ALU.add, op1=ALU.mult),
                         reads=[cB], writes=[cB])
                    S.op("dve", lambda e, l=l, s=s: e.tensor_scalar(out=modG[:, l, s * 8:(s + 1) * 8], in0=modv[:, l, (3 * s + 2) * 8:(3 * s + 3) * 8],
                                                                   scalar1=(1.0 if s == 1 else 0.5), scalar2=None, op0=ALU.mult),
                         reads=[cB], writes=[cB])
            S.barrier()

        def make_h(l, s, g, h_ap, hB, sq, sqB, rs, rsB, tmp, tmpB, tmp2=None):
            pt, pb = ps()
            for c in range(8):
                S.op("act" if c % 2 else "dve",
                     (lambda e, c=c: e.activation(out=sq[:, c % 2, :], in_=xT[:, c, g * 512:(g + 1) * 512], func=AF.Square)) if c % 2 else
                     (lambda e, c=c: e.tensor_tensor(out=sq[:, c % 2, :], in0=xT[:, c, g * 512:(g + 1) * 512], in1=xT[:, c, g * 512:(g + 1) * 512], op=ALU.mult)),
                     reads=[xB[c][g]], writes=[sqB[c % 2]])
                mm(pt[:], ones, sq[:, c % 2, :], c == 0, c == 7, [sqB[c % 2], cB], [pb])
            rstd_from_ps(pt, pb, 128, D, rs, rsB, tmp, tmpB)
            tps = [(tmp, tmpB)] + ([tmp2] if tmp2 is not None else [])
            for c in range(8):
                tq, tqB = tps[c % len(tps)]
                S.op("dve", lambda e, c=c, tq=tq: e.tensor_tensor(out=tq[:, :], in0=xT[:, c, g * 512:(g + 1) * 512], in1=rs[:, :], op=ALU.mult),
                     reads=[xB[c][g], rsB], writes=[tqB])
                S.op("act", lambda e, c=c, tq=tq: e.activation(out=h_ap(c), in_=tq[:, :], func=AF.Identity,
                                                               scale=modA[:, l, s * 8 + c:s * 8 + c + 1], bias=modv[:, l, 3 * s * 8 + c:3 * s * 8 + c + 1]),
                     reads=[tqB, cB], writes=[hB])

        def x_update(l, s, o, g, pt, pb):
            S.op("dve", lambda e: e.scalar_tensor_tensor(out=xT[:, o, g * 512:(g + 1) * 512], in0=pt[:], scalar=modG[:, l, s * 8 + o:s * 8 + o + 1],
                                                         in1=xT[:, o, g * 512:(g + 1) * 512], op0=ALU.mult, op1=ALU.add),
                 reads=[pb, cB], writes=[xB[o][g]])

        def ffn(l, s, fi):
            with ExitStack() as ph:
                h = sb(ph, "f_h", [128, 8, 1024], F32R)
                hB = [Buf("f_h0"), Buf("f_h1")]
                sq = sb(ph, "f_sq", [128, 2, 512], F32R)
                sqB = [Buf("f_sq0"), Buf("f_sq1")]
                rs = sb(ph, "f_rs", [128, 512])
                rsB = Buf("f_rs")
                tmp = sb(ph, "f_tmp", [128, 512])
                tmpB = Buf("f_tmp")
                tmpx = sb(ph, "f_tmpx", [128, 512])
                tmpxB = Buf("f_tmpx")
                NW = 3
                wg = [sb(ph, "f_wg%d" % i, [128, 8, 256], F32R) for i in range(NW)]
                wgB = [Buf("f_wg%d" % i) for i in range(NW)]
                FB = 3
                act = [sb(ph, "f_act%d" % i, [128, FB, 1024], F32R) for i in range(2)]
                actB = [[Buf("f_act%d_%d" % (i, f)) for f in range(FB)] for i in range(2)]
                wo = [sb(ph, "f_wo%d" % i, [128, FB, D], F32R) for i in range(2)]
                woB = [Buf("f_wo%d" % i) for i in range(2)]
                sg = [sb(ph, "f_sg%d" % i, [128, 512]) for i in range(2)]
                sgB = [Buf("f_sg%d" % i) for i in range(2)]
                wiv = w_fi_d[l, fi].rearrange("(k p) n -> p k n", p=128)
                wov = w_fo_d[l, fi].rearrange("(f p) n -> p f n", p=128)
                wcnt = 0
                sgc = 0
                wbc = 0
                for tg in range(2):
                    for hh in range(2):
                        g = tg * 2 + hh
                        make_h(l, s, g, lambda c, hh=hh: h[:, c, hh * 512:(hh + 1) * 512], hB[hh], sq, sqB, rs, rsB, tmp, tmpB, tmp2=(tmpx, tmpxB))
                    blocks = [(b0, min(FB, NFF - b0)) for b0 in range(0, NFF, FB)]
                    for bi, (b0, nb) in enumerate(blocks):
                        ab = bi % 2
                        wb_ = wbc % 2
                        wbc += 1
                        S.dma("pool", wo[wb_][:, 0:nb, :], wov[:, b0:b0 + nb, :], writes=[woB[wb_]])
                        for f in range(nb):
                            wt, wb = wg[wcnt % NW], wgB[wcnt % NW]
                            q = "sp" if wcnt % 2 else "act"
                            wcnt += 1
                            S.dma(q, wt[:, :, 0:128], wiv[:, :, (b0 + f) * 128:(b0 + f + 1) * 128], writes=[wb])
                            S.dma(q, wt[:, :, 128:256], wiv[:, :, DFF + (b0 + f) * 128:DFF + (b0 + f + 1) * 128], writes=[wb])
                            for hh in range(2):
                                pg, pgb = ps()
                                pu, pub = ps()
                                for k in range(8):
                                    mm(pg[:], wt[:, k, 0:128], h[:, k, hh * 512:(hh + 1) * 512], k == 0, k == 7, [wb, hB[hh]], [pgb], lazy=True)
                                for k in range(8):
                                    mm(pu[:], wt[:, k, 128:256], h[:, k, hh * 512:(hh + 1) * 512], k == 0, k == 7, [wb, hB[hh]], [pub], lazy=True)
                                st, stb = sg[sgc % 2], sgB[sgc % 2]
                                sgc += 1
                                S.op("act", lambda e, st=st, pg=pg: e.activation(out=st[:], in_=pg[:], func=AF.Silu), reads=[pgb], writes=[stb])
                                S.op("dve", lambda e, st=st, pu=pu, ab=ab, f=f, hh=hh: e.tensor_tensor(out=act[ab][:, f, hh * 512:(hh + 1) * 512], in0=st[:], in1=pu[:], op=ALU.mult),
                                     reads=[stb, pub], writes=[actB[ab][f]])
                        for o in range(8):
                            for hh in range(2):
                                po, pob = ps()
                                for f in range(nb):
                                    mm(po[:], wo[wb_][:, f, o * 128:(o + 1) * 128], act[ab][:, f, hh * 512:(hh + 1) * 512], f == 0, f == nb - 1,
                                       [woB[wb_], actB[ab][f]], [pob], lazy=True)
                                x_update(l, s, o, tg * 2 + hh, po, pob)
                S.barrier()

        def mixer(l):
            import os
            mixlim = int(os.environ.get("MK_MIX", "99"))
            alim = int(os.environ.get("MK_A", "99"))
            plim = int(os.environ.get("MK_P", "1000000000"))
            pcount = [0]

            def chk():
                pcount[0] += 1
                if pcount[0] == plim:
                    S.barrier()
                    S.mute = True
            with ExitStack() as ph:
                tabB = Buf("tab")
                hall = sb(ph, "m_hall", [128, 8, NT], BF16)
                hallB = [Buf("m_hall%d" % g) for g in range(NG)]
                hb_ = [None]
                hbB = [Buf("m_h0")]
                sq = sb(ph, "m_sq", [128, 2, 512], F32R)
                sqB = [Buf("m_sq0"), Buf("m_sq1")]
                tmp = sb(ph, "m_tmp", [128, 512])
                tmpB = Buf("m_tmp")
                tm2 = sb(ph, "m_tm2", [128, 512])
                tm2B = Buf("m_tm2")
                wp = [sb(ph, "m_wp%d" % i, [128, 8, 128], F32R) for i in range(2)]
                wpB = [Buf("m_wp%d" % i) for i in range(2)]
                wo = sb(ph, "m_wo", [128, D], F32R)
                woB = Buf("m_wo")
                lv = sb(ph, "m_lv", [128, 16])
                lvB = Buf("m_lv")
                wiv = w_in_d[l].rearrange("(k p) n -> p k n", p=128)

                for g in range(NG):
                    pt, pb = ps()
                    for c in range(8):
                        S.op("act" if c % 2 else "dve",
                             (lambda e, c=c, g=g: e.activation(out=sq[:, c % 2, :], in_=xT[:, c, g * 512:(g + 1) * 512], func=AF.Square)) if c % 2 else
                             (lambda e, c=c, g=g: e.tensor_tensor(out=sq[:, c % 2, :], in0=xT[:, c, g * 512:(g + 1) * 512], in1=xT[:, c, g * 512:(g + 1) * 512], op=ALU.mult)),
                             reads=[xB[c][g]], writes=[sqB[c % 2]])
                        mm(pt[:], ones, sq[:, c % 2, :], c == 0, c == 7, [sqB[c % 2], cB], [pb])
                    rstd_from_ps(pt, pb, 128, D, tm2, tm2B, tmp, tmpB)
                    for c in range(8):
                        S.op("dve", lambda e, c=c, g=g: e.tensor_tensor(out=tmp[:, :], in0=xT[:, c, g * 512:(g + 1) * 512], in1=tm2[:, :], op=ALU.mult),
                             reads=[xB[c][g], tm2B], writes=[tmpB])
                        S.op("act", lambda e, c=c, g=g: e.activation(out=hall[:, c, g * 512:(g + 1) * 512], in_=tmp[:, :], func=AF.Identity,
                                                                     scale=modA[:, l, 8 + c:8 + c + 1], bias=modv[:, l, 24 + c:24 + c + 1]),
                             reads=[tmpB, cB], writes=[hallB[g]])

                hcB = [Buf("m_hc%d" % c) for c in range(8)]

                def get_h(g):
                    ht = hb_[0]
                    for c in range(8):
                        if c % 2:
                            S.op("act", lambda e, c=c: e.copy(out=ht[:, c, :], in_=hall[:, c, g * 512:(g + 1) * 512]), reads=[hallB[g]], writes=[hcB[c]])
                        else:
                            S.op("dve", lambda e, c=c: e.tensor_copy(out=ht[:, c, :], in_=hall[:, c, g * 512:(g + 1) * 512]), reads=[hallB[g]], writes=[hcB[c]])
                    return ht, hcB

                def load_wp(i, col0, ncol, q="sp"):
                    S.dma(q, wp[i][:, :, 0:ncol], wiv[:, :, col0:col0 + ncol], writes=[wpB[i]])

                def proj(i, ncol, ht, hB):
                    pt, pb = ps()
                    for k in range(8):
                        mm(pt[0:ncol, :], wp[i][:, k, 0:ncol], ht[:, k, :], k == 0, k == 7, [wpB[i], hB[k]], [pb], lazy=True)
                    return pt, pb

                def out_proj(rows, src_ap, srcB, g):
                    for o in range(8):
                        po, pob = ps()
                        mm(po[:], wo[0:rows, o * 128:(o + 1) * 128], src_ap, True, True, [woB, srcB], [pob])
                        x_update(l, 1, o, g, po, pob)

                if mixlim < 1:
                    S.barrier()
                    return
                lp = vecs[:, l, 148:276]
                S.op("dve", lambda e: e.tensor_tensor(out=tmp[:, 0:32], in0=lp[:, 0:32], in1=lp[:, 32:64], op=ALU.mult), reads=[cB], writes=[tmpB])
                S.op("dve", lambda e: e.reduce_sum(out=lv[:, 0:1], in_=tmp[:, 0:32], axis=AX.X), reads=[tmpB], writes=[lvB])
                S.op("dve", lambda e: e.tensor_tensor(out=tmp[:, 0:32], in0=lp[:, 64:96], in1=lp[:, 96:128], op=ALU.mult), reads=[cB, lvB], writes=[tmpB])
                S.op("dve", lambda e: e.reduce_sum(out=lv[:, 1:2], in_=tmp[:, 0:32], axis=AX.X), reads=[tmpB], writes=[lvB])
                S.op("act", lambda e: e.activation(out=lv[:, 2:4], in_=lv[:, 0:2], func=AF.Exp), reads=[lvB], writes=[lvB])
                lam_init = 0.8 - 0.6 * math.exp(-0.3 * l)
                S.op("dve", lambda e: e.scalar_tensor_tensor(out=lv[:, 4:5], in0=lv[:, 3:4], scalar=-lam_init, in1=lv[:, 2:3], op0=ALU.add, op1=ALU.subtract),
                     reads=[lvB], writes=[lvB])
                S.op("dve", lambda e: e.tensor_scalar(out=lv[:, 5:6], in0=vecs[:, l, 98:99], scalar1=1.0 - lam_init, scalar2=None, op0=ALU.mult),
                     reads=[cB, lvB], writes=[lvB])
                S.op("act", lambda e: e.activation(out=lv[:, 8:16], in_=vecs[:, l, 135:143], func=AF.Exp, scale=-1.0), reads=[cB, lvB], writes=[lvB])
                S.op("act", lambda e: e.activation(out=lv[:, 8:16], in_=lv[:, 8:16], func=AF.Ln, bias=1.0), reads=[lvB], writes=[lvB])
                S.op("dve", lambda e: e.tensor_scalar(out=lv[:, 8:16], in0=lv[:, 8:16], scalar1=-8.0, scalar2=None, op0=ALU.mult), reads=[lvB], writes=[lvB])

                with ExitStack() as pc:
                    cosT = sb(pc, "cosT", [128, NT])
                    sinT = sb(pc, "sinT", [128, NT])
                    S.dma("sp", cosT[:], cos_d, writes=[tabB])
                    S.dma("sp", sinT[:], sin_d, writes=[tabB])
                    cq0 = sb(pc, "c_cq0", [128, NT], F32R)
                    cq1 = sb(pc, "c_cq1", [128, NK], F32R)
                    ckv = sb(pc, "c_ckv", [128, NK], F32R)
                    cq0B, cq1B, krB, ckvB = Buf("c_cq0"), Buf("c_cq1"), Buf("c_kr"), Buf("c_ckv")
                    wuq = sb(pc, "c_wuq", [128, 2, 384], F32R)
                    wukv = sb(pc, "c_wukv", [128, 512], F32R)
                    wuB = Buf("c_wu")
                    pc0 = ExitStack()
                    hb_[0] = sb(pc0, "c_h", [128, 8, 512], F32R)
                    cst = sb(pc0, "c_cst", [128, 4, 160])
                    cstB = Buf("c_cst")
                    ost = sb(pc0, "c_ost", [128, 4, 160])
                    ostB = Buf("c_ost")
                    S.dma("sp", wuq[:, 0, :], w_uq_d[l, 0:128, :], writes=[wuB])
                    S.dma("sp", wuq[0:64, 1, :], w_uq_d[l, 128:192, :], writes=[wuB])
                    S.dma("sp", wukv[:], w_ukv_d[l], writes=[wuB])
                    S.dma("sp", cst[:, :, 0:128], cckv_d[l].rearrange("(t p) n -> p t n", p=128), writes=[cstB])
                    S.dma("sp", cst[:, :, 128:160], ckr_d[l].rearrange("(t p) n -> p t n", p=128), writes=[cstB])
                    pt, pb = ps()
                    for t in range(4):
                        S.op("pe", lambda e, t=t, pt=pt: e.transpose(pt[:, t * 128:(t + 1) * 128], cst[:, t, 0:128], ident[:]), reads=[cstB, cB], writes=[pb], acc=(t > 0), inc=(t == 3))
                    S.op("act", lambda e, pt=pt: e.copy(out=ckv[:, 0:NC_], in_=pt[:]), reads=[pb], writes=[ckvB])
                    pt, pb = ps()
                    for t in range(4):
                        S.op("pe", lambda e, t=t, pt=pt: e.transpose(pt[0:32, t * 128:(t + 1) * 128], cst[:, t, 128:160], ident[:]), reads=[cstB, cB], writes=[pb], acc=(t > 0), inc=(t == 3))
                    S.op("act", lambda e, pt=pt: e.copy(out=cq1[64:96, 0:NC_], in_=pt[0:32, :]), reads=[pb], writes=[krB])
                    for g in range(NG):
                        sl = slice(g * 512, (g + 1) * 512)
                        ksl = slice(NC_ + g * 512, NC_ + (g + 1) * 512)
                        ht, hB = get_h(g)
                        load_wp(0, 1792, 128, "sp")
                        load_wp(1, 1920, 64, "act")
                        p0, p0b = proj(0, 128, ht, hB)
                        p1, p1b = proj(1, 64, ht, hB)
                        S.op("act", lambda e, p0=p0: e.activation(out=sq[:, 0, :], in_=p0[:], func=AF.Square), reads=[p0b], writes=[sqB[0]])
                        S.op("act", lambda e, p1=p1: e.activation(out=sq[0:64, 1, :], in_=p1[0:64, :], func=AF.Square), reads=[p1b], writes=[sqB[1]])
                        p2, p2b = ps()
                        mm(p2[:], ones[:, :], sq[:, 0, :], True, False, [sqB[0], cB], [p2b])
                        mm(p2[:], ones[0:64, :], sq[0:64, 1, :], False, True, [sqB[1], cB], [p2b])
                        rstd_from_ps(p2, p2b, 128, 192, tm2, tm2B, tmp, tmpB)
                        S.op("dve", lambda e, p0=p0: e.tensor_tensor(out=tmp[:, :], in0=p0[:], in1=tm2[:, :], op=ALU.mult), reads=[p0b, tm2B], writes=[tmpB])
                        S.op("act", lambda e, sl=sl: e.activation(out=cq0[:, sl], in_=tmp[:, :], func=AF.Identity, scale=vecs[:, l, 143:144]), reads=[tmpB, cB], writes=[cq0B])
                        S.op("dve", lambda e, p1=p1: e.tensor_tensor(out=tmp[0:64, :], in0=p1[0:64, :], in1=tm2[0:64, :], op=ALU.mult), reads=[p1b, tm2B], writes=[tmpB])
                        S.op("act", lambda e, sl=sl: e.activation(out=cq1[0:64, sl], in_=tmp[0:64, :], func=AF.Identity, scale=vecs[0:64, l, 144:145]), reads=[tmpB, cB], writes=[cq1B])
                        load_wp(0, 1984, 128, "sp")
                        load_wp(1, 2112, 32, "act")
                        p0, p0b = proj(0, 128, ht, hB)
                        p1, p1b = proj(1, 32, ht, hB)
                        S.op("act", lambda e, p0=p0: e.activation(out=sq[:, 0, :], in_=p0[:], func=AF.Square), reads=[p0b], writes=[sqB[0]])
                        p2, p2b = ps()
                        mm(p2[:], ones[:, :], sq[:, 0, :], True, True, [sqB[0], cB], [p2b])
                        rstd_from_ps(p2, p2b, 128, 128, tm2, tm2B, tmp, tmpB)
                        S.op("dve", lambda e, p0=p0: e.tensor_tensor(out=tmp[:, :], in0=p0[:], in1=tm2[:, :], op=ALU.mult), reads=[p0b, tm2B], writes=[tmpB])
                        S.op("act", lambda e, ksl=ksl: e.activation(out=ckv[:, ksl], in_=tmp[:, :], func=AF.Identity, scale=vecs[:, l, 145:146]), reads=[tmpB, cB], writes=[ckvB])
                        S.op("act", lambda e, p1=p1, ksl=ksl: e.copy(out=cq1[64:96, ksl], in_=p1[0:32, :]), reads=[p1b], writes=[krB])
                        p3, p3b = ps()
                        for t in range(4):
                            S.op("pe", lambda e, t=t, p3=p3, ksl=ksl: e.transpose(p3[:, t * 128:(t + 1) * 128], ckv[:, ksl.start + t * 128:ksl.start + (t + 1) * 128].bitcast(F32), ident[:]),
                                 reads=[ckvB, cB], writes=[p3b], acc=(t > 0), inc=(t == 3))
                        S.op("dve", lambda e, p3=p3: e.tensor_copy(out=ost[:, :, 0:128], in_=p3[:].rearrange("p (t n) -> p t n", n=128)), reads=[p3b], writes=[ostB])
                        p4, p4b = ps()
                        for t in range(4):
                            S.op("pe", lambda e, t=t, p4=p4, ksl=ksl: e.transpose(p4[:, t * 32:(t + 1) * 32], cq1[64:96, ksl.start + t * 128:ksl.start + (t + 1) * 128].bitcast(F32), ident[64:96, 64:96]),
                                 reads=[krB, cB], writes=[p4b], acc=(t > 0), inc=(t == 3))
                        S.op("dve", lambda e, p4=p4: e.tensor_copy(out=ost[:, :, 128:160], in_=p4[:, 0:128].rearrange("p (t n) -> p t n", n=32)), reads=[p4b, ostB], writes=[ostB])
                        S.dma("pool", ockv_d[l, sl, :].rearrange("(t p) n -> p t n", p=128), ost[:, :, 0:128], reads=[ostB])
                        S.dma("pool", okr_d[l, sl, :].rearrange("(t p) n -> p t n", p=128), ost[:, :, 128:160], reads=[ostB])
                    S.barrier()
                    pc0.close()
                    if mixlim < 2:
                        return
                    Qm = sb(pc, "c_Q", [104, NT], BF16)
                    Km = sb(pc, "c_K", [104, NK], BF16)
                    Vm = sb(pc, "c_V", [128, NKB, 65], BF16)
                    QB, KB, VB = Buf("c_Q"), Buf("c_K"), Buf("c_V")
                    PT = [sb(pc, "c_PT%d" % i, [128, 512], BF16) for i in range(4)]
                    PTB = [Buf("c_PT%d" % i) for i in range(4)]
                    osb = sb(pc, "c_osb", [65, 512], F32R)
                    osbB = Buf("c_osb")
                    oc = sb(pc, "c_oc", [64, 512], F32R)
                    ocB = Buf("c_oc")
                    qn = sb(pc, "c_qn", [96, 512], F32R)
                    qnB = Buf("c_qn")
                    rs = sb(pc, "c_rs", [96, 512])
                    rsB = Buf("c_rs")
                    S.dma("sp", Qm[96:104, :], mq_d, writes=[QB])
                    S.dma("sp", Km[96:104, :], mk_d, writes=[KB])
                    S.op("pool", lambda e: e.memset(Vm[:, :, 64:65], 1.0), writes=[VB])
                    sc = 96 ** -0.5
                    ptc = [0]
                    for hd in range(4):
                        S.dma("pool", wo[0:64, :], w_out_d[l, 768 + hd * 64:768 + (hd + 1) * 64, :], writes=[woB])
                        for kg in range(NK // 512):
                            ksl = slice(kg * 512, (kg + 1) * 512)
                            pk, pkb = ps()
                            mm(pk[0:64, :], wukv[:, hd * 128:hd * 128 + 64], ckv[:, ksl], True, True, [wuB, ckvB], [pkb])
                            S.op("act", lambda e, pk=pk: e.activation(out=sq[0:64, 0, :], in_=pk[0:64, :], func=AF.Square), reads=[pkb], writes=[sqB[0]])
                            S.op("dve", lambda e, ksl=ksl: e.tensor_tensor(out=sq[0:32, 1, :], in0=cq1[64:96, ksl].bitcast(F32), in1=cq1[64:96, ksl].bitcast(F32), op=ALU.mult),
                                 reads=[krB], writes=[sqB[1]])
                            p2, p2b = ps()
                            mm(p2[0:96, :], ones[0:64, 0:96], sq[0:64, 0, :], True, False, [sqB[0], cB], [p2b])
                            mm(p2[0:96, :], ones[0:32, 0:96], sq[0:32, 1, :], False, True, [sqB[1], cB], [p2b])
                            rstd_from_ps(p2, p2b, 96, 96, rs, rsB, tm2, tm2B)
                            S.op("dve", lambda e, pk=pk: e.tensor_tensor(out=tmp[0:64, :], in0=pk[0:64, :], in1=rs[0:64, :], op=ALU.mult), reads=[pkb, rsB], writes=[tmpB])
                            S.op("dve", lambda e, ksl=ksl: e.tensor_tensor(out=tmp[64:96, :], in0=cq1[64:96, ksl].bitcast(F32), in1=rs[64:96, :], op=ALU.mult), reads=[krB, rsB], writes=[tmpB])
                            if kg == 0:
                                S.op("act", lambda e, ksl=ksl: e.activation(out=Km[0:96, ksl], in_=tmp[0:96, :], func=AF.Identity, scale=vecs[0:96, l, 147:148]), reads=[tmpB, cB], writes=[KB])
                            else:
                                g = kg - 1
                                S.op("act", lambda e: e.activation(out=qn[:, :], in_=tmp[0:96, :], func=AF.Identity, scale=vecs[0:96, l, 147:148]), reads=[tmpB, cB], writes=[qnB])
                                p4, p4b = ps()
                                mm(p4[0:96, :], R96[0:96, 0:96], qn[:, :], True, True, [qnB, cB], [p4b])
                                S.op("act", lambda e, ksl=ksl: e.copy(out=Km[0:64, ksl], in_=qn[0:64, :].bitcast(F32)), reads=[qnB], writes=[KB])
                                S.op("dve", lambda e, g=g: e.tensor_tensor(out=tmp[64:96, :], in0=qn[64:96, :].bitcast(F32), in1=cosT[64:96, g * 512:(g + 1) * 512], op=ALU.mult),
                                     reads=[qnB, tabB], writes=[tmpB])
                                S.op("dve", lambda e, p4=p4, g=g: e.tensor_tensor(out=tm2[64:96, :], in0=p4[64:96, :], in1=sinT[64:96, g * 512:(g + 1) * 512], op=ALU.mult),
                                     reads=[p4b, tabB], writes=[tm2B])
                                S.op("dve", lambda e, ksl=ksl: e.tensor_tensor(out=Km[64:96, ksl], in0=tmp[64:96, :], in1=tm2[64:96, :], op=ALU.add), reads=[tmpB, tm2B], writes=[KB])
                            pv, pvb = ps()
                            for t in range(4):
                                mm(pv[:, t * 64:(t + 1) * 64], ckv[:, kg * 512 + t * 128:kg * 512 + (t + 1) * 128], wukv[:, hd * 128 + 64:hd * 128 + 128], True, True, [wuB, ckvB], [pvb])
                            S.op("act", lambda e, pv=pv, kg=kg: e.copy(out=Vm[:, kg * 4:kg * 4 + 4, 0:64], in_=pv[:, 0:256].rearrange("p (t n) -> p t n", n=64)), reads=[pvb], writes=[VB])
                        for g in range(NG):
                            sl = slice(g * 512, (g + 1) * 512)
                            pq, pqb = ps()
                            mm(pq[0:96, :], wuq[:, 0, hd * 96:(hd + 1) * 96], cq0[:, sl], True, False, [wuB, cq0B], [pqb])
                            mm(pq[0:96, :], wuq[0:64, 1, hd * 96:(hd + 1) * 96], cq1[0:64, sl], False, True, [wuB, cq1B], [pqb])
                            S.op("act", lambda e, pq=pq: e.activation(out=sq[0:96, 0, :], in_=pq[0:96, :], func=AF.Square), reads=[pqb], writes=[sqB[0]])
                            p2, p2b = ps()
                            mm(p2[0:96, :], ones[0:96, 0:96], sq[0:96, 0, :], True, True, [sqB[0], cB], [p2b])
                            rstd_from_ps(p2, p2b, 96, 96, rs, rsB, tm2, tm2B)
                            S.op("dve", lambda e, pq=pq: e.scalar_tensor_tensor(out=qn[:, :], in0=pq[0:96, :], scalar=vecs[0:96, l, 146:147], in1=rs[0:96, :], op0=ALU.mult, op1=ALU.mult),
                                 reads=[pqb, rsB, cB], writes=[qnB])
                            p4, p4b = ps()
                            mm(p4[0:96, :], R96[0:96, 0:96], qn[:, :], True, True, [qnB, cB], [p4b])
                            S.op("act", lambda e, sl=sl: e.copy(out=Qm[0:64, sl], in_=qn[0:64, :].bitcast(F32)), reads=[qnB], writes=[QB])
                            S.op("dve", lambda e, sl=sl: e.tensor_tensor(out=tmp[64:96, :], in0=qn[64:96, :].bitcast(F32), in1=cosT[64:96, sl], op=ALU.mult), reads=[qnB, tabB], writes=[tmpB])
                            S.op("dve", lambda e, p4=p4, sl=sl: e.tensor_tensor(out=tm2[64:96, :], in0=p4[64:96, :], in1=sinT[64:96, sl], op=ALU.mult), reads=[p4b, tabB], writes=[tm2B])
                            S.op("dve", lambda e, sl=sl: e.tensor_tensor(out=Qm[64:96, sl], in0=tmp[64:96, :], in1=tm2[64:96, :], op=ALU.add), reads=[tmpB, tm2B], writes=[QB])
                        for g in range(NG):
                            O, OB = PS[6], PSB[6]

                            def s_mm(kb, pS, pSb, g=g):
                                S.op("pe", lambda e: e.matmul(pS[:], Km[0:104, kb * 128:(kb + 1) * 128], Qm[0:104, g * 512:(g + 1) * 512], start=True, stop=True),
                                     reads=[KB, QB], writes=[pSb])

                            def pv_mm(kb, P_, PB_):
                                mm(PS[6][0:65, :], Vm[:, kb, :], P_[:], kb == 0, kb == NKB - 1, [VB, PB_], [PSB[6]])

                            attn_pipe(list(range(NKB)), s_mm, pv_mm, PT, PTB, ptc, sc, LA=3)
                            S.op("act", lambda e, O=O: e.copy(out=osb[:, :], in_=O[0:65, :]), reads=[OB], writes=[osbB])
                            pd, pdb = ps()
                            mm(pd[0:64, :], sel65[0:65, 0:64], osb[:, :], True, True, [osbB, cB], [pdb])
                            S.op("act", lambda e, pd=pd: e.activation(out=tm2[0:64, :], in_=pd[0:64, :], func=AF.Ln), reads=[pdb], writes=[tm2B])
                            S.op("act", lambda e: e.activation(out=tm2[0:64, :], in_=tm2[0:64, :], func=AF.Exp, scale=-1.0), reads=[tm2B], writes=[tm2B])
                            S.op("dve", lambda e: e.tensor_tensor(out=oc[:, :], in0=osb[0:64, :].bitcast(F32), in1=tm2[0:64, :], op=ALU.mult), reads=[osbB, tm2B], writes=[ocB])
                            out_proj(64, oc[:, :], ocB, g)
                    S.barrier()

                if mixlim < 3:
                    return
                with ExitStack() as pa:
                    cosT = sb(pa, "cosT", [128, NT])
                    sinT = sb(pa, "sinT", [128, NT])
                    S.dma("sp", cosT[:], cos_d, writes=[tabB])
                    S.dma("sp", sinT[:], sin_d, writes=[tabB])
                    hb_[0] = sb(pa, "a_h", [128, 8, 512], F32R)
                    Qd = sb(pa, "a_Q", [72, 2, NT], BF16)
                    Kd = sb(pa, "a_K", [72, 2, NK], BF16)
                    Vd = sb(pa, "a_V", [128, NKB, 65], BF16)
                    QB, KB, VB = Buf("a_Q"), Buf("a_K"), Buf("a_V")
                    PT = [sb(pa, "a_PT%d" % i, [128, 512], BF16) for i in range(4)]
                    PTB = [Buf("a_PT%d" % i) for i in range(4)]
                    osb = sb(pa, "a_osb", [65, 2, 512], F32R)
                    osbB = Buf("a_osb")
                    on = sb(pa, "a_on", [64, 2, 512])
                    onB = Buf("a_on")
                    oa = sb(pa, "a_oa", [64, 512], F32R)
                    oaB = Buf("a_oa")
                    kn = sb(pa, "a_kn", [64, 512], F32R)
                    knB = Buf("a_kn")
                    rs = sb(pa, "a_rs", [64, 512])
                    rsB = Buf("a_rs")
                    kn1 = sb(pa, "a_kn1", [64, 512], F32R)
                    rs1 = sb(pa, "a_rs1", [64, 512])
                    tmpb = sb(pa, "a_tmpb", [64, 512])
                    tm2b = sb(pa, "a_tm2b", [64, 512])
                    knw, knwB = [kn, kn1], [knB, Buf("a_kn1")]
                    rsw, rswB = [rs, rs1], [rsB, Buf("a_rs1")]
                    tw1, tw1B = [tmp, tmpb], [tmpB, Buf("a_tmpb")]
                    tw2, tw2B = [tm2, tm2b], [tm2B, Buf("a_tm2b")]
                    cst = sb(pa, "a_cst", [128, 4, 64])
                    cstB = Buf("a_cst")
                    ost = sb(pa, "a_ost", [128, 4, 64])
                    ostB = Buf("a_ost")
                    S.op("pool", lambda e: e.memset(Qd[:], 0.0), writes=[QB])
                    S.op("pool", lambda e: e.memset(Kd[:], 0.0), writes=[KB])
                    for c in range(2):
                        S.dma("sp", Qd[32:40, c, :], mq_d, writes=[QB])
                        S.dma("sp", Kd[32:40, c, :], mk_d, writes=[KB])
                    S.op("pool", lambda e: e.memset(Vd[:, :, 64:65], 1.0), writes=[VB])
                    ptc = [0]
                    for hd in range(4):
                        load_wp(0, 0 + hd * 64, 64, "sp")
                        load_wp(1, 256 + hd * 64, 64, "act")
                        S.dma("pool", wo[0:64, :], w_out_d[l, hd * 64:(hd + 1) * 64, :], writes=[woB])
                        S.dma("sp", cst[:], ck_d[l, :, hd * 64:(hd + 1) * 64].rearrange("(t p) n -> p t n", p=128), writes=[cstB])
                        pt, pb = ps()
                        for t in range(4):
                            S.op("pe", lambda e, t=t, pt=pt: e.transpose(pt[0:64, t * 128:(t + 1) * 128], cst[:, t, :], ident[:]),
                                 reads=[cstB, cB], writes=[pb], acc=(t > 0), inc=(t == 3))
                        for c in range(2):
                            S.op("dve", lambda e, c=c, pt=pt: e.tensor_copy(out=Kd[0:32, c, 0:NC_], in_=pt[c * 32:(c + 1) * 32, :]), reads=[pb], writes=[KB])
                        S.dma("sp", cst[:], cv_d[l, :, hd * 64:(hd + 1) * 64].rearrange("(t p) n -> p t n", p=128), reads=[], writes=[cstB])
                        S.op("dve", lambda e: e.tensor_copy(out=Vd[:, 0:4, 0:64], in_=cst[:]), reads=[cstB], writes=[VB])
                        if alim < 1:
                            S.barrier()
                            return
                        for g in range(NG):
                            ht, hB = get_h(g)
                            W2 = (0, 1)
                            pts = [proj(w, 64, ht, hB) for w in W2]
                            for w in W2:
                                S.op("act", lambda e, w=w: e.activation(out=sq[0:64, w, :], in_=pts[w][0][0:64, :], func=AF.Square), reads=[pts[w][1]], writes=[sqB[w]])
                            p2s = []
                            for w in W2:
                                p2, p2b = ps()
                                mm(p2[0:64, :], bd32[0:64, 0:64], sq[0:64, w, :], True, True, [sqB[w], cB], [p2b])
                                p2s.append((p2, p2b))
                            for w in W2:
                                S.op("act", lambda e, w=w: e.activation(out=tw2[w][0:64, :], in_=p2s[w][0][0:64, :], func=AF.Ln, scale=1.0 / 32, bias=epsc[0:64, 0:1]),
                                     reads=[p2s[w][1], cB], writes=[tw2B[w]])
                            for w in W2:
                                S.op("act", lambda e, w=w: e.activation(out=rsw[w][:, :], in_=tw2[w][0:64, :], func=AF.Exp, scale=-0.5), reads=[tw2B[w]], writes=[rswB[w]])
                            for w in W2:
                                S.op("dve", lambda e, w=w: e.scalar_tensor_tensor(out=knw[w][:, :], in0=pts[w][0][0:64, :], scalar=vecs[0:64, l, 96 + w:97 + w], in1=rsw[w][:, :],
                                                                                  op0=ALU.mult, op1=ALU.mult),
                                     reads=[pts[w][1], rswB[w], cB], writes=[knwB[w]])
                            p4s = []
                            for w in W2:
                                p4, p4b = ps()
                                mm(p4[0:64, :], R64[0:64, 0:64], knw[w][:, :], True, True, [knwB[w], cB], [p4b])
                                p4s.append((p4, p4b))
                            p3, p3b = ps()
                            for t in range(4):
                                S.op("pe", lambda e, t=t, p3=p3: e.transpose(p3[:, t * 64:(t + 1) * 64], knw[1][:, t * 128:(t + 1) * 128].bitcast(F32), ident[0:64, 0:64]),
                                     reads=[knwB[1], cB], writes=[p3b], acc=(t > 0), inc=(t == 3))
                            S.op("dve", lambda e, p3=p3: e.tensor_copy(out=ost[:].rearrange("p t n -> p (t n)"), in_=p3[:, 0:256]), reads=[p3b], writes=[ostB])
                            S.dma("pool", ok_d[l, g * 512:(g + 1) * 512, hd * 64:(hd + 1) * 64].rearrange("(t p) n -> p t n", p=128), ost[:], reads=[ostB])
                            for w in W2:
                                S.op("dve", lambda e, w=w: e.tensor_tensor(out=tw1[w][0:64, :], in0=knw[w][:, :].bitcast(F32), in1=cosT[0:64, g * 512:(g + 1) * 512], op=ALU.mult),
                                     reads=[knwB[w], tabB], writes=[tw1B[w]])
                            for w in W2:
                                S.op("dve", lambda e, w=w: e.tensor_tensor(out=tw2[w][0:64, :], in0=p4s[w][0][0:64, :], in1=sinT[0:64, g * 512:(g + 1) * 512], op=ALU.mult),
                                     reads=[p4s[w][1], tabB], writes=[tw2B[w]])
                            for w in W2:
                                for c in range(2):
                                    if w == 0:
                                        dst = Qd[0:32, c, g * 512:(g + 1) * 512]
                                    else:
                                        dst = Kd[0:32, c, NC_ + g * 512:NC_ + (g + 1) * 512]
                                    S.op("dve", lambda e, c=c, dst=dst, w=w: e.tensor_tensor(out=dst, in0=tw1[w][c * 32:(c + 1) * 32, :], in1=tw2[w][c * 32:(c + 1) * 32, :], op=ALU.add),
                                         reads=[tw1B[w], tw2B[w]], writes=[QB if w == 0 else KB])
                            chk()
                            S.dma("sp", wp[0][:, :, 64:128], wiv[:, :, 512 + hd * 64:512 + (hd + 1) * 64], writes=[wpB[0]]) if g == 0 else None
                            pv, pvb = ps()
                            for t in range(4):
                                for k in range(8):
                                    mm(pv[:, t * 64:(t + 1) * 64], ht[:, k, t * 128:(t + 1) * 128], wp[0][:, k, 64:128], k == 0, k == 7, [wpB[0], hB[k]], [pvb])
                            S.op("act", lambda e, pv=pv, g=g: e.copy(out=Vd[:, 4 + g * 4:8 + g * 4, 0:64], in_=pv[:, 0:256].rearrange("p (t n) -> p t n", n=64)),
                                 reads=[pvb], writes=[VB])
                            S.op("dve", lambda e, pv=pv: e.tensor_copy(out=ost[:].rearrange("p t n -> p (t n)"), in_=pv[:, 0:256]), reads=[pvb], writes=[ostB])
                            S.dma("pool", ov_d[l, g * 512:(g + 1) * 512, hd * 64:(hd + 1) * 64].rearrange("(t p) n -> p t n", p=128), ost[:], reads=[ostB])
                            chk()
                        if alim < 2:
                            S.barrier()
                            return
                        sc = 32 ** -0.5
                        for g in range(NG):
                            items = [(kb, c) for kb in range(NKB) for c in range(2)]

                            def s_mm(it, pS, pSb, g=g):
                                kb, c = it
                                S.op("pe", lambda e: e.matmul(pS[:], Kd[0:72, c, kb * 128:(kb + 1) * 128], Qd[0:72, c, g * 512:(g + 1) * 512], start=True, stop=True),
                                     reads=[KB, QB], writes=[pSb])

                            def pv_mm(it, P_, PB_):
                                kb, c = it
                                mm(PS[6 + c][0:65, :], Vd[:, kb, :], P_[:], kb == 0, kb == NKB - 1, [VB, PB_], [PSB[6 + c]])

                            attn_pipe(items, s_mm, pv_mm, PT, PTB, ptc, sc, LA=3)
                            for c in range(2):
                                O, OB = PS[6 + c], PSB[6 + c]
                                S.op("act", lambda e, c=c, O=O: e.copy(out=osb[:, c, :], in_=O[0:65, :]), reads=[OB], writes=[osbB])
                                pd, pdb = ps()
                                mm(pd[0:64, :], sel65[0:65, 0:64], osb[:, c, :], True, True, [osbB, cB], [pdb])
                                S.op("act", lambda e, pd=pd: e.activation(out=tm2[0:64, :], in_=pd[0:64, :], func=AF.Ln), reads=[pdb], writes=[tm2B])
                                S.op("act", lambda e: e.activation(out=tm2[0:64, :], in_=tm2[0:64, :], func=AF.Exp, scale=-1.0), reads=[tm2B], writes=[tm2B])
                                S.op("dve", lambda e, c=c: e.tensor_tensor(out=on[:, c, :], in0=osb[0:64, c, :].bitcast(F32), in1=tm2[0:64, :], op=ALU.mult),
                                     reads=[osbB, tm2B], writes=[onB])
                            S.op("dve", lambda e: e.scalar_tensor_tensor(out=on[:, 0, :], in0=on[:, 1, :], scalar=lv[0:64, 4:5], in1=on[:, 0, :], op0=ALU.mult, op1=ALU.add),
                                 reads=[onB, lvB], writes=[onB])
                            S.op("act", lambda e: e.activation(out=sq[0:64, 0, :], in_=on[:, 0, :], func=AF.Square), reads=[onB], writes=[sqB[0]])
                            p2, p2b = ps()
                            mm(p2[0:64, :], ones[0:64, 0:64], sq[0:64, 0, :], True, True, [sqB[0], cB], [p2b])
                            rstd_from_ps(p2, p2b, 64, 64, rs, rsB, tm2, tm2B)
                            S.op("dve", lambda e: e.tensor_tensor(out=tmp[0:64, :], in0=on[:, 0, :], in1=rs[:, :], op=ALU.mult), reads=[onB, rsB], writes=[tmpB])
                            S.op("act", lambda e: e.activation(out=oa[:, :], in_=tmp[0:64, :], func=AF.Identity, scale=lv[0:64, 5:6]), reads=[tmpB, lvB], writes=[oaB])
                            out_proj(64, oa[:, :], oaB, g)
                    S.barrier()

                if mixlim < 4:
                    return
                with ExitStack() as pb_:
                    hb_[0] = sb(pb_, "b_h", [128, 8, 512], F32R)
                    xb = sb(pb_, "b_xb", [128, NT])
                    xc = sb(pb_, "b_xc", [128, NT], F32R)
                    gg = sb(pb_, "b_gg", [128, NT])
                    Pb = sb(pb_, "b_P", [128, NT])
                    Qb = sb(pb_, "b_Q", [128, NT])
                    xbB, xcB, ggB, PbB, QbB = Buf("b_xb"), Buf("b_xc"), Buf("b_gg"), Buf("b_P"), Buf("b_Q")
                    wgt = sb(pb_, "b_wg", [128, 4, 128], F32R)
                    wgtB = Buf("b_wg")
                    A2 = sb(pb_, "b_A2", [128, NT])
                    A2B = Buf("b_A2")
                    rrd = [sb(pb_, "b_r%d" % d, [128, 512]) for d in range(2)]
                    iid = [sb(pb_, "b_i%d" % d, [128, 512]) for d in range(2)]
                    tad = [sb(pb_, "b_t%d" % d, [128, 512]) for d in range(2)]
                    rrdB = [Buf("b_r%d" % d) for d in range(2)]
                    iidB = [Buf("b_i%d" % d) for d in range(2)]
                    tadB = [Buf("b_t%d" % d) for d in range(2)]
                    nk = sb(pb_, "b_nk", [128, 4])
                    nkB = Buf("b_nk")
                    xcf = xc[:].bitcast(F32)
                    for cc in range(4):
                        load_wp(0, 768 + cc * 128, 128, "sp")
                        load_wp(1, 1280 + cc * 128, 128, "act")
                        S.dma("pool", wo[:, :], w_out_d[l, 256 + cc * 128:256 + (cc + 1) * 128, :], writes=[woB])
                        S.op("dve", lambda e: e.tensor_scalar(out=wgt[:], in0=mhalf[:, :].rearrange("p (a b) -> p a b", b=128), scalar1=0.0, scalar2=None, op0=ALU.mult), reads=[cB], writes=[wgtB])
                        for d in range(2):
                            for gt in range(2):
                                for bl in range(2):
                                    S.dma("sp", wgt[bl * 64:(bl + 1) * 64, d * 2 + gt, bl * 64:(bl + 1) * 64], w_gate_d[l, d, gt, cc * 2 + bl], writes=[wgtB])
                        for g in range(NG):
                            ht, hB = get_h(g)
                            pt, pb = proj(0, 128, ht, hB)
                            S.op("act", lambda e, pt=pt, g=g: e.copy(out=xb[:, g * 512:(g + 1) * 512], in_=pt[:]), reads=[pb], writes=[xbB])
                            pt, pb = proj(1, 128, ht, hB)
                            if g % 2 == 0:
                                ta, taB, tb, tbB = tmp, tmpB, tm2, tm2B
                            else:
                                ta, taB, tb, tbB = rrd[0], rrdB[0], rrd[1], rrdB[1]
                            S.op("act", lambda e, pt=pt, ta=ta: e.activation(out=ta[:, :], in_=pt[:], func=AF.Square), reads=[pb], writes=[taB])
                            S.op("dve", lambda e, ta=ta: e.tensor_scalar(out=ta[:, :], in0=ta[:, :], scalar1=0.044715, scalar2=1.0, op0=ALU.mult, op1=ALU.add), reads=[taB], writes=[taB])
                            S.op("dve", lambda e, pt=pt, ta=ta: e.tensor_tensor(out=ta[:, :], in0=ta[:, :], in1=pt[:], op=ALU.mult), reads=[taB, pb], writes=[taB])
                            S.op("act", lambda e, ta=ta, tb=tb: e.activation(out=tb[:, :], in_=ta[:, :], func=AF.Sigmoid, scale=1.5957691216057308), reads=[taB], writes=[tbB])
                            S.op("dve", lambda e, pt=pt, g=g, tb=tb: e.tensor_tensor(out=gg[:, g * 512:(g + 1) * 512], in0=tb[:, :], in1=pt[:], op=ALU.mult), reads=[tbB, pb], writes=[ggB])
                        cw = lambda j: vecs[:, l, 99 + j * 4 + cc:100 + j * 4 + cc]
                        S.op("dve", lambda e: e.tensor_scalar(out=xc[:, :], in0=xb[:, :], scalar1=cw(2), scalar2=vecs[:, l, 115 + cc:116 + cc], op0=ALU.mult, op1=ALU.add),
                             reads=[xbB, cB], writes=[xcB])
                        for j, off in ((0, -2), (1, -1), (3, 1)):
                            lo, hi = max(0, -off), NT - max(0, off)
                            S.op("dve", lambda e, j=j, off=off, lo=lo, hi=hi: e.scalar_tensor_tensor(out=xc[:, lo:hi], in0=xb[:, lo + off:hi + off], scalar=cw(j), in1=xcf[:, lo:hi],
                                                                                                  op0=ALU.mult, op1=ALU.add), reads=[xbB, cB], writes=[xcB])
                        S.op("dve", lambda e: e.tensor_scalar(out=nk[:, 0:4], in0=vecs[:, l, 99 + cc:99 + cc + 13:4], scalar1=keepv[:, 1:2], scalar2=None, op0=ALU.mult),
                             reads=[cB], writes=[nkB])
                        xc3 = xc[:].rearrange("p (s t) -> p s t", t=256)
                        xcf3 = xcf.rearrange("p (s t) -> p s t", t=256)
                        xb3 = xb[:].rearrange("p (s t) -> p s t", t=256)
                        S.op("dve", lambda e: e.scalar_tensor_tensor(out=xc3[:, 1:8, 0:2], in0=xb3[:, 0:7, 254:256], scalar=nk[:, 0:1], in1=xcf3[:, 1:8, 0:2], op0=ALU.mult, op1=ALU.add),
                             reads=[xbB, nkB], writes=[xcB])
                        S.op("dve", lambda e: e.scalar_tensor_tensor(out=xc3[:, 1:8, 0:1], in0=xb3[:, 0:7, 255:256], scalar=nk[:, 1:2], in1=xcf3[:, 1:8, 0:1], op0=ALU.mult, op1=ALU.add),
                             reads=[xbB, nkB], writes=[xcB])
                        S.op("dve", lambda e: e.scalar_tensor_tensor(out=xc3[:, 0:7, 255:256], in0=xb3[:, 1:8, 0:1], scalar=nk[:, 3:4], in1=xcf3[:, 0:7, 255:256], op0=ALU.mult, op1=ALU.add),
                             reads=[xbB, nkB], writes=[xcB])
                        AA = [xb, A2]
                        AAB = [xbB, A2B]
                        HB_ = [Pb, Qb]
                        HBB = [PbB, QbB]
                        D2 = (0, 1)
                        for g in range(NG):
                            sl = slice(g * 512, (g + 1) * 512)
                            prs, pis = [], []
                            for d in D2:
                                pr, prb = ps()
                                mm(pr[:], wgt[:, d * 2 + 0, :], xc[:, sl], True, True, [wgtB, xcB], [prb])
                                prs.append((pr, prb))
                            for d in D2:
                                pi, pib = ps()
                                mm(pi[:], wgt[:, d * 2 + 1, :], xc[:, sl], True, True, [wgtB, xcB], [pib])
                                pis.append((pi, pib))
                            for d in D2:
                                S.op("act", lambda e, d=d: e.activation(out=rrd[d][:, :], in_=prs[d][0][:], func=AF.Sigmoid, bias=vecs[:, l, 119 + (d * 2 + 0) * 4 + cc:120 + (d * 2 + 0) * 4 + cc]),
                                     reads=[prs[d][1], cB], writes=[rrdB[d]])
                            for d in D2:
                                S.op("act", lambda e, d=d: e.activation(out=iid[d][:, :], in_=pis[d][0][:], func=AF.Sigmoid, bias=vecs[:, l, 119 + (d * 2 + 1) * 4 + cc:120 + (d * 2 + 1) * 4 + cc]),
                                     reads=[pis[d][1], cB], writes=[iidB[d]])
                            for d in D2:
                                S.op("act", lambda e, d=d: e.activation(out=AA[d][:, sl], in_=rrd[d][:, :], func=AF.Exp, scale=lv[:, 8 + d * 4 + cc:9 + d * 4 + cc]),
                                     reads=[rrdB[d], lvB, xcB], writes=[AAB[d]])
                            for d in D2:
                                S.op("dve", lambda e, d=d: e.tensor_tensor(out=tad[d][:, :], in0=AA[d][:, sl], in1=AA[d][:, sl], op=ALU.mult), reads=[AAB[d]], writes=[tadB[d]])
                            for d in D2:
                                S.op("dve", lambda e, d=d: e.tensor_scalar(out=tad[d][:, :], in0=tad[d][:, :], scalar1=-1.0, scalar2=1.0, op0=ALU.mult, op1=ALU.add), reads=[tadB[d]], writes=[tadB[d]])
                            for d in D2:
                                S.op("dve", lambda e, d=d: e.tensor_scalar(out=tad[d][:, :], in0=tad[d][:, :], scalar1=1e-20, scalar2=None, op0=ALU.max), reads=[tadB[d]], writes=[tadB[d]])
                            for d in D2:
                                S.op("act", lambda e, d=d: e.activation(out=tad[d][:, :], in_=tad[d][:, :], func=AF.Ln), reads=[tadB[d]], writes=[tadB[d]])
                            for d in D2:
                                S.op("act", lambda e, d=d: e.activation(out=rrd[d][:, :], in_=tad[d][:, :], func=AF.Exp, scale=0.5), reads=[tadB[d], rrdB[d]], writes=[rrdB[d]])
                            for d in D2:
                                S.op("dve", lambda e, d=d: e.tensor_tensor(out=iid[d][:, :], in0=iid[d][:, :], in1=xcf[:, sl], op=ALU.mult), reads=[iidB[d], xcB], writes=[iidB[d]])
                            for d in D2:
                                S.op("dve", lambda e, d=d: e.tensor_tensor(out=HB_[d][:, sl], in0=iid[d][:, :], in1=rrd[d][:, :], op=ALU.mult), reads=[iidB[d], rrdB[d]], writes=[HBB[d]])
                        for d in D2:
                            A3 = AA[d][:].rearrange("p (s t) -> p s t", t=256)
                            col = 0 if d == 0 else 255
                            S.op("dve", lambda e, col=col, A3=A3: e.tensor_scalar(out=A3[:, :, col:col + 1], in0=A3[:, :, col:col + 1], scalar1=keepv[:, 0:1], scalar2=None, op0=ALU.mult),
                                 reads=[AAB[d], cB], writes=[AAB[d]])
                            h0 = lru0[:, l, d * 4 + cc:d * 4 + cc + 1]
                            if d == 0:
                                S.op("dve", lambda e, h0=h0: e.tensor_tensor_scan(out=Pb[:, :], data0=AA[0][:, :], data1=Pb[:, :], initial=h0, op0=ALU.mult, op1=ALU.add),
                                     reads=[AAB[0], PbB, cB], writes=[PbB])
                                fin = Pb[:].rearrange("p (s t) -> p s t", t=256)[:, :, 255]
                                fB = PbB
                            else:
                                Qf = Qb[:]
                                S.op("dve", lambda e, h0=h0, Qf=Qf: e.tensor_tensor_scan(out=Qf[:, ::-1], data0=AA[1][:, ::-1], data1=Qf[:, ::-1], initial=h0, op0=ALU.mult, op1=ALU.add),
                                     reads=[AAB[1], QbB, cB], writes=[QbB])
                                fin = Qf.rearrange("p (s t) -> p s t", t=256)[:, :, 0]
                                fB = QbB
                            c0 = ((l * 2 + d) * 4 + cc) * 8
                            S.op("dve", lambda e, fin=fin, c0=c0: e.tensor_copy(out=stT[:, c0:c0 + 8], in_=fin), reads=[fB], writes=[stB])
                        S.op("dve", lambda e: e.tensor_tensor(out=Pb[:, :], in0=Pb[:, :], in1=Qb[:, :], op=ALU.add), reads=[PbB, QbB], writes=[PbB])
                        S.op("dve", lambda e: e.tensor_tensor(out=xc[:, :], in0=Pb[:, :], in1=gg[:, :], op=ALU.mult), reads=[PbB, ggB], writes=[xcB])
                        for g in range(NG):
                            out_proj(128, xc[:, g * 512:(g + 1) * 512], xcB, g)
                    S.barrier()

        import os
        stop = int(os.environ.get("MK_STOP", "99"))
        stage = 0
        for l in range(DEPTH):
            for fn in (lambda: ffn(l, 0, 0), lambda: mixer(l), lambda: ffn(l, 2, 1)):
                if stage < stop:
                    try:
                        fn()
                    except StopMixer:
                        pass
                    S.mute = False
                stage += 1

        with ExitStack() as ph:
            stg = [sb(ph, "ystg%d" % i, [128, D]) for i in range(2)]
            stgB = [Buf("ystg%d" % i) for i in range(2)]
            for tt in range(NT // 128):
                st_, stb_ = stg[tt % 2], stgB[tt % 2]
                g = tt // 4
                for half in range(2):
                    pt, pb = ps()
                    for cc in range(4):
                        c = half * 4 + cc
                        S.op("pe", lambda e, c=c, cc=cc, pt=pt: e.transpose(pt[:, cc * 128:(cc + 1) * 128], xT[:, c, tt * 128:(tt + 1) * 128], ident[:]),
                             reads=[xB[c][g], cB], writes=[pb], acc=(cc > 0), inc=(cc == 3))
                    S.op("dve" if half else "act",
                         (lambda e, pt=pt, st_=st_: e.tensor_copy(out=st_[:, 512:1024], in_=pt[:])) if half else
                         (lambda e, pt=pt, st_=st_: e.copy(out=st_[:, 0:512], in_=pt[:])),
                         reads=[pb], writes=[stb_])
                S.dma("sp" if tt % 2 else "act", y_d[tt * 128:(tt + 1) * 128, :], st_[:], reads=[stb_])
            pt, pb = ps()
            S.op("pe", lambda e: e.transpose(pt[:, 0:128], stT[:], ident[:]), reads=[stB, cB], writes=[pb])
            S.op("dve", lambda e: e.tensor_copy(out=stg[0][:, 0:128], in_=pt[:, 0:128]), reads=[pb, stgB[0]], writes=[stgB[0]])
            S.dma("sp", ost_d, stg[0][:, 0:128], reads=[stgB[0]])
            S.barrier()
    return nc


_CACHE = {}


def _consts():
    cR = np.zeros((128, 5 * 128), np.float32)
    cR[:, 0:128] = 1.0
    for b in range(4):
        cR[b * 32:(b + 1) * 32, 128 + b * 32:128 + (b + 1) * 32] = 1.0
    cR[64, 256:256 + 64] = 1.0
    R32 = np.zeros((32, 32), np.float32)
    for m in range(16):
        R32[m + 16, m] = -1.0
        R32[m, m + 16] = 1.0
    for b in range(4):
        cR[b * 32:(b + 1) * 32, 384 + b * 32:384 + (b + 1) * 32] = R32
    cR[64:96, 512 + 64:512 + 96] = R32
    return cR


def _rope_tables():
    T = NT
    rows = T // 64
    row = np.repeat(np.arange(rows), 64).astype(np.float32)
    col = np.tile(np.arange(64), rows).astype(np.float32)
    n = 8
    inv = (10000.0 ** (-np.arange(n, dtype=np.float32) / n)).astype(np.float32)
    ang = np.concatenate([row[:, None] * inv, col[:, None] * inv], axis=-1)
    cos, sin = np.cos(ang).astype(np.float32), np.sin(ang).astype(np.float32)
    idx = np.arange(128) % 16
    return np.ascontiguousarray(cos[:, idx].T), np.ascontiguousarray(sin[:, idx].T)


def kernel(x_prompt, x_sample, cache_diff_k, cache_diff_v, cache_mla_ckv, cache_mla_krope, state_lru, c, c_ctx,
           norm_g, w_ada, b_ada, w_ffn_in, w_ffn_out, w_in, w_out, diff_qk_norm, diff_lambda, diff_subln,
           lru_conv_w, lru_conv_b, lru_w_gate, lru_b_gate, lru_lambda, mla_cq_norm, mla_ckv_norm,
           mla_w_uq, mla_w_ukv, mla_qk_norm):
    f = lambda a: np.ascontiguousarray(np.asarray(a, dtype=np.float32))
    x_prompt, x_sample = f(x_prompt), f(x_sample)
    if "nc" not in _CACHE:
        _CACHE["nc"] = build_program()
    nc = _CACHE["nc"]

    vecs = np.zeros((DEPTH, 128, NV), np.float32)
    p = np.arange(128)
    for l in range(DEPTH):
        vecs[l, :, 0:24] = f(norm_g)[l].reshape(3, 8, 128).transpose(2, 0, 1).reshape(128, 24)
        vecs[l, :, 24:96] = f(b_ada)[l].reshape(72, 128).T
        vecs[l, :, 96] = f(diff_qk_norm)[l, 0][p % 32]
        vecs[l, :, 97] = f(diff_qk_norm)[l, 1][p % 32]
        vecs[l, :, 98] = f(diff_subln)[l][p % 64]
        vecs[l, :, 99:115] = f(lru_conv_w)[l].reshape(4, 4, 128).transpose(2, 0, 1).reshape(128, 16)
        vecs[l, :, 115:119] = f(lru_conv_b)[l].reshape(4, 128).T
        vecs[l, :, 119:135] = f(lru_b_gate)[l].reshape(4, 4, 128).transpose(2, 0, 1).reshape(128, 16)
        vecs[l, :, 135:143] = f(lru_lambda)[l].reshape(2, 4, 128).transpose(2, 0, 1).reshape(128, 8)
        vecs[l, :, 143] = f(mla_cq_norm)[l, 0:128]
        vecs[l, 0:64, 144] = f(mla_cq_norm)[l, 128:192]
        vecs[l, :, 145] = f(mla_ckv_norm)[l]
        vecs[l, 0:96, 146] = f(mla_qk_norm)[l, 0]
        vecs[l, 0:96, 147] = f(mla_qk_norm)[l, 1]
        vecs[l, :, 148:276] = f(diff_lambda)[l].reshape(1, 128)
    cR = _consts()
    ident = np.eye(128, dtype=np.float32)
    cosT, sinT = _rope_tables()
    seg = np.arange(NT) // 256
    mq_p = (seg[None, :] == np.arange(8)[:, None]).astype(np.float32)
    mk_p = np.full((8, NK), -BIG, np.float32)
    mk_p[:, NC_:] = np.where(seg[None, :] == np.arange(8)[:, None], 0.0, -BIG)
    bf = ml_dtypes.bfloat16
    shared = {
        "constsR": cR, "ident": ident, "vecs": vecs,
        "w_ada": f(w_ada), "w_ffn_in": f(w_ffn_in), "w_ffn_out": f(w_ffn_out), "w_in": f(w_in), "w_out": f(w_out),
        "lru_w_gate": f(lru_w_gate), "mla_w_uq": f(mla_w_uq), "mla_w_ukv": f(mla_w_ukv),
    }
    in_maps = []
    for core in range(8):
        m = dict(shared)
        if core < 4:
            m["x"] = x_prompt[core * 8:(core + 1) * 8].reshape(NT, D)
            m["cond"] = np.ascontiguousarray(f(c_ctx).reshape(8, 128).T)
            m["ck"] = np.zeros((DEPTH, NC_, 256), np.float32)
            m["cv"] = np.zeros((DEPTH, NC_, 256), np.float32)
            m["cckv"] = np.zeros((DEPTH, NC_, 128), np.float32)
            m["ckr"] = np.zeros((DEPTH, NC_, 32), np.float32)
            m["lru0"] = np.zeros((DEPTH, 128, 8), np.float32)
            m["keepv"] = np.ascontiguousarray(np.tile(np.array([[0.0, -1.0]], np.float32), (128, 1)))
            m["cosT"] = np.ones((128, NT), np.float32)
            m["sinT"] = np.zeros((128, NT), np.float32)
            m["maskq"] = mq_p.astype(bf)
            m["maskk"] = mk_p.astype(bf)
        else:
            b = core - 4
            m["x"] = x_sample[b]
            m["cond"] = np.ascontiguousarray(f(c)[b].reshape(8, 128).T)
            m["ck"] = f(cache_diff_k)[b].reshape(DEPTH, NC_, 256)
            m["cv"] = f(cache_diff_v)[b].reshape(DEPTH, NC_, 256)
            m["cckv"] = f(cache_mla_ckv)[b]
            m["ckr"] = f(cache_mla_krope)[b]
            m["lru0"] = np.ascontiguousarray(f(state_lru)[b].reshape(DEPTH, 2, 4, 128).transpose(0, 3, 1, 2).reshape(DEPTH, 128, 8))
            m["keepv"] = np.ascontiguousarray(np.tile(np.array([[1.0, 0.0]], np.float32), (128, 1)))
            m["cosT"] = cosT
            m["sinT"] = sinT
            m["maskq"] = np.zeros((8, NT), bf)
            m["maskk"] = np.zeros((8, NK), bf)
        in_maps.append(m)

    if _CACHE.get("prep_only"):
        return in_maps
    res = run_bass_kernel_spmd(nc, in_maps, core_ids=list(range(8)))
    r = res.results
    y_prompt = np.stack([r[i]["y"] for i in range(4)]).reshape(32, 256, D)
    y_sample = np.stack([r[i]["y"] for i in range(4, 8)]).reshape(4, NT, D)

    def gather(name, tail):
        a = np.stack([r[i][name] for i in range(4)])
        a = a.reshape(4, DEPTH, 8, 256, -1).transpose(0, 2, 1, 3, 4).reshape((32, DEPTH, 256) + tail)
        return np.ascontiguousarray(a)

    new_k = gather("o_k", (4, 2, 32))
    new_v = gather("o_v", (4, 64))
    new_ckv = gather("o_ckv", (128,))
    new_kr = gather("o_kr", (32,))
    st = np.stack([r[i]["o_st"] for i in range(4)])
    st = st.reshape(4, DEPTH, 2, 4, 8, 128).transpose(0, 4, 1, 2, 3, 5).reshape(32, DEPTH, 2, 512)
    return (y_prompt.astype(np.float32), y_sample.astype(np.float32), new_k.astype(np.float32), new_v.astype(np.float32),
            new_ckv.astype(np.float32), new_kr.astype(np.float32), np.ascontiguousarray(st).astype(np.float32))
```

```python
import math
import numpy as np
import ml_dtypes
import concourse.bass as bass
import concourse.mybir as mybir
from concourse.bass_utils import run_bass_kernel_spmd

F32 = mybir.dt.float32
F32R = mybir.dt.float32r
BF16 = mybir.dt.bfloat16
AF = mybir.ActivationFunctionType
ALU = mybir.AluOpType
AX = mybir.AxisListType

D = 1024
NT = 2048
NG = 4
NC_ = 512
NK = NT + NC_
NKB = NK // 128
DFF = 2816
NFF = DFF // 128
DEPTH = 2
EPS = 1e-6
BIG = 2048.0
NV = 276


class StopMixer(Exception):
    pass


class Buf:
    __slots__ = ("name", "w", "r", "x")

    def __init__(self, name, x=False):
        self.name = name
        self.w = None
        self.r = []
        self.x = x


class Sync:
    def __init__(self, nc, es):
        self.nc = nc
        self.eng = {"pe": nc.tensor, "act": nc.scalar, "dve": nc.vector, "pool": nc.gpsimd, "sp": nc.sync}
        self.sem = {k: es.enter_context(nc.semaphore("s_" + k)) for k in ("pe", "act", "dve", "pool")}
        self.cnt = {k: 0 for k in self.sem}
        self.pend = {k: False for k in self.sem}
        self.waited = {k: {} for k in self.eng}
        self.nslot = 12
        self.dq = {}
        for q in ("sp", "act", "pool"):
            self.dq[q] = {"sems": [es.enter_context(nc.semaphore("d_%s%d" % (q, i))) for i in range(self.nslot)],
                          "val": [0] * self.nslot, "i": 0}
        self.allsems = {}

    def _wait(self, e, ev):
        if ev is None:
            return
        sem, val = ev
        key = id(sem)
        if self.waited[e].get(key, 0) >= val:
            return
        self.waited[e][key] = val
        self.allsems[key] = sem
        self.eng[e].wait_ge(sem, val)

    def _deps(self, e, reads, writes, acc):
        for b in reads:
            self._wait(e, b.w)
            if b.x:
                for ev in b.r:
                    self._wait(e, ev)
        for b in writes:
            if not (acc and e == "pe"):
                self._wait(e, b.w)
            for ev in b.r:
                self._wait(e, ev)

    def _mark(self, ev, reads, writes):
        for b in writes:
            b.w = ev
            b.r = []
        for b in reads:
            b.r.append(ev)
            if len(b.r) > 24:
                b.r = b.r[-24:]

    mute = False

    def op(self, e, fn, reads=(), writes=(), acc=False, inc=True):
        if self.mute:
            return None
        self._deps(e, reads, writes, acc)
        ins = fn(self.eng[e])
        if inc:
            self.cnt[e] += 1
            ins.then_inc(self.sem[e], 1)
            self.pend[e] = False
            ev = (self.sem[e], self.cnt[e])
        else:
            self.pend[e] = True
            ev = (self.sem[e], self.cnt[e] + 1)
        self._mark(ev, reads, writes)
        return ev

    def dma(self, q, out, in_, reads=(), writes=()):
        if self.mute:
            return None
        if q == "act":
            q = "sp"
        dq = self.dq[q]
        i = dq["i"]
        dq["i"] = (i + 1) % self.nslot
        sem = dq["sems"][i]
        if dq["val"][i]:
            self._wait(q, (sem, dq["val"][i]))
        self._deps(q, reads, writes, False)
        dq["val"][i] += 16
        self.eng[q].dma_start(out=out, in_=in_).then_inc(sem, 16)
        ev = (sem, dq["val"][i])
        self._mark(ev, reads, writes)
        return ev

    def barrier(self):
        if self.mute:
            return
        evs = []
        for k in self.sem:
            assert not self.pend[k]
            if self.cnt[k]:
                evs.append((self.sem[k], self.cnt[k]))
        for q in self.dq.values():
            for s, v in zip(q["sems"], q["val"]):
                if v:
                    evs.append((s, v))
        for e in self.eng:
            for ev in evs:
                self._wait(e, ev)


def build_program():
    nc = bass.Bass("TRN2", target_bir_lowering=False)
    nc.dge_precook = False
    from contextlib import ExitStack

    def din(name, shape, dt=F32):
        return nc.dram_tensor(name, list(shape), dt, kind="ExternalInput").ap()

    def dout(name, shape, dt=F32):
        return nc.dram_tensor(name, list(shape), dt, kind="ExternalOutput").ap()

    x_d = din("x", [NT, D])
    cond_d = din("cond", [128, 8])
    ck_d = din("ck", [DEPTH, NC_, 256])
    cv_d = din("cv", [DEPTH, NC_, 256])
    cckv_d = din("cckv", [DEPTH, NC_, 128])
    ckr_d = din("ckr", [DEPTH, NC_, 32])
    lru0_d = din("lru0", [DEPTH, 128, 8])
    keep_d = din("keepv", [128, 2])
    cos_d = din("cosT", [128, NT])
    sin_d = din("sinT", [128, NT])
    mq_d = din("maskq", [8, NT], BF16)
    mk_d = din("maskk", [8, NK], BF16)
    cR_d = din("constsR", [128, 5 * 128], F32R)
    id_d = din("ident", [128, 128])
    vecs_d = din("vecs", [DEPTH, 128, NV])
    w_ada_d = din("w_ada", [DEPTH, D, 9 * D], F32R)
    w_fi_d = din("w_ffn_in", [DEPTH, 2, D, 2 * DFF], F32R)
    w_fo_d = din("w_ffn_out", [DEPTH, 2, DFF, D], F32R)
    w_in_d = din("w_in", [DEPTH, D, 2144], F32R)
    w_out_d = din("w_out", [DEPTH, D, D], F32R)
    w_gate_d = din("lru_w_gate", [DEPTH, 2, 2, 8, 64, 64], F32R)
    w_uq_d = din("mla_w_uq", [DEPTH, 192, 384], F32R)
    w_ukv_d = din("mla_w_ukv", [DEPTH, 128, 512], F32R)

    y_d = dout("y", [NT, D])
    ok_d = dout("o_k", [DEPTH, NT, 256])
    ov_d = dout("o_v", [DEPTH, NT, 256])
    ockv_d = dout("o_ckv", [DEPTH, NT, 128])
    okr_d = dout("o_kr", [DEPTH, NT, 32])
    ost_d = dout("o_st", [128, 128])

    with ExitStack() as es:
        S = Sync(nc, es)

        uid = [0]

        def sb(stack, name, shape, dt=F32):
            uid[0] += 1
            return stack.enter_context(nc.sbuf_tensor("sb%d_%s" % (uid[0], name), list(shape), dt))

        xT = sb(es, "xT", [128, 8, NT])
        xB = [[Buf("x%d_%d" % (c, g)) for g in range(NG)] for c in range(8)]
        cR = sb(es, "cR", [128, 5 * 128], F32R)
        ident = sb(es, "ident", [128, 128])
        vecs = sb(es, "vecs", [128, DEPTH, NV])
        modv = sb(es, "modv", [128, DEPTH, 72])
        modA = sb(es, "modA", [128, DEPTH, 24])
        modG = sb(es, "modG", [128, DEPTH, 24])
        keepv = sb(es, "keepv", [128, 2])
        lru0 = sb(es, "lru0", [128, DEPTH, 8])
        stT = sb(es, "stT", [128, 128])
        mhalf = sb(es, "mhalf", [128, 512])
        epsc = sb(es, "epsc", [128, 1])
        small = sb(es, "small", [128, 64])
        cB = Buf("consts")
        stB = Buf("stT")
        smB = Buf("small")
        PS = [es.enter_context(nc.psum_tensor("ps%d" % i, [128, 512], F32)) for i in range(8)]
        PSB = [Buf("ps%d" % i, x=True) for i in range(8)]
        psi = [0]

        def ps():
            i = psi[0]
            psi[0] = (i + 1) % 6
            return PS[i], PSB[i]

        ones = cR[:, 0:128]
        bd32 = cR[:, 128:256]
        sel65 = cR[:, 256:384]
        R64 = cR[:, 384:512]
        R96 = cR[:, 512:640]

        S.dma("sp", cR[:], cR_d, writes=[cB])
        S.dma("sp", ident[:], id_d, writes=[cB])
        S.dma("sp", vecs[:], vecs_d.rearrange("l p n -> p l n"), writes=[cB])
        S.dma("sp", keepv[:], keep_d, writes=[cB])
        S.dma("sp", lru0[:], lru0_d.rearrange("l p n -> p l n"), writes=[cB])
        S.op("pool", lambda e: e.memset(mhalf[:], 0.0), writes=[cB])
        S.op("pool", lambda e: e.memset(epsc[:], EPS), writes=[cB])
        S.op("pool", lambda e: e.memset(stT[:], 0.0), writes=[stB])

        def rstd_from_ps(pst, psb, rows, n, dst, dstB, tmp, tmpB, cols=512):
            S.op("act", lambda e: e.activation(out=tmp[0:rows, 0:cols], in_=pst[0:rows, 0:cols], func=AF.Ln, scale=1.0 / n, bias=epsc[0:rows, 0:1]),
                 reads=[psb, cB], writes=[tmpB])
            S.op("act", lambda e: e.activation(out=dst[0:rows, 0:cols], in_=tmp[0:rows, 0:cols], func=AF.Exp, scale=-0.5),
                 reads=[tmpB], writes=[dstB])

        def mm(out, lhsT, rhs, start, stop, reads, writes, lazy=False):
            return S.op("pe", lambda e: e.matmul(out, lhsT, rhs, start=start, stop=stop), reads=reads, writes=writes,
                        acc=not start, inc=(stop if lazy else True))


        def attn_pipe(items, s_mm, pv_mm, PT, PTB, ptc, sc, LA=2):
            q = []
            n = len(items)
            for i in range(n + LA):
                if i < n:
                    pS, pSb = ps()
                    s_mm(items[i], pS, pSb)
                    q.append((pS, pSb))
                j = i - LA
                if j >= 0:
                    pS, pSb = q[j]
                    k_ = ptc[0] % len(PT)
                    ptc[0] += 1
                    P_, PB_ = PT[k_], PTB[k_]
                    S.op("act", lambda e, pS=pS, P_=P_: e.activation(out=P_[:], in_=pS[:], func=AF.Exp, scale=sc), reads=[pSb], writes=[PB_])
                    pv_mm(items[j], P_, PB_)

        with ExitStack() as ph:
            stg = sb(ph, "xstg", [128, 4, D])
            stgB = Buf("xstg")
            for g in range(NG):
                S.dma("sp", stg[:], x_d[g * 512:(g + 1) * 512, :].rearrange("(t p) n -> p t n", p=128), writes=[stgB])
                for c in range(8):
                    pt, pb = ps()
                    for t in range(4):
                        S.op("pe", lambda e, t=t, c=c, pt=pt: e.transpose(pt[:, t * 128:(t + 1) * 128], stg[:, t, c * 128:(c + 1) * 128], ident[:]),
                             reads=[stgB, cB], writes=[pb], acc=(t > 0), inc=(t == 3))
                    S.op("dve" if c % 2 else "act",
                         (lambda e, c=c, pt=pt, g=g: e.tensor_copy(out=xT[:, c, g * 512:(g + 1) * 512], in_=pt[:]))
                         if c % 2 else
                         (lambda e, c=c, pt=pt, g=g: e.copy(out=xT[:, c, g * 512:(g + 1) * 512], in_=pt[:])),
                         reads=[pb], writes=[xB[c][g]])

            cnd = sb(ph, "cnd", [128, 8])
            s2 = sb(ph, "s2", [128, 8, 2], F32R)
            cndB = Buf("cnd")
            S.dma("sp", cnd[:], cond_d, writes=[cndB])
            for j in range(2):
                S.op("act", lambda e, j=j: e.activation(out=s2[:, :, j], in_=cnd[:], func=AF.Silu), reads=[cndB], writes=[smB])
            wad = [sb(ph, "wad%d" % i, [128, 8, 512], F32R) for i in range(2)]
            wadB = [Buf("wad%d" % i) for i in range(2)]
            for l in range(DEPTH):
                pm, pmb = ps()
                wv = w_ada_d[l].rearrange("(k p) n -> p k n", p=128)
                for cg in range(18):
                    wt, wb = wad[cg % 2], wadB[cg % 2]
                    S.dma("sp" if cg % 2 else "act", wt[:], wv[:, :, cg * 512:(cg + 1) * 512], writes=[wb])
                    for jj in range(4):
                        j = cg * 4 + jj
                        for k in range(8):
                            mm(pm[:, 2 * j:2 * j + 2], wt[:, k, jj * 128:(jj + 1) * 128], s2[:, k, :], k == 0, k == 7,
                               [wb, smB], [pmb], lazy=True)
                pmv = pm[:, 0:144].rearrange("p (j t) -> p j t", t=2)
                S.op("dve", lambda e, l=l, pmv=pmv: e.tensor_tensor(out=modv[:, l, :], in0=pmv[:, :, 0], in1=vecs[:, l, 24:96], op=ALU.add),
                     reads=[pmb, cB], writes=[cB])
                for s in range(3):
                    S.op("dve", lambda e, l=l, s=s: e.scalar_tensor_tensor(out=modA[:, l, s * 8:(s + 1) * 8], in0=modv[:, l, (3 * s + 1) * 8:(3 * s + 2) * 8],
                                                                          scalar=1.0, in1=vecs[:, l, s * 8:(s + 1) * 8], op0=ALU.add, op1=ALU.mult),
                         reads=[cB], writes=[cB])
                    S.op("dve", lambda e, l=l, s=s: e.tensor_scalar(out=modG[:, l, s * 8:(s + 1) * 8], in0=modv[:, l, (3 * s + 2) * 8:(3 * s + 3) * 8],
                                                                   scalar1=(1.0 if s == 1 else 0.5), scalar2=None, op0=ALU.mult),
                         reads=[cB], writes=[cB])
            S.barrier()

        def make_h(l, s, g, h_ap, hB, sq, sqB, rs, rsB, tmp, tmpB, tmp2=None):
            pt, pb = ps()
            for c in range(8):
                S.op("act" if c % 2 else "dve",
                     (lambda e, c=c: e.activation(out=sq[:, c % 2, :], in_=xT[:, c, g * 512:(g + 1) * 512], func=AF.Square)) if c % 2 else
                     (lambda e, c=c: e.tensor_tensor(out=sq[:, c % 2, :], in0=xT[:, c, g * 512:(g + 1) * 512], in1=xT[:, c, g * 512:(g + 1) * 512], op=ALU.mult)),
                     reads=[xB[c][g]], writes=[sqB[c % 2]])
                mm(pt[:], ones, sq[:, c % 2, :], c == 0, c == 7, [sqB[c % 2], cB], [pb])
            rstd_from_ps(pt, pb, 128, D, rs, rsB, tmp, tmpB)
            tps = [(tmp, tmpB)] + ([tmp2] if tmp2 is not None else [])
            for c in range(8):
                tq, tqB = tps[c % len(tps)]
                S.op("dve", lambda e, c=c, tq=tq: e.tensor_tensor(out=tq[:, :], in0=xT[:, c, g * 512:(g + 1) * 512], in1=rs[:, :], op=ALU.mult),
                     reads=[xB[c][g], rsB], writes=[tqB])
                S.op("act", lambda e, c=c, tq=tq: e.activation(out=h_ap(c), in_=tq[:, :], func=AF.Identity,
                                                               scale=modA[:, l, s * 8 + c:s * 8 + c + 1], bias=modv[:, l, 3 * s * 8 + c:3 * s * 8 + c + 1]),
                     reads=[tqB, cB], writes=[hB])

        def x_update(l, s, o, g, pt, pb):
            S.op("dve", lambda e: e.scalar_tensor_tensor(out=xT[:, o, g * 512:(g + 1) * 512], in0=pt[:], scalar=modG[:, l, s * 8 + o:s * 8 + o + 1],
                                                         in1=xT[:, o, g * 512:(g + 1) * 512], op0=ALU.mult, op1=ALU.add),
                 reads=[pb, cB], writes=[xB[o][g]])

        def ffn(l, s, fi):
            with ExitStack() as ph:
                h = sb(ph, "f_h", [128, 8, 1024], F32R)
                hB = [Buf("f_h0"), Buf("f_h1")]
                sq = sb(ph, "f_sq", [128, 2, 512], F32R)
                sqB = [Buf("f_sq0"), Buf("f_sq1")]
                rs = sb(ph, "f_rs", [128, 512])
                rsB = Buf("f_rs")
                tmp = sb(ph, "f_tmp", [128, 512])
                tmpB = Buf("f_tmp")
                tmpx = sb(ph, "f_tmpx", [128, 512])
                tmpxB = Buf("f_tmpx")
                NW = 3
                wg = [sb(ph, "f_wg%d" % i, [128, 8, 256], F32R) for i in range(NW)]
                wgB = [Buf("f_wg%d" % i) for i in range(NW)]
                FB = 3
                act = [sb(ph, "f_act%d" % i, [128, FB, 1024], F32R) for i in range(2)]
                actB = [[Buf("f_act%d_%d" % (i, f)) for f in range(FB)] for i in range(2)]
                wo = [sb(ph, "f_wo%d" % i, [128, FB, D], F32R) for i in range(2)]
                woB = [Buf("f_wo%d" % i) for i in range(2)]
                sg = [sb(ph, "f_sg%d" % i, [128, 512]) for i in range(2)]
                sgB = [Buf("f_sg%d" % i) for i in range(2)]
                wiv = w_fi_d[l, fi].rearrange("(k p) n -> p k n", p=128)
                wov = w_fo_d[l, fi].rearrange("(f p) n -> p f n", p=128)
                wcnt = 0
                sgc = 0
                wbc = 0
                for tg in range(2):
                    for hh in range(2):
                        g = tg * 2 + hh
                        make_h(l, s, g, lambda c, hh=hh: h[:, c, hh * 512:(hh + 1) * 512], hB[hh], sq, sqB, rs, rsB, tmp, tmpB, tmp2=(tmpx, tmpxB))
                    blocks = [(b0, min(FB, NFF - b0)) for b0 in range(0, NFF, FB)]
                    for bi, (b0, nb) in enumerate(blocks):
                        ab = bi % 2
                        wb_ = wbc % 2
                        wbc += 1
                        S.dma("pool", wo[wb_][:, 0:nb, :], wov[:, b0:b0 + nb, :], writes=[woB[wb_]])
                        for f in range(nb):
                            wt, wb = wg[wcnt % NW], wgB[wcnt % NW]
                            q = "sp" if wcnt % 2 else "act"
                            wcnt += 1
                            S.dma(q, wt[:, :, 0:128], wiv[:, :, (b0 + f) * 128:(b0 + f + 1) * 128], writes=[wb])
                            S.dma(q, wt[:, :, 128:256], wiv[:, :, DFF + (b0 + f) * 128:DFF + (b0 + f + 1) * 128], writes=[wb])
                            for hh in range(2):
                                pg, pgb = ps()
                                pu, pub = ps()
                                for k in range(8):
                                    mm(pg[:], wt[:, k, 0:128], h[:, k, hh * 512:(hh + 1) * 512], k == 0, k == 7, [wb, hB[hh]], [pgb], lazy=True)
                                for k in range(8):
                                    mm(pu[:], wt[:, k, 128:256], h[:, k, hh * 512:(hh + 1) * 512], k == 0, k == 7, [wb, hB[hh]], [pub], lazy=True)
                                st, stb = sg[sgc % 2], sgB[sgc % 2]
                                sgc += 1
                                S.op("act", lambda e, st=st, pg=pg: e.activation(out=st[:], in_=pg[:], func=AF.Silu), reads=[pgb], writes=[stb])
                                S.op("dve", lambda e, st=st, pu=pu, ab=ab, f=f, hh=hh: e.tensor_tensor(out=act[ab][:, f, hh * 512:(hh + 1) * 512], in0=st[:], in1=pu[:], op=ALU.mult),
                                     reads=[stb, pub], writes=[actB[ab][f]])
                        for o in range(8):
                            for hh in range(2):
                                po, pob = ps()
                                for f in range(nb):
                                    mm(po[:], wo[wb_][:, f, o * 128:(o + 1) * 128], act[ab][:, f, hh * 512:(hh + 1) * 512], f == 0, f == nb - 1,
                                       [woB[wb_], actB[ab][f]], [pob], lazy=True)
                                x_update(l, s, o, tg * 2 + hh, po, pob)
                S.barrier()

        def mixer(l):
            import os
            mixlim = int(os.environ.get("MK_MIX", "99"))
            alim = int(os.environ.get("MK_A", "99"))
            plim = int(os.environ.get("MK_P", "1000000000"))
            pcount = [0]

            def chk():
                pcount[0] += 1
                if pcount[0] == plim:
                    S.barrier()
                    S.mute = True
            with ExitStack() as ph:
                tabB = Buf("tab")
                hall = sb(ph, "m_hall", [128, 8, NT], BF16)
                hallB = [Buf("m_hall%d" % g) for g in range(NG)]
                hb_ = [None]
                hbB = [Buf("m_h0")]
                sq = sb(ph, "m_sq", [128, 2, 512], F32R)
                sqB = [Buf("m_sq0"), Buf("m_sq1")]
                tmp = sb(ph, "m_tmp", [128, 512])
                tmpB = Buf("m_tmp")
                tm2 = sb(ph, "m_tm2", [128, 512])
                tm2B = Buf("m_tm2")
                wp = [sb(ph, "m_wp%d" % i, [128, 8, 128], F32R) for i in range(2)]
                wpB = [Buf("m_wp%d" % i) for i in range(2)]
                wo = sb(ph, "m_wo", [128, D], F32R)
                woB = Buf("m_wo")
                lv = sb(ph, "m_lv", [128, 16])
                lvB = Buf("m_lv")
                wiv = w_in_d[l].rearrange("(k p) n -> p k n", p=128)

                for g in range(NG):
                    pt, pb = ps()
                    for c in range(8):
                        S.op("act" if c % 2 else "dve",
                             (lambda e, c=c, g=g: e.activation(out=sq[:, c % 2, :], in_=xT[:, c, g * 512:(g + 1) * 512], func=AF.Square)) if c % 2 else
                             (lambda e, c=c, g=g: e.tensor_tensor(out=sq[:, c % 2, :], in0=xT[:, c, g * 512:(g + 1) * 512], in1=xT[:, c, g * 512:(g + 1) * 512], op=ALU.mult)),
                             reads=[xB[c][g]], writes=[sqB[c % 2]])
                        mm(pt[:], ones, sq[:, c % 2, :], c == 0, c == 7, [sqB[c % 2], cB], [pb])
                    rstd_from_ps(pt, pb, 128, D, tm2, tm2B, tmp, tmpB)
                    for c in range(8):
                        S.op("dve", lambda e, c=c, g=g: e.tensor_tensor(out=tmp[:, :], in0=xT[:, c, g * 512:(g + 1) * 512], in1=tm2[:, :], op=ALU.mult),
                             reads=[xB[c][g], tm2B], writes=[tmpB])
                        S.op("act", lambda e, c=c, g=g: e.activation(out=hall[:, c, g * 512:(g + 1) * 512], in_=tmp[:, :], func=AF.Identity,
                                                                     scale=modA[:, l, 8 + c:8 + c + 1], bias=modv[:, l, 24 + c:24 + c + 1]),
                             reads=[tmpB, cB], writes=[hallB[g]])

                hcB = [Buf("m_hc%d" % c) for c in range(8)]

                def get_h(g):
                    ht = hb_[0]
                    for c in range(8):
                        if c % 2:
                            S.op("act", lambda e, c=c: e.copy(out=ht[:, c, :], in_=hall[:, c, g * 512:(g + 1) * 512]), reads=[hallB[g]], writes=[hcB[c]])
                        else:
                            S.op("dve", lambda e, c=c: e.tensor_copy(out=ht[:, c, :], in_=hall[:, c, g * 512:(g + 1) * 512]), reads=[hallB[g]], writes=[hcB[c]])
                    return ht, hcB

                def load_wp(i, col0, ncol, q="sp"):
                    S.dma(q, wp[i][:, :, 0:ncol], wiv[:, :, col0:col0 + ncol], writes=[wpB[i]])

                def proj(i, ncol, ht, hB):
                    pt, pb = ps()
                    for k in range(8):
                        mm(pt[0:ncol, :], wp[i][:, k, 0:ncol], ht[:, k, :], k == 0, k == 7, [wpB[i], hB[k]], [pb], lazy=True)
                    return pt, pb

                def out_proj(rows, src_ap, srcB, g):
                    for o in range(8):
                        po, pob = ps()
                        mm(po[:], wo[0:rows, o * 128:(o + 1) * 128], src_ap, True, True, [woB, srcB], [pob])
                        x_update(l, 1, o, g, po, pob)

                if mixlim < 1:
                    S.barrier()
                    return
                lp = vecs[:, l, 148:276]
                S.op("dve", lambda e: e.tensor_tensor(out=tmp[:, 0:32], in0=lp[:, 0:32], in1=lp[:, 32:64], op=ALU.mult), reads=[cB], writes=[tmpB])
                S.op("dve", lambda e: e.reduce_sum(out=lv[:, 0:1], in_=tmp[:, 0:32], axis=AX.X), reads=[tmpB], writes=[lvB])
                S.op("dve", lambda e: e.tensor_tensor(out=tmp[:, 0:32], in0=lp[:, 64:96], in1=lp[:, 96:128], op=ALU.mult), reads=[cB, lvB], writes=[tmpB])
                S.op("dve", lambda e: e.reduce_sum(out=lv[:, 1:2], in_=tmp[:, 0:32], axis=AX.X), reads=[tmpB], writes=[lvB])
                S.op("act", lambda e: e.activation(out=lv[:, 2:4], in_=lv[:, 0:2], func=AF.Exp), reads=[lvB], writes=[lvB])
                lam_init = 0.8 - 0.6 * math.exp(-0.3 * l)
                S.op("dve", lambda e: e.scalar_tensor_tensor(out=lv[:, 4:5], in0=lv[:, 3:4], scalar=-lam_init, in1=lv[:, 2:3], op0=ALU.add, op1=ALU.subtract),
                     reads=[lvB], writes=[lvB])
                S.op("dve", lambda e: e.tensor_scalar(out=lv[:, 5:6], in0=vecs[:, l, 98:99], scalar1=1.0 - lam_init, scalar2=None, op0=ALU.mult),
                     reads=[cB, lvB], writes=[lvB])
                S.op("act", lambda e: e.activation(out=lv[:, 8:16], in_=vecs[:, l, 135:143], func=AF.Exp, scale=-1.0), reads=[cB, lvB], writes=[lvB])
                S.op("act", lambda e: e.activation(out=lv[:, 8:16], in_=lv[:, 8:16], func=AF.Ln, bias=1.0), reads=[lvB], writes=[lvB])
                S.op("dve", lambda e: e.tensor_scalar(out=lv[:, 8:16], in0=lv[:, 8:16], scalar1=-8.0, scalar2=None, op0=ALU.mult), reads=[lvB], writes=[lvB])

                with ExitStack() as pc:
                    cosT = sb(pc, "cosT", [128, NT])
                    sinT = sb(pc, "sinT", [128, NT])
                    S.dma("sp", cosT[:], cos_d, writes=[tabB])
                    S.dma("sp", sinT[:], sin_d, writes=[tabB])
                    cq0 = sb(pc, "c_cq0", [128, NT], F32R)
                    cq1 = sb(pc, "c_cq1", [128, NK], F32R)
                    ckv = sb(pc, "c_ckv", [128, NK], F32R)
                    cq0B, cq1B, krB, ckvB = Buf("c_cq0"), Buf("c_cq1"), Buf("c_kr"), Buf("c_ckv")
                    wuq = sb(pc, "c_wuq", [128, 2, 384], F32R)
                    wukv = sb(pc, "c_wukv", [128, 512], F32R)
                    wuB = Buf("c_wu")
                    pc0 = ExitStack()
                    hb_[0] = sb(pc0, "c_h", [128, 8, 512], F32R)
                    cst = sb(pc0, "c_cst", [128, 4, 160])
                    cstB = Buf("c_cst")
                    ost = sb(pc0, "c_ost", [128, 4, 160])
                    ostB = Buf("c_ost")
                    S.dma("sp", wuq[:, 0, :], w_uq_d[l, 0:128, :], writes=[wuB])
                    S.dma("sp", wuq[0:64, 1, :], w_uq_d[l, 128:192, :], writes=[wuB])
                    S.dma("sp", wukv[:], w_ukv_d[l], writes=[wuB])
                    S.dma("sp", cst[:, :, 0:128], cckv_d[l].rearrange("(t p) n -> p t n", p=128), writes=[cstB])
                    S.dma("sp", cst[:, :, 128:160], ckr_d[l].rearrange("(t p) n -> p t n", p=128), writes=[cstB])
                    pt, pb = ps()
                    for t in range(4):
                        S.op("pe", lambda e, t=t, pt=pt: e.transpose(pt[:, t * 128:(t + 1) * 128], cst[:, t, 0:128], ident[:]), reads=[cstB, cB], writes=[pb], acc=(t > 0), inc=(t == 3))
                    S.op("act", lambda e, pt=pt: e.copy(out=ckv[:, 0:NC_], in_=pt[:]), reads=[pb], writes=[ckvB])
                    pt, pb = ps()
                    for t in range(4):
                        S.op("pe", lambda e, t=t, pt=pt: e.transpose(pt[0:32, t * 128:(t + 1) * 128], cst[:, t, 128:160], ident[:]), reads=[cstB, cB], writes=[pb], acc=(t > 0), inc=(t == 3))
                    S.op("act", lambda e, pt=pt: e.copy(out=cq1[64:96, 0:NC_], in_=pt[0:32, :]), reads=[pb], writes=[krB])
                    for g in range(NG):
                        sl = slice(g * 512, (g + 1) * 512)
                        ksl = slice(NC_ + g * 512, NC_ + (g + 1) * 512)
                        ht, hB = get_h(g)
                        load_wp(0, 1792, 128, "sp")
                        load_wp(1, 1920, 64, "act")
                        p0, p0b = proj(0, 128, ht, hB)
                        p1, p1b = proj(1, 64, ht, hB)
                        S.op("act", lambda e, p0=p0: e.activation(out=sq[:, 0, :], in_=p0[:], func=AF.Square), reads=[p0b], writes=[sqB[0]])
                        S.op("act", lambda e, p1=p1: e.activation(out=sq[0:64, 1, :], in_=p1[0:64, :], func=AF.Square), reads=[p1b], writes=[sqB[1]])
                        p2, p2b = ps()
                        mm(p2[:], ones[:, :], sq[:, 0, :], True, False, [sqB[0], cB], [p2b])
                        mm(p2[:], ones[0:64, :], sq[0:64, 1, :], False, True, [sqB[1], cB], [p2b])
                        rstd_from_ps(p2, p2b, 128, 192, tm2, tm2B, tmp, tmpB)
                        S.op("dve", lambda e, p0=p0: e.tensor_tensor(out=tmp[:, :], in0=p0[:], in1=tm2[:, :], op=ALU.mult), reads=[p0b, tm2B], writes=[tmpB])
                        S.op("act", lambda e, sl=sl: e.activation(out=cq0[:, sl], in_=tmp[:, :], func=AF.Identity, scale=vecs[:, l, 143:144]), reads=[tmpB, cB], writes=[cq0B])
                        S.op("dve", lambda e, p1=p1: e.tensor_tensor(out=tmp[0:64, :], in0=p1[0:64, :], in1=tm2[0:64, :], op=ALU.mult), reads=[p1b, tm2B], writes=[tmpB])
                        S.op("act", lambda e, sl=sl: e.activation(out=cq1[0:64, sl], in_=tmp[0:64, :], func=AF.Identity, scale=vecs[0:64, l, 144:145]), reads=[tmpB, cB], writes=[cq1B])
                        load_wp(0, 1984, 128, "sp")
                        load_wp(1, 2112, 32, "act")
                        p0, p0b = proj(0, 128, ht, hB)
                        p1, p1b = proj(1, 32, ht, hB)
                        S.op("act", lambda e, p0=p0: e.activation(out=sq[:, 0, :], in_=p0[:], func=AF.Square), reads=[p0b], writes=[sqB[0]])
                        p2, p2b = ps()
                        mm(p2[:], ones[:, :], sq[:, 0, :], True, True, [sqB[0], cB], [p2b])
                        rstd_from_ps(p2, p2b, 128, 128, tm2, tm2B, tmp, tmpB)
                        S.op("dve", lambda e, p0=p0: e.tensor_tensor(out=tmp[:, :], in0=p0[:], in1=tm2[:, :], op=ALU.mult), reads=[p0b, tm2B], writes=[tmpB])
                        S.op("act", lambda e, ksl=ksl: e.activation(out=ckv[:, ksl], in_=tmp[:, :], func=AF.Identity, scale=vecs[:, l, 145:146]), reads=[tmpB, cB], writes=[ckvB])
                        S.op("act", lambda e, p1=p1, ksl=ksl: e.copy(out=cq1[64:96, ksl], in_=p1[0:32, :]), reads=[p1b], writes=[krB])
                        p3, p3b = ps()
                        for t in range(4):
                            S.op("pe", lambda e, t=t, p3=p3, ksl=ksl: e.transpose(p3[:, t * 128:(t + 1) * 128], ckv[:, ksl.start + t * 128:ksl.start + (t + 1) * 128].bitcast(F32), ident[:]),
                                 reads=[ckvB, cB], writes=[p3b], acc=(t > 0), inc=(t == 3))
                        S.op("dve", lambda e, p3=p3: e.tensor_copy(out=ost[:, :, 0:128], in_=p3[:].rearrange("p (t n) -> p t n", n=128)), reads=[p3b], writes=[ostB])
                        p4, p4b = ps()
                        for t in range(4):
                            S.op("pe", lambda e, t=t, p4=p4, ksl=ksl: e.transpose(p4[:, t * 32:(t + 1) * 32], cq1[64:96, ksl.start + t * 128:ksl.start + (t + 1) * 128].bitcast(F32), ident[64:96, 64:96]),
                                 reads=[krB, cB], writes=[p4b], acc=(t > 0), inc=(t == 3))
                        S.op("dve", lambda e, p4=p4: e.tensor_copy(out=ost[:, :, 128:160], in_=p4[:, 0:128].rearrange("p (t n) -> p t n", n=32)), reads=[p4b, ostB], writes=[ostB])
                        S.dma("pool", ockv_d[l, sl, :].rearrange("(t p) n -> p t n", p=128), ost[:, :, 0:128], reads=[ostB])
                        S.dma("pool", okr_d[l, sl, :].rearrange("(t p) n -> p t n", p=128), ost[:, :, 128:160], reads=[ostB])
                    S.barrier()
                    pc0.close()
                    if mixlim < 2:
                        return
                    Qm = sb(pc, "c_Q", [104, NT], BF16)
                    Km = sb(pc, "c_K", [104, NK], BF16)
                    Vm = sb(pc, "c_V", [128, NKB, 65], BF16)
                    QB, KB, VB = Buf("c_Q"), Buf("c_K"), Buf("c_V")
                    PT = [sb(pc, "c_PT%d" % i, [128, 512], BF16) for i in range(4)]
                    PTB = [Buf("c_PT%d" % i) for i in range(4)]
                    osb = sb(pc, "c_osb", [65, 512], F32R)
                    osbB = Buf("c_osb")
                    oc = sb(pc, "c_oc", [64, 512], F32R)
                    ocB = Buf("c_oc")
                    qn = sb(pc, "c_qn", [96, 512], F32R)
                    qnB = Buf("c_qn")
                    rs = sb(pc, "c_rs", [96, 512])
                    rsB = Buf("c_rs")
                    S.dma("sp", Qm[96:104, :], mq_d, writes=[QB])
                    S.dma("sp", Km[96:104, :], mk_d, writes=[KB])
                    S.op("pool", lambda e: e.memset(Vm[:, :, 64:65], 1.0), writes=[VB])
                    sc = 96 ** -0.5
                    ptc = [0]
                    for hd in range(4):
                        S.dma("pool", wo[0:64, :], w_out_d[l, 768 + hd * 64:768 + (hd + 1) * 64, :], writes=[woB])
                        for kg in range(NK // 512):
                            ksl = slice(kg * 512, (kg + 1) * 512)
                            pk, pkb = ps()
                            mm(pk[0:64, :], wukv[:, hd * 128:hd * 128 + 64], ckv[:, ksl], True, True, [wuB, ckvB], [pkb])
                            S.op("act", lambda e, pk=pk: e.activation(out=sq[0:64, 0, :], in_=pk[0:64, :], func=AF.Square), reads=[pkb], writes=[sqB[0]])
                            S.op("dve", lambda e, ksl=ksl: e.tensor_tensor(out=sq[0:32, 1, :], in0=cq1[64:96, ksl].bitcast(F32), in1=cq1[64:96, ksl].bitcast(F32), op=ALU.mult),
                                 reads=[krB], writes=[sqB[1]])
                            p2, p2b = ps()
                            mm(p2[0:96, :], ones[0:64, 0:96], sq[0:64, 0, :], True, False, [sqB[0], cB], [p2b])
                            mm(p2[0:96, :], ones[0:32, 0:96], sq[0:32, 1, :], False, True, [sqB[1], cB], [p2b])
                            rstd_from_ps(p2, p2b, 96, 96, rs, rsB, tm2, tm2B)
                            S.op("dve", lambda e, pk=pk: e.tensor_tensor(out=tmp[0:64, :], in0=pk[0:64, :], in1=rs[0:64, :], op=ALU.mult), reads=[pkb, rsB], writes=[tmpB])
                            S.op("dve", lambda e, ksl=ksl: e.tensor_tensor(out=tmp[64:96, :], in0=cq1[64:96, ksl].bitcast(F32), in1=rs[64:96, :], op=ALU.mult), reads=[krB, rsB], writes=[tmpB])
                            if kg == 0:
                                S.op("act", lambda e, ksl=ksl: e.activation(out=Km[0:96, ksl], in_=tmp[0:96, :], func=AF.Identity, scale=vecs[0:96, l, 147:148]), reads=[tmpB, cB], writes=[KB])
                            else:
                                g = kg - 1
                                S.op("act", lambda e: e.activation(out=qn[:, :], in_=tmp[0:96, :], func=AF.Identity, scale=vecs[0:96, l, 147:148]), reads=[tmpB, cB], writes=[qnB])
                                p4, p4b = ps()
                                mm(p4[0:96, :], R96[0:96, 0:96], qn[:, :], True, True, [qnB, cB], [p4b])
                                S.op("act", lambda e, ksl=ksl: e.copy(out=Km[0:64, ksl], in_=qn[0:64, :].bitcast(F32)), reads=[qnB], writes=[KB])
                                S.op("dve", lambda e, g=g: e.tensor_tensor(out=tmp[64:96, :], in0=qn[64:96, :].bitcast(F32), in1=cosT[64:96, g * 512:(g + 1) * 512], op=ALU.mult),
                                     reads=[qnB, tabB], writes=[tmpB])
                                S.op("dve", lambda e, p4=p4, g=g: e.tensor_tensor(out=tm2[64:96, :], in0=p4[64:96, :], in1=sinT[64:96, g * 512:(g + 1) * 512], op=ALU.mult),
                                     reads=[p4b, tabB], writes=[tm2B])
                                S.op("dve", lambda e, ksl=ksl: e.tensor_tensor(out=Km[64:96, ksl], in0=tmp[64:96, :], in1=tm2[64:96, :], op=ALU.add), reads=[tmpB, tm2B], writes=[KB])
                            pv, pvb = ps()
                            for t in range(4):
                                mm(pv[:, t * 64:(t + 1) * 64], ckv[:, kg * 512 + t * 128:kg * 512 + (t + 1) * 128], wukv[:, hd * 128 + 64:hd * 128 + 128], True, True, [wuB, ckvB], [pvb])
                            S.op("act", lambda e, pv=pv, kg=kg: e.copy(out=Vm[:, kg * 4:kg * 4 + 4, 0:64], in_=pv[:, 0:256].rearrange("p (t n) -> p t n", n=64)), reads=[pvb], writes=[VB])
                        for g in range(NG):
                            sl = slice(g * 512, (g + 1) * 512)
                            pq, pqb = ps()
                            mm(pq[0:96, :], wuq[:, 0, hd * 96:(hd + 1) * 96], cq0[:, sl], True, False, [wuB, cq0B], [pqb])
                            mm(pq[0:96, :], wuq[0:64, 1, hd * 96:(hd + 1) * 96], cq1[0:64, sl], False, True, [wuB, cq1B], [pqb])
                            S.op("act", lambda e, pq=pq: e.activation(out=sq[0:96, 0, :], in_=pq[0:96, :], func=AF.Square), reads=[pqb], writes=[sqB[0]])
                            p2, p2b = ps()
                            mm(p2[0:96, :], ones[0:96, 0:96], sq[0:96, 0, :], True, True, [sqB[0], cB], [p2b])
                            rstd_from_ps(p2, p2b, 96, 96, rs, rsB, tm2, tm2B)
                            S.op("dve", lambda e, pq=pq: e.scalar_tensor_tensor(out=qn[:, :], in0=pq[0:96, :], scalar=vecs[0:96, l, 146:147], in1=rs[0:96, :], op0=ALU.mult, op1=ALU.mult),
                                 reads=[pqb, rsB, cB], writes=[qnB])
                            p4, p4b = ps()
                            mm(p4[0:96, :], R96[0:96, 0:96], qn[:, :], True, True, [qnB, cB], [p4b])
                            S.op("act", lambda e, sl=sl: e.copy(out=Qm[0:64, sl], in_=qn[0:64, :].bitcast(F32)), reads=[qnB], writes=[QB])
                            S.op("dve", lambda e, sl=sl: e.tensor_tensor(out=tmp[64:96, :], in0=qn[64:96, :].bitcast(F32), in1=cosT[64:96, sl], op=ALU.mult), reads=[qnB, tabB], writes=[tmpB])
                            S.op("dve", lambda e, p4=p4, sl=sl: e.tensor_tensor(out=tm2[64:96, :], in0=p4[64:96, :], in1=sinT[64:96, sl], op=ALU.mult), reads=[p4b, tabB], writes=[tm2B])
                            S.op("dve", lambda e, sl=sl: e.tensor_tensor(out=Qm[64:96, sl], in0=tmp[64:96, :], in1=tm2[64:96, :], op=ALU.add), reads=[tmpB, tm2B], writes=[QB])
                        for g in range(NG):
                            O, OB = PS[6], PSB[6]

                            def s_mm(kb, pS, pSb, g=g):
                                S.op("pe", lambda e: e.matmul(pS[:], Km[0:104, kb * 128:(kb + 1) * 128], Qm[0:104, g * 512:(g + 1) * 512], start=True, stop=True),
                                     reads=[KB, QB], writes=[pSb])

                            def pv_mm(kb, P_, PB_):
                                mm(PS[6][0:65, :], Vm[:, kb, :], P_[:], kb == 0, kb == NKB - 1, [VB, PB_], [PSB[6]])

                            attn_pipe(list(range(NKB)), s_mm, pv_mm, PT, PTB, ptc, sc, LA=3)
                            S.op("act", lambda e, O=O: e.copy(out=osb[:, :], in_=O[0:65, :]), reads=[OB], writes=[osbB])
                            pd, pdb = ps()
                            mm(pd[0:64, :], sel65[0:65, 0:64], osb[:, :], True, True, [osbB, cB], [pdb])
                            S.op("act", lambda e, pd=pd: e.activation(out=tm2[0:64, :], in_=pd[0:64, :], func=AF.Ln), reads=[pdb], writes=[tm2B])
                            S.op("act", lambda e: e.activation(out=tm2[0:64, :], in_=tm2[0:64, :], func=AF.Exp, scale=-1.0), reads=[tm2B], writes=[tm2B])
                            S.op("dve", lambda e: e.tensor_tensor(out=oc[:, :], in0=osb[0:64, :].bitcast(F32), in1=tm2[0:64, :], op=ALU.mult), reads=[osbB, tm2B], writes=[ocB])
                            out_proj(64, oc[:, :], ocB, g)
                    S.barrier()

                if mixlim < 3:
                    return
                with ExitStack() as pa:
                    cosT = sb(pa, "cosT", [128, NT])
                    sinT = sb(pa, "sinT", [128, NT])
                    S.dma("sp", cosT[:], cos_d, writes=[tabB])
                    S.dma("sp", sinT[:], sin_d, writes=[tabB])
                    hb_[0] = sb(pa, "a_h", [128, 8, 512], F32R)
                    Qd = sb(pa, "a_Q", [72, 2, NT], BF16)
                    Kd = sb(pa, "a_K", [72, 2, NK], BF16)
                    Vd = sb(pa, "a_V", [128, NKB, 65], BF16)
                    QB, KB, VB = Buf("a_Q"), Buf("a_K"), Buf("a_V")
                    PT = [sb(pa, "a_PT%d" % i, [128, 512], BF16) for i in range(4)]
                    PTB = [Buf("a_PT%d" % i) for i in range(4)]
                    osb = sb(pa, "a_osb", [65, 2, 512], F32R)
                    osbB = Buf("a_osb")
                    on = sb(pa, "a_on", [64, 2, 512])
                    onB = Buf("a_on")
                    oa = sb(pa, "a_oa", [64, 512], F32R)
                    oaB = Buf("a_oa")
                    kn = sb(pa, "a_kn", [64, 512], F32R)
                    knB = Buf("a_kn")
                    rs = sb(pa, "a_rs", [64, 512])
                    rsB = Buf("a_rs")
                    kn1 = sb(pa, "a_kn1", [64, 512], F32R)
                    rs1 = sb(pa, "a_rs1", [64, 512])
                    tmpb = sb(pa, "a_tmpb", [64, 512])
                    tm2b = sb(pa, "a_tm2b", [64, 512])
                    knw, knwB = [kn, kn1], [knB, Buf("a_kn1")]
                    rsw, rswB = [rs, rs1], [rsB, Buf("a_rs1")]
                    tw1, tw1B = [tmp, tmpb], [tmpB, Buf("a_tmpb")]
                    tw2, tw2B = [tm2, tm2b], [tm2B, Buf("a_tm2b")]
                    cst = sb(pa, "a_cst", [128, 4, 64])
                    cstB = Buf("a_cst")
                    ost = sb(pa, "a_ost", [128, 4, 64])
                    ostB = Buf("a_ost")
                    S.op("pool", lambda e: e.memset(Qd[:], 0.0), writes=[QB])
                    S.op("pool", lambda e: e.memset(Kd[:], 0.0), writes=[KB])
                    for c in range(2):
                        S.dma("sp", Qd[32:40, c, :], mq_d, writes=[QB])
                        S.dma("sp", Kd[32:40, c, :], mk_d, writes=[KB])
                    S.op("pool", lambda e: e.memset(Vd[:, :, 64:65], 1.0), writes=[VB])
                    ptc = [0]
                    for hd in range(4):
                        load_wp(0, 0 + hd * 64, 64, "sp")
                        load_wp(1, 256 + hd * 64, 64, "act")
                        S.dma("pool", wo[0:64, :], w_out_d[l, hd * 64:(hd + 1) * 64, :], writes=[woB])
                        S.dma("sp", cst[:], ck_d[l, :, hd * 64:(hd + 1) * 64].rearrange("(t p) n -> p t n", p=128), writes=[cstB])
                        pt, pb = ps()
                        for t in range(4):
                            S.op("pe", lambda e, t=t, pt=pt: e.transpose(pt[0:64, t * 128:(t + 1) * 128], cst[:, t, :], ident[:]),
                                 reads=[cstB, cB], writes=[pb], acc=(t > 0), inc=(t == 3))
                        for c in range(2):
                            S.op("dve", lambda e, c=c, pt=pt: e.tensor_copy(out=Kd[0:32, c, 0:NC_], in_=pt[c * 32:(c + 1) * 32, :]), reads=[pb], writes=[KB])
                        S.dma("sp", cst[:], cv_d[l, :, hd * 64:(hd + 1) * 64].rearrange("(t p) n -> p t n", p=128), reads=[], writes=[cstB])
                        S.op("dve", lambda e: e.tensor_copy(out=Vd[:, 0:4, 0:64], in_=cst[:]), reads=[cstB], writes=[VB])
                        if alim < 1:
                            S.barrier()
                            return
                        for g in range(NG):
                            ht, hB = get_h(g)
                            W2 = (0, 1)
                            pts = [proj(w, 64, ht, hB) for w in W2]
                            for w in W2:
                                S.op("act", lambda e, w=w: e.activation(out=sq[0:64, w, :], in_=pts[w][0][0:64, :], func=AF.Square), reads=[pts[w][1]], writes=[sqB[w]])
                            p2s = []
                            for w in W2:
                                p2, p2b = ps()
                                mm(p2[0:64, :], bd32[0:64, 0:64], sq[0:64, w, :], True, True, [sqB[w], cB], [p2b])
                                p2s.append((p2, p2b))
                            for w in W2:
                                S.op("act", lambda e, w=w: e.activation(out=tw2[w][0:64, :], in_=p2s[w][0][0:64, :], func=AF.Ln, scale=1.0 / 32, bias=epsc[0:64, 0:1]),
                                     reads=[p2s[w][1], cB], writes=[tw2B[w]])
                            for w in W2:
                                S.op("act", lambda e, w=w: e.activation(out=rsw[w][:, :], in_=tw2[w][0:64, :], func=AF.Exp, scale=-0.5), reads=[tw2B[w]], writes=[rswB[w]])
                            for w in W2:
                                S.op("dve", lambda e, w=w: e.scalar_tensor_tensor(out=knw[w][:, :], in0=pts[w][0][0:64, :], scalar=vecs[0:64, l, 96 + w:97 + w], in1=rsw[w][:, :],
                                                                                  op0=ALU.mult, op1=ALU.mult),
                                     reads=[pts[w][1], rswB[w], cB], writes=[knwB[w]])
                            p4s = []
                            for w in W2:
                                p4, p4b = ps()
                                mm(p4[0:64, :], R64[0:64, 0:64], knw[w][:, :], True, True, [knwB[w], cB], [p4b])
                                p4s.append((p4, p4b))
                            p3, p3b = ps()
                            for t in range(4):
                                S.op("pe", lambda e, t=t, p3=p3: e.transpose(p3[:, t * 64:(t + 1) * 64], knw[1][:, t * 128:(t + 1) * 128].bitcast(F32), ident[0:64, 0:64]),
                                     reads=[knwB[1], cB], writes=[p3b], acc=(t > 0), inc=(t == 3))
                            S.op("dve", lambda e, p3=p3: e.tensor_copy(out=ost[:].rearrange("p t n -> p (t n)"), in_=p3[:, 0:256]), reads=[p3b], writes=[ostB])
                            S.dma("pool", ok_d[l, g * 512:(g + 1) * 512, hd * 64:(hd + 1) * 64].rearrange("(t p) n -> p t n", p=128), ost[:], reads=[ostB])
                            for w in W2:
                                S.op("dve", lambda e, w=w: e.tensor_tensor(out=tw1[w][0:64, :], in0=knw[w][:, :].bitcast(F32), in1=cosT[0:64, g * 512:(g + 1) * 512], op=ALU.mult),
                                     reads=[knwB[w], tabB], writes=[tw1B[w]])
                            for w in W2:
                                S.op("dve", lambda e, w=w: e.tensor_tensor(out=tw2[w][0:64, :], in0=p4s[w][0][0:64, :], in1=sinT[0:64, g * 512:(g + 1) * 512], op=ALU.mult),
                                     reads=[p4s[w][1], tabB], writes=[tw2B[w]])
                            for w in W2:
                                for c in range(2):
                                    if w == 0:
                                        dst = Qd[0:32, c, g * 512:(g + 1) * 512]
                                    else:
                                        dst = Kd[0:32, c, NC_ + g * 512:NC_ + (g + 1) * 512]
                                    S.op("dve", lambda e, c=c, dst=dst, w=w: e.tensor_tensor(out=dst, in0=tw1[w][c * 32:(c + 1) * 32, :], in1=tw2[w][c * 32:(c + 1) * 32, :], op=ALU.add),
                                         reads=[tw1B[w], tw2B[w]], writes=[QB if w == 0 else KB])
                            chk()
                            S.dma("sp", wp[0][:, :, 64:128], wiv[:, :, 512 + hd * 64:512 + (hd + 1) * 64], writes=[wpB[0]]) if g == 0 else None
                            pv, pvb = ps()
                            for t in range(4):
                                for k in range(8):
                                    mm(pv[:, t * 64:(t + 1) * 64], ht[:, k, t * 128:(t + 1) * 128], wp[0][:, k, 64:128], k == 0, k == 7, [wpB[0], hB[k]], [pvb])
                            S.op("act", lambda e, pv=pv, g=g: e.copy(out=Vd[:, 4 + g * 4:8 + g * 4, 0:64], in_=pv[:, 0:256].rearrange("p (t n) -> p t n", n=64)),
                                 reads=[pvb], writes=[VB])
                            S.op("dve", lambda e, pv=pv: e.tensor_copy(out=ost[:].rearrange("p t n -> p (t n)"), in_=pv[:, 0:256]), reads=[pvb], writes=[ostB])
                            S.dma("pool", ov_d[l, g * 512:(g + 1) * 512, hd * 64:(hd + 1) * 64].rearrange("(t p) n -> p t n", p=128), ost[:], reads=[ostB])
                            chk()
                        if alim < 2:
                            S.barrier()
                            return
                        sc = 32 ** -0.5
                        for g in range(NG):
                            items = [(kb, c) for kb in range(NKB) for c in range(2)]

                            def s_mm(it, pS, pSb, g=g):
                                kb, c = it
                                S.op("pe", lambda e: e.matmul(pS[:], Kd[0:72, c, kb * 128:(kb + 1) * 128], Qd[0:72, c, g * 512:(g + 1) * 512], start=True, stop=True),
                                     reads=[KB, QB], writes=[pSb])

                            def pv_mm(it, P_, PB_):
                                kb, c = it
                                mm(PS[6 + c][0:65, :], Vd[:, kb, :], P_[:], kb == 0, kb == NKB - 1, [VB, PB_], [PSB[6 + c]])

                            attn_pipe(items, s_mm, pv_mm, PT, PTB, ptc, sc, LA=3)
                            for c in range(2):
                                O, OB = PS[6 + c], PSB[6 + c]
                                S.op("act", lambda e, c=c, O=O: e.copy(out=osb[:, c, :], in_=O[0:65, :]), reads=[OB], writes=[osbB])
                                pd, pdb = ps()
                                mm(pd[0:64, :], sel65[0:65, 0:64], osb[:, c, :], True, True, [osbB, cB], [pdb])
                                S.op("act", lambda e, pd=pd: e.activation(out=tm2[0:64, :], in_=pd[0:64, :], func=AF.Ln), reads=[pdb], writes=[tm2B])
                                S.op("act", lambda e: e.activation(out=tm2[0:64, :], in_=tm2[0:64, :], func=AF.Exp, scale=-1.0), reads=[tm2B], writes=[tm2B])
                                S.op("dve", lambda e, c=c: e.tensor_tensor(out=on[:, c, :], in0=osb[0:64, c, :].bitcast(F32), in1=tm2[0:64, :], op=ALU.mult),
                                     reads=[osbB, tm2B], writes=[onB])
                            S.op("dve", lambda e: e.scalar_tensor_tensor(out=on[:, 0, :], in0=on[:, 1, :], scalar=lv[0:64, 4:5], in1=on[:, 0, :], op0=ALU.mult, op1=ALU.add),
                                 reads=[onB, lvB], writes=[onB])
                            S.op("act", lambda e: e.activation(out=sq[0:64, 0, :], in_=on[:, 0, :], func=AF.Square), reads=[onB], writes=[sqB[0]])
                            p2, p2b = ps()
                            mm(p2[0:64, :], ones[0:64, 0:64], sq[0:64, 0, :], True, True, [sqB[0], cB], [p2b])
                            rstd_from_ps(p2, p2b, 64, 64, rs, rsB, tm2, tm2B)
                            S.op("dve", lambda e: e.tensor_tensor(out=tmp[0:64, :], in0=on[:, 0, :], in1=rs[:, :], op=ALU.mult), reads=[onB, rsB], writes=[tmpB])
                            S.op("act", lambda e: e.activation(out=oa[:, :], in_=tmp[0:64, :], func=AF.Identity, scale=lv[0:64, 5:6]), reads=[tmpB, lvB], writes=[oaB])
                            out_proj(64, oa[:, :], oaB, g)
                    S.barrier()

                if mixlim < 4:
                    return
                with ExitStack() as pb_:
                    hb_[0] = sb(pb_, "b_h", [128, 8, 512], F32R)
                    xb = sb(pb_, "b_xb", [128, NT])
                    xc = sb(pb_, "b_xc", [128, NT], F32R)
                    gg = sb(pb_, "b_gg", [128, NT])
                    Pb = sb(pb_, "b_P", [128, NT])
                    Qb = sb(pb_, "b_Q", [128, NT])
                    xbB, xcB, ggB, PbB, QbB = Buf("b_xb"), Buf("b_xc"), Buf("b_gg"), Buf("b_P"), Buf("b_Q")
                    wgt = sb(pb_, "b_wg", [128, 4, 128], F32R)
                    wgtB = Buf("b_wg")
                    A2 = sb(pb_, "b_A2", [128, NT])
                    A2B = Buf("b_A2")
                    rrd = [sb(pb_, "b_r%d" % d, [128, 512]) for d in range(2)]
                    iid = [sb(pb_, "b_i%d" % d, [128, 512]) for d in range(2)]
                    tad = [sb(pb_, "b_t%d" % d, [128, 512]) for d in range(2)]
                    rrdB = [Buf("b_r%d" % d) for d in range(2)]
                    iidB = [Buf("b_i%d" % d) for d in range(2)]
                    tadB = [Buf("b_t%d" % d) for d in range(2)]
                    nk = sb(pb_, "b_nk", [128, 4])
                    nkB = Buf("b_nk")
                    xcf = xc[:].bitcast(F32)
                    for cc in range(4):
                        load_wp(0, 768 + cc * 128, 128, "sp")
                        load_wp(1, 1280 + cc * 128, 128, "act")
                        S.dma("pool", wo[:, :], w_out_d[l, 256 + cc * 128:256 + (cc + 1) * 128, :], writes=[woB])
                        S.op("dve", lambda e: e.tensor_scalar(out=wgt[:], in0=mhalf[:, :].rearrange("p (a b) -> p a b", b=128), scalar1=0.0, scalar2=None, op0=ALU.mult), reads=[cB], writes=[wgtB])
                        for d in range(2):
                            for gt in range(2):
                                for bl in range(2):
                                    S.dma("sp", wgt[bl * 64:(bl + 1) * 64, d * 2 + gt, bl * 64:(bl + 1) * 64], w_gate_d[l, d, gt, cc * 2 + bl], writes=[wgtB])
                        for g in range(NG):
                            ht, hB = get_h(g)
                            pt, pb = proj(0, 128, ht, hB)
                            S.op("act", lambda e, pt=pt, g=g: e.copy(out=xb[:, g * 512:(g + 1) * 512], in_=pt[:]), reads=[pb], writes=[xbB])
                            pt, pb = proj(1, 128, ht, hB)
                            S.op("act", lambda e, pt=pt: e.activation(out=tmp[:, :], in_=pt[:], func=AF.Square), reads=[pb], writes=[tmpB])
                            S.op("dve", lambda e: e.tensor_scalar(out=tmp[:, :], in0=tmp[:, :], scalar1=0.044715, scalar2=1.0, op0=ALU.mult, op1=ALU.add), reads=[tmpB], writes=[tmpB])
                            S.op("dve", lambda e, pt=pt: e.tensor_tensor(out=tmp[:, :], in0=tmp[:, :], in1=pt[:], op=ALU.mult), reads=[tmpB, pb], writes=[tmpB])
                            S.op("act", lambda e: e.activation(out=tm2[:, :], in_=tmp[:, :], func=AF.Sigmoid, scale=1.5957691216057308), reads=[tmpB], writes=[tm2B])
                            S.op("dve", lambda e, pt=pt, g=g: e.tensor_tensor(out=gg[:, g * 512:(g + 1) * 512], in0=tm2[:, :], in1=pt[:], op=ALU.mult), reads=[tm2B, pb], writes=[ggB])
                        cw = lambda j: vecs[:, l, 99 + j * 4 + cc:100 + j * 4 + cc]
                        S.op("dve", lambda e: e.tensor_scalar(out=xc[:, :], in0=xb[:, :], scalar1=cw(2), scalar2=vecs[:, l, 115 + cc:116 + cc], op0=ALU.mult, op1=ALU.add),
                             reads=[xbB, cB], writes=[xcB])
                        for j, off in ((0, -2), (1, -1), (3, 1)):
                            lo, hi = max(0, -off), NT - max(0, off)
                            S.op("dve", lambda e, j=j, off=off, lo=lo, hi=hi: e.scalar_tensor_tensor(out=xc[:, lo:hi], in0=xb[:, lo + off:hi + off], scalar=cw(j), in1=xcf[:, lo:hi],
                                                                                                  op0=ALU.mult, op1=ALU.add), reads=[xbB, cB], writes=[xcB])
                        S.op("dve", lambda e: e.tensor_scalar(out=nk[:, 0:4], in0=vecs[:, l, 99 + cc:99 + cc + 13:4], scalar1=keepv[:, 1:2], scalar2=None, op0=ALU.mult),
                             reads=[cB], writes=[nkB])
                        xc3 = xc[:].rearrange("p (s t) -> p s t", t=256)
                        xcf3 = xcf.rearrange("p (s t) -> p s t", t=256)
                        xb3 = xb[:].rearrange("p (s t) -> p s t", t=256)
                        S.op("dve", lambda e: e.scalar_tensor_tensor(out=xc3[:, 1:8, 0:2], in0=xb3[:, 0:7, 254:256], scalar=nk[:, 0:1], in1=xcf3[:, 1:8, 0:2], op0=ALU.mult, op1=ALU.add),
                             reads=[xbB, nkB], writes=[xcB])
                        S.op("dve", lambda e: e.scalar_tensor_tensor(out=xc3[:, 1:8, 0:1], in0=xb3[:, 0:7, 255:256], scalar=nk[:, 1:2], in1=xcf3[:, 1:8, 0:1], op0=ALU.mult, op1=ALU.add),
                             reads=[xbB, nkB], writes=[xcB])
                        S.op("dve", lambda e: e.scalar_tensor_tensor(out=xc3[:, 0:7, 255:256], in0=xb3[:, 1:8, 0:1], scalar=nk[:, 3:4], in1=xcf3[:, 0:7, 255:256], op0=ALU.mult, op1=ALU.add),
                             reads=[xbB, nkB], writes=[xcB])
                        AA = [xb, A2]
                        AAB = [xbB, A2B]
                        HB_ = [Pb, Qb]
                        HBB = [PbB, QbB]
                        D2 = (0, 1)
                        for g in range(NG):
                            sl = slice(g * 512, (g + 1) * 512)
                            prs, pis = [], []
                            for d in D2:
                                pr, prb = ps()
                                mm(pr[:], wgt[:, d * 2 + 0, :], xc[:, sl], True, True, [wgtB, xcB], [prb])
                                prs.append((pr, prb))
                            for d in D2:
                                pi, pib = ps()
                                mm(pi[:], wgt[:, d * 2 + 1, :], xc[:, sl], True, True, [wgtB, xcB], [pib])
                                pis.append((pi, pib))
                            for d in D2:
                                S.op("act", lambda e, d=d: e.activation(out=rrd[d][:, :], in_=prs[d][0][:], func=AF.Sigmoid, bias=vecs[:, l, 119 + (d * 2 + 0) * 4 + cc:120 + (d * 2 + 0) * 4 + cc]),
                                     reads=[prs[d][1], cB], writes=[rrdB[d]])
                            for d in D2:
                                S.op("act", lambda e, d=d: e.activation(out=iid[d][:, :], in_=pis[d][0][:], func=AF.Sigmoid, bias=vecs[:, l, 119 + (d * 2 + 1) * 4 + cc:120 + (d * 2 + 1) * 4 + cc]),
                                     reads=[pis[d][1], cB], writes=[iidB[d]])
                            for d in D2:
                                S.op("act", lambda e, d=d: e.activation(out=AA[d][:, sl], in_=rrd[d][:, :], func=AF.Exp, scale=lv[:, 8 + d * 4 + cc:9 + d * 4 + cc]),
                                     reads=[rrdB[d], lvB, xcB], writes=[AAB[d]])
                            for d in D2:
                                S.op("dve", lambda e, d=d: e.tensor_tensor(out=tad[d][:, :], in0=AA[d][:, sl], in1=AA[d][:, sl], op=ALU.mult), reads=[AAB[d]], writes=[tadB[d]])
                            for d in D2:
                                S.op("dve", lambda e, d=d: e.tensor_scalar(out=tad[d][:, :], in0=tad[d][:, :], scalar1=-1.0, scalar2=1.0, op0=ALU.mult, op1=ALU.add), reads=[tadB[d]], writes=[tadB[d]])
                            for d in D2:
                                S.op("dve", lambda e, d=d: e.tensor_scalar(out=tad[d][:, :], in0=tad[d][:, :], scalar1=1e-20, scalar2=None, op0=ALU.max), reads=[tadB[d]], writes=[tadB[d]])
                            for d in D2:
                                S.op("act", lambda e, d=d: e.activation(out=tad[d][:, :], in_=tad[d][:, :], func=AF.Ln), reads=[tadB[d]], writes=[tadB[d]])
                            for d in D2:
                                S.op("act", lambda e, d=d: e.activation(out=rrd[d][:, :], in_=tad[d][:, :], func=AF.Exp, scale=0.5), reads=[tadB[d], rrdB[d]], writes=[rrdB[d]])
                            for d in D2:
                                S.op("dve", lambda e, d=d: e.tensor_tensor(out=iid[d][:, :], in0=iid[d][:, :], in1=xcf[:, sl], op=ALU.mult), reads=[iidB[d], xcB], writes=[iidB[d]])
                            for d in D2:
                                S.op("dve", lambda e, d=d: e.tensor_tensor(out=HB_[d][:, sl], in0=iid[d][:, :], in1=rrd[d][:, :], op=ALU.mult), reads=[iidB[d], rrdB[d]], writes=[HBB[d]])
                        for d in D2:
                            A3 = AA[d][:].rearrange("p (s t) -> p s t", t=256)
                            col = 0 if d == 0 else 255
                            S.op("dve", lambda e, col=col, A3=A3: e.tensor_scalar(out=A3[:, :, col:col + 1], in0=A3[:, :, col:col + 1], scalar1=keepv[:, 0:1], scalar2=None, op0=ALU.mult),
                                 reads=[AAB[d], cB], writes=[AAB[d]])
                            h0 = lru0[:, l, d * 4 + cc:d * 4 + cc + 1]
                            if d == 0:
                                S.op("dve", lambda e, h0=h0: e.tensor_tensor_scan(out=Pb[:, :], data0=AA[0][:, :], data1=Pb[:, :], initial=h0, op0=ALU.mult, op1=ALU.add),
                                     reads=[AAB[0], PbB, cB], writes=[PbB])
                                fin = Pb[:].rearrange("p (s t) -> p s t", t=256)[:, :, 255]
                                fB = PbB
                            else:
                                Qf = Qb[:]
                                S.op("dve", lambda e, h0=h0, Qf=Qf: e.tensor_tensor_scan(out=Qf[:, ::-1], data0=AA[1][:, ::-1], data1=Qf[:, ::-1], initial=h0, op0=ALU.mult, op1=ALU.add),
                                     reads=[AAB[1], QbB, cB], writes=[QbB])
                                fin = Qf.rearrange("p (s t) -> p s t", t=256)[:, :, 0]
                                fB = QbB
                            c0 = ((l * 2 + d) * 4 + cc) * 8
                            S.op("dve", lambda e, fin=fin, c0=c0: e.tensor_copy(out=stT[:, c0:c0 + 8], in_=fin), reads=[fB], writes=[stB])
                        S.op("dve", lambda e: e.tensor_tensor(out=Pb[:, :], in0=Pb[:, :], in1=Qb[:, :], op=ALU.add), reads=[PbB, QbB], writes=[PbB])
                        S.op("dve", lambda e: e.tensor_tensor(out=xc[:, :], in0=Pb[:, :], in1=gg[:, :], op=ALU.mult), reads=[PbB, ggB], writes=[xcB])
                        for g in range(NG):
                            out_proj(128, xc[:, g * 512:(g + 1) * 512], xcB, g)
                    S.barrier()

        import os
        stop = int(os.environ.get("MK_STOP", "99"))
        stage = 0
        for l in range(DEPTH):
            for fn in (lambda: ffn(l, 0, 0), lambda: mixer(l), lambda: ffn(l, 2, 1)):
                if stage < stop:
                    try:
                        fn()
                    except StopMixer:
                        pass
                    S.mute = False
                stage += 1

        with ExitStack() as ph:
            stg = [sb(ph, "ystg%d" % i, [128, D]) for i in range(2)]
            stgB = [Buf("ystg%d" % i) for i in range(2)]
            for tt in range(NT // 128):
                st_, stb_ = stg[tt % 2], stgB[tt % 2]
                g = tt // 4
                for half in range(2):
                    pt, pb = ps()
                    for cc in range(4):
                        c = half * 4 + cc
                        S.op("pe", lambda e, c=c, cc=cc, pt=pt: e.transpose(pt[:, cc * 128:(cc + 1) * 128], xT[:, c, tt * 128:(tt + 1) * 128], ident[:]),
                             reads=[xB[c][g], cB], writes=[pb], acc=(cc > 0), inc=(cc == 3))
                    S.op("dve" if half else "act",
                         (lambda e, pt=pt, st_=st_: e.tensor_copy(out=st_[:, 512:1024], in_=pt[:])) if half else
                         (lambda e, pt=pt, st_=st_: e.copy(out=st_[:, 0:512], in_=pt[:])),
                         reads=[pb], writes=[stb_])
                S.dma("sp" if tt % 2 else "act", y_d[tt * 128:(tt + 1) * 128, :], st_[:], reads=[stb_])
            pt, pb = ps()
            S.op("pe", lambda e: e.transpose(pt[:, 0:128], stT[:], ident[:]), reads=[stB, cB], writes=[pb])
            S.op("dve", lambda e: e.tensor_copy(out=stg[0][:, 0:128], in_=pt[:, 0:128]), reads=[pb, stgB[0]], writes=[stgB[0]])
            S.dma("sp", ost_d, stg[0][:, 0:128], reads=[stgB[0]])
            S.barrier()
    return nc


_CACHE = {}


def _consts():
    cR = np.zeros((128, 5 * 128), np.float32)
    cR[:, 0:128] = 1.0
    for b in range(4):
        cR[b * 32:(b + 1) * 32, 128 + b * 32:128 + (b + 1) * 32] = 1.0
    cR[64, 256:256 + 64] = 1.0
    R32 = np.zeros((32, 32), np.float32)
    for m in range(16):
        R32[m + 16, m] = -1.0
        R32[m, m + 16] = 1.0
    for b in range(4):
        cR[b * 32:(b + 1) * 32, 384 + b * 32:384 + (b + 1) * 32] = R32
    cR[64:96, 512 + 64:512 + 96] = R32
    return cR


def _rope_tables():
    T = NT
    rows = T // 64
    row = np.repeat(np.arange(rows), 64).astype(np.float32)
    col = np.tile(np.arange(64), rows).astype(np.float32)
    n = 8
    inv = (10000.0 ** (-np.arange(n, dtype=np.float32) / n)).astype(np.float32)
    ang = np.concatenate([row[:, None] * inv, col[:, None] * inv], axis=-1)
    cos, sin = np.cos(ang).astype(np.float32), np.sin(ang).astype(np.float32)
    idx = np.arange(128) % 16
    return np.ascontiguousarray(cos[:, idx].T), np.ascontiguousarray(sin[:, idx].T)


def kernel(x_prompt, x_sample, cache_diff_k, cache_diff_v, cache_mla_ckv, cache_mla_krope, state_lru, c, c_ctx,
           norm_g, w_ada, b_ada, w_ffn_in, w_ffn_out, w_in, w_out, diff_qk_norm, diff_lambda, diff_subln,
           lru_conv_w, lru_conv_b, lru_w_gate, lru_b_gate, lru_lambda, mla_cq_norm, mla_ckv_norm,
           mla_w_uq, mla_w_ukv, mla_qk_norm):
    f = lambda a: np.ascontiguousarray(np.asarray(a, dtype=np.float32))
    x_prompt, x_sample = f(x_prompt), f(x_sample)
    if "nc" not in _CACHE:
        _CACHE["nc"] = build_program()
    nc = _CACHE["nc"]

    vecs = np.zeros((DEPTH, 128, NV), np.float32)
    p = np.arange(128)
    for l in range(DEPTH):
        vecs[l, :, 0:24] = f(norm_g)[l].reshape(3, 8, 128).transpose(2, 0, 1).reshape(128, 24)
        vecs[l, :, 24:96] = f(b_ada)[l].reshape(72, 128).T
        vecs[l, :, 96] = f(diff_qk_norm)[l, 0][p % 32]
        vecs[l, :, 97] = f(diff_qk_norm)[l, 1][p % 32]
        vecs[l, :, 98] = f(diff_subln)[l][p % 64]
        vecs[l, :, 99:115] = f(lru_conv_w)[l].reshape(4, 4, 128).transpose(2, 0, 1).reshape(128, 16)
        vecs[l, :, 115:119] = f(lru_conv_b)[l].reshape(4, 128).T
        vecs[l, :, 119:135] = f(lru_b_gate)[l].reshape(4, 4, 128).transpose(2, 0, 1).reshape(128, 16)
        vecs[l, :, 135:143] = f(lru_lambda)[l].reshape(2, 4, 128).transpose(2, 0, 1).reshape(128, 8)
        vecs[l, :, 143] = f(mla_cq_norm)[l, 0:128]
        vecs[l, 0:64, 144] = f(mla_cq_norm)[l, 128:192]
        vecs[l, :, 145] = f(mla_ckv_norm)[l]
        vecs[l, 0:96, 146] = f(mla_qk_norm)[l, 0]
        vecs[l, 0:96, 147] = f(mla_qk_norm)[l, 1]
        vecs[l, :, 148:276] = f(diff_lambda)[l].reshape(1, 128)
    cR = _consts()
    ident = np.eye(128, dtype=np.float32)
    cosT, sinT = _rope_tables()
    seg = np.arange(NT) // 256
    mq_p = (seg[None, :] == np.arange(8)[:, None]).astype(np.float32)
    mk_p = np.full((8, NK), -BIG, np.float32)
    mk_p[:, NC_:] = np.where(seg[None, :] == np.arange(8)[:, None], 0.0, -BIG)
    bf = ml_dtypes.bfloat16
    shared = {
        "constsR": cR, "ident": ident, "vecs": vecs,
        "w_ada": f(w_ada), "w_ffn_in": f(w_ffn_in), "w_ffn_out": f(w_ffn_out), "w_in": f(w_in), "w_out": f(w_out),
        "lru_w_gate": f(lru_w_gate), "mla_w_uq": f(mla_w_uq), "mla_w_ukv": f(mla_w_ukv),
    }
    in_maps = []
    for core in range(8):
        m = dict(shared)
        if core < 4:
            m["x"] = x_prompt[core * 8:(core + 1) * 8].reshape(NT, D)
            m["cond"] = np.ascontiguousarray(f(c_ctx).reshape(8, 128).T)
            m["ck"] = np.zeros((DEPTH, NC_, 256), np.float32)
            m["cv"] = np.zeros((DEPTH, NC_, 256), np.float32)
            m["cckv"] = np.zeros((DEPTH, NC_, 128), np.float32)
            m["ckr"] = np.zeros((DEPTH, NC_, 32), np.float32)
            m["lru0"] = np.zeros((DEPTH, 128, 8), np.float32)
            m["keepv"] = np.ascontiguousarray(np.tile(np.array([[0.0, -1.0]], np.float32), (128, 1)))
            m["cosT"] = np.ones((128, NT), np.float32)
            m["sinT"] = np.zeros((128, NT), np.float32)
            m["maskq"] = mq_p.astype(bf)
            m["maskk"] = mk_p.astype(bf)
        else:
            b = core - 4
            m["x"] = x_sample[b]
            m["cond"] = np.ascontiguousarray(f(c)[b].reshape(8, 128).T)
            m["ck"] = f(cache_diff_k)[b].reshape(DEPTH, NC_, 256)
            m["cv"] = f(cache_diff_v)[b].reshape(DEPTH, NC_, 256)
            m["cckv"] = f(cache_mla_ckv)[b]
            m["ckr"] = f(cache_mla_krope)[b]
            m["lru0"] = np.ascontiguousarray(f(state_lru)[b].reshape(DEPTH, 2, 4, 128).transpose(0, 3, 1, 2).reshape(DEPTH, 128, 8))
            m["keepv"] = np.ascontiguousarray(np.tile(np.array([[1.0, 0.0]], np.float32), (128, 1)))
            m["cosT"] = cosT
            m["sinT"] = sinT
            m["maskq"] = np.zeros((8, NT), bf)
            m["maskk"] = np.zeros((8, NK), bf)
        in_maps.append(m)

    if _CACHE.get("prep_only"):
        return in_maps
    res = run_bass_kernel_spmd(nc, in_maps, core_ids=list(range(8)))
    r = res.results
    y_prompt = np.stack([r[i]["y"] for i in range(4)]).reshape(32, 256, D)
    y_sample = np.stack([r[i]["y"] for i in range(4, 8)]).reshape(4, NT, D)

    def gather(name, tail):
        a = np.stack([r[i][name] for i in range(4)])
        a = a.reshape(4, DEPTH, 8, 256, -1).transpose(0, 2, 1, 3, 4).reshape((32, DEPTH, 256) + tail)
        return np.ascontiguousarray(a)

    new_k = gather("o_k", (4, 2, 32))
    new_v = gather("o_v", (4, 64))
    new_ckv = gather("o_ckv", (128,))
    new_kr = gather("o_kr", (32,))
    st = np.stack([r[i]["o_st"] for i in range(4)])
    st = st.reshape(4, DEPTH, 2, 4, 8, 128).transpose(0, 4, 1, 2, 3, 5).reshape(32, DEPTH, 2, 512)
    return (y_prompt.astype(np.float32), y_sample.astype(np.float32), new_k.astype(np.float32), new_v.astype(np.float32),
            new_ckv.astype(np.float32), new_kr.astype(np.float32), np.ascontiguousarray(st).astype(np.float32))
```
